# Optimizing a Trainium2 kernel written in Bass

```python
import math
import jax, jax.numpy as jnp
from jax import lax
import numpy as np

D_MODEL = 1024
BATCH = 16
SEQ = 4096
DEPTH = 4

BRANCH_W = D_MODEL // 2
N_BRANCH = 3
CONV_W = BRANCH_W
CONV_K = 31
DA_HEADS = 4
DA_HEAD_DIM = BRANCH_W // DA_HEADS // 2
DA_V_DIM = 2 * DA_HEAD_DIM
Q_BLOCK = 128
SGU_W = BRANCH_W
SGU_GROUPS = 4
SGU_GROUP_DIM = SGU_W // SGU_GROUPS
CHUNK = 128
IN_COLS = 3 * CONV_W + 4 * BRANCH_W + 3 * SGU_W + N_BRANCH * D_MODEL
EPS = 1e-6

kernel_name = "hybrid_conv_diffattn_sgu_gated_trunk"


def rms_norm(x, g):
    xf = x.astype(jnp.float32)
    y = xf * lax.rsqrt(jnp.mean(xf * xf, axis=-1, keepdims=True) + EPS)
    return (y * g.astype(jnp.float32)).astype(x.dtype)


def layer_norm(x, g, b):
    xf = x.astype(jnp.float32)
    mu = jnp.mean(xf, axis=-1, keepdims=True)
    var = jnp.mean(jnp.square(xf - mu), axis=-1, keepdims=True)
    y = (xf - mu) * lax.rsqrt(var + EPS)
    return (y * g.astype(jnp.float32) + b.astype(jnp.float32)).astype(x.dtype)


def split_columns(proj):
    sizes = [2 * CONV_W, CONV_W,
             BRANCH_W, BRANCH_W, BRANCH_W, BRANCH_W,
             SGU_W, SGU_W, SGU_W,
             N_BRANCH * D_MODEL]
    idx = list(np.cumsum(sizes)[:-1])
    return jnp.split(proj, idx, axis=-1)


def conv_branch(a_glu, gate, conv_w, conv_b, ln_g, ln_b):
    a, g = jnp.split(a_glu, 2, axis=-1)
    z = a * jax.nn.sigmoid(g)
    z = lax.conv_general_dilated(
        z, conv_w[:, None, :].astype(z.dtype), window_strides=(1,),
        padding=[(CONV_K - 1, 0)],
        dimension_numbers=('NWC', 'WIO', 'NWC'),
        feature_group_count=CONV_W) + conv_b
    z = jax.nn.silu(layer_norm(z, ln_g, ln_b))
    return z * jax.nn.silu(gate)


def diff_attention(q, k, v, gate, lam_q1, lam_k1, lam_q2, lam_k2, sub_g, lambda_init):
    B, S, _ = q.shape
    q = q.reshape(B, S, DA_HEADS, 2, DA_HEAD_DIM)
    k = k.reshape(B, S, DA_HEADS, 2, DA_HEAD_DIM)
    v = v.reshape(B, S, DA_HEADS, DA_V_DIM)
    scale = DA_HEAD_DIM ** -0.5
    lam = (jnp.exp(jnp.sum(lam_q1.astype(jnp.float32) * lam_k1.astype(jnp.float32)))
           - jnp.exp(jnp.sum(lam_q2.astype(jnp.float32) * lam_k2.astype(jnp.float32)))
           + lambda_init)
    n_blocks = S // Q_BLOCK
    qb = q.reshape(B, n_blocks, Q_BLOCK, DA_HEADS, 2, DA_HEAD_DIM).transpose(1, 0, 2, 3, 4, 5)
    kpos = jnp.arange(S)

    def one_block(args):
        qi, bidx = args
        s = jnp.einsum('bqhpd,bkhpd->bhpqk', qi, k).astype(jnp.float32) * scale
        qpos = bidx * Q_BLOCK + jnp.arange(Q_BLOCK)
        mask = kpos[None, :] <= qpos[:, None]
        s = jnp.where(mask, s, -jnp.inf)
        p = jax.nn.softmax(s, axis=-1)
        attn = p[:, :, 0] - lam * p[:, :, 1]
        return jnp.einsum('bhqk,bkhe->bqhe', attn.astype(v.dtype), v)

    o = lax.map(one_block, (qb, jnp.arange(n_blocks)))
    o = o.transpose(1, 0, 2, 3, 4).reshape(B, S, DA_HEADS, DA_V_DIM)
    o = rms_norm(o, sub_g) * (1.0 - lambda_init)
    return o.reshape(B, S, DA_HEADS * DA_V_DIM) * jax.nn.silu(gate)


def sgu_branch(u, v, gate, ln_g, ln_b, w_s, b_s):
    B, S, _ = v.shape
    v = layer_norm(v, ln_g, ln_b)
    vc = v.reshape(B, S // CHUNK, CHUNK, SGU_GROUPS, SGU_GROUP_DIM)
    causal = jnp.tril(jnp.ones((CHUNK, CHUNK), dtype=w_s.dtype))
    w = w_s * causal[None]
    mixed = jnp.einsum('gts,bnsgc->bntgc', w, vc) + b_s.T[None, None, :, :, None]
    return u * mixed.reshape(B, S, SGU_W) * jax.nn.silu(gate)


def setup_inputs(seed: int = 0) -> dict:
    key = jax.random.key(seed)
    ks = jax.random.split(key, 21)
    f32 = jnp.float32
    nrm = lambda k, shape, scale: jax.random.normal(k, shape, f32) * scale
    L = DEPTH
    return {
        "x": jax.random.normal(ks[0], (BATCH, SEQ, D_MODEL), f32),
        "norm_g": 1.0 + nrm(ks[1], (L, D_MODEL), 0.02),
        "w_in": nrm(ks[2], (L, D_MODEL, IN_COLS), D_MODEL ** -0.5),
        "conv_w": nrm(ks[3], (L, CONV_K, CONV_W), CONV_K ** -0.5),
        "conv_b": nrm(ks[4], (L, CONV_W), 0.02),
        "conv_ln_g": 1.0 + nrm(ks[5], (L, CONV_W), 0.02),
        "conv_ln_b": nrm(ks[6], (L, CONV_W), 0.02),
        "lam_q1": nrm(ks[7], (L, DA_HEAD_DIM), 0.1),
        "lam_k1": nrm(ks[8], (L, DA_HEAD_DIM), 0.1),
        "lam_q2": nrm(ks[9], (L, DA_HEAD_DIM), 0.1),
        "lam_k2": nrm(ks[10], (L, DA_HEAD_DIM), 0.1),
        "diff_norm_g": 1.0 + nrm(ks[11], (L, DA_V_DIM), 0.02),
        "sgu_ln_g": 1.0 + nrm(ks[12], (L, SGU_W), 0.02),
        "sgu_ln_b": nrm(ks[13], (L, SGU_W), 0.02),
        "sgu_w": nrm(ks[14], (L, SGU_GROUPS, CHUNK, CHUNK), CHUNK ** -0.5),
        "sgu_b": 1.0 + nrm(ks[15], (L, SGU_GROUPS, CHUNK), 0.02),
        "w_pa": nrm(ks[16], (L, CONV_W, D_MODEL), CONV_W ** -0.5),
        "w_pb": nrm(ks[17], (L, BRANCH_W, D_MODEL), BRANCH_W ** -0.5),
        "w_pc": nrm(ks[18], (L, SGU_W, D_MODEL), SGU_W ** -0.5),
        "w_o": nrm(ks[19], (L, D_MODEL, D_MODEL), D_MODEL ** -0.5),
        "final_norm_g": 1.0 + nrm(ks[20], (D_MODEL,), 0.02),
    }


def reference(x, norm_g, w_in, conv_w, conv_b, conv_ln_g, conv_ln_b, lam_q1, lam_k1,
              lam_q2, lam_k2, diff_norm_g, sgu_ln_g, sgu_ln_b, sgu_w, sgu_b,
              w_pa, w_pb, w_pc, w_o, final_norm_g):
    for l in range(DEPTH):
        lambda_init = 0.8 - 0.6 * math.exp(-0.3 * l)
        h = rms_norm(x, norm_g[l])
        proj = jnp.einsum('bsd,dn->bsn', h, w_in[l])
        (a_glu, a_gate, q, k, v, b_gate, u, sv, c_gate, gates) = split_columns(proj)
        ya = conv_branch(a_glu, a_gate, conv_w[l], conv_b[l], conv_ln_g[l], conv_ln_b[l]) @ w_pa[l]
        yb = diff_attention(q, k, v, b_gate, lam_q1[l], lam_k1[l], lam_q2[l], lam_k2[l],
                            diff_norm_g[l], lambda_init) @ w_pb[l]
        yc = sgu_branch(u, sv, c_gate, sgu_ln_g[l], sgu_ln_b[l], sgu_w[l], sgu_b[l]) @ w_pc[l]
        ga, gb, gc = jnp.split(jax.nn.sigmoid(gates), N_BRANCH, axis=-1)
        merged = ga * ya + gb * yb + gc * yc
        x = x + merged @ w_o[l]
    return rms_norm(x, final_norm_g)
```

```python
import math
from contextlib import ExitStack

import numpy as np
import concourse.bass as bass
import concourse.mybir as mybir
from concourse.bass_utils import run_bass_kernel_spmd

F32 = mybir.dt.float32
BF16 = mybir.dt.bfloat16
AF = mybir.ActivationFunctionType
ALU = mybir.AluOpType
AX = mybir.AxisListType

ARENA_MAX = [0]
D = 1024
NCOL = 8192
EPS = 1e-6
SAME_ENG_SYNC = False

OFF_A, OFF_G, OFF_AG = 0, 512, 1024
OFF_Q, OFF_K, OFF_V, OFF_BG = 1536, 2048, 2560, 3072
OFF_U, OFF_SV, OFF_CG = 3584, 4096, 4608
OFF_GATES = 5120


class Buf:
    __slots__ = ("name", "w", "r", "dsem", "dval", "dkey")

    def __init__(self, name):
        self.name = name
        self.w = {}
        self.r = {}
        self.dsem = None
        self.dval = 0
        self.dkey = None


class _Rec:
    def __init__(self):
        self.calls = []

    def __getattr__(self, name):
        def f(*a, **k):
            self.calls.append((name, a, k))
            return None

        return f


class Sched:
    def __init__(self, nc, stack):
        self.nc = nc
        self.stack = stack
        self.names = ["pe", "act", "dve", "pool", "sp"]
        self.streams = {n: [] for n in self.names}
        self.esem = {n: stack.enter_context(nc.semaphore("es_" + n)) for n in ["pe", "act", "dve", "pool"]}
        self.cnt = {n: 0 for n in self.esem}
        self.seen = {n: {} for n in self.names}
        self.all = {}
        self.nd = 0
        self.dpool = {}

    def _deps(self, eng, r, w):
        deps = {}

        def add(d):
            for key, tv in d.items():
                if key not in deps or deps[key][1] < tv[1]:
                    deps[key] = tv

        for b in r:
            add(b.w)
        for b in w:
            add(b.w)
            add(b.r)
        if not SAME_ENG_SYNC and eng != "pool":
            deps.pop(eng, None)
        return deps

    def _wait(self, eng, deps):
        for key, (sem, val) in deps.items():
            if self.seen[eng].get(key, 0) < val:
                self.streams[eng].append(("wait", sem, val))
                self.seen[eng][key] = val

    def op(self, eng, fn, r=(), w=()):
        rec = _Rec()
        fn(rec)
        assert rec.calls
        if eng == "pool" and len(rec.calls) > 1:
            for c in rec.calls:
                self._op1(eng, [c], r, w)
        else:
            self._op1(eng, rec.calls, r, w)

    def _op1(self, eng, calls, r, w):
        self._wait(eng, self._deps(eng, r, w))
        self.cnt[eng] += 1
        tok = (self.esem[eng], self.cnt[eng])
        self.streams[eng].append(("op", calls, self.esem[eng]))
        self.all[eng] = tok
        for b in r:
            b.r[eng] = tok
        for b in w:
            b.w[eng] = tok

    def dma(self, out, in_, sb, r=(), w=(), q="sp"):
        self._wait(q, self._deps(q, r, w))
        if sb.dsem is None:
            if sb.name not in self.dpool:
                self.dpool[sb.name] = [self.stack.enter_context(self.nc.semaphore("ds%d" % self.nd)), "d%d" % self.nd, 0]
                self.nd += 1
            sb.dsem, sb.dkey, sb.dval = self.dpool[sb.name]
        sb.dval += 16
        self.dpool[sb.name][2] = sb.dval
        tok = (sb.dsem, sb.dval)
        self.streams[q].append(("dma", out, in_, sb.dsem))
        self.all[sb.dkey] = tok
        for b in r:
            b.r[sb.dkey] = tok
        for b in w:
            b.w[sb.dkey] = tok

    def barrier(self):
        for e in self.names:
            d = dict(self.all)
            d.pop(e, None)
            self._wait(e, d)

    def replay(self):
        nc = self.nc
        streams = self.streams

        def mk(name):
            def body(e):
                for it in streams[name]:
                    if it[0] == "wait":
                        e.wait_ge(it[1], it[2])
                    elif it[0] == "op":
                        ins = None
                        for (nm, a, k) in it[1]:
                            ins = getattr(e, nm)(*a, **k)
                        ins.then_inc(it[2], 1)
                    else:
                        e.dma_start(out=it[1], in_=it[2]).then_inc(it[3], 16)

            return body

        with nc.Block() as block:
            block.tensor(mk("pe"))
            block.scalar(mk("act"))
            block.vector(mk("dve"))
            block.gpsimd(mk("pool"))
            block.sync(mk("sp"))


def build(L=4, NS=2, S=4096, dbg=False, apply_final=True, l0=0):
    nc = bass.Bass("TRN2", target_bir_lowering=False)
    T = NS * S
    NB = S // 512
    NT = S // 128
    stack = ExitStack()
    with stack:
        stack.enter_context(nc.allow_low_precision("bf16 matmul operands, fp32 accumulation"))

        def din(name, shape):
            return nc.dram_tensor(name, list(shape), F32, kind="ExternalInput").ap()

        x_in = din("x", [T, D])
        norm_g = din("norm_g", [L, D])
        w_in = din("w_in", [L, D, NCOL])
        conv_w = din("conv_w", [L, 31, 512])
        conv_b = din("conv_b", [L, 512])
        conv_ln_g = din("conv_ln_g", [L, 512])
        conv_ln_b = din("conv_ln_b", [L, 512])
        lam_q1 = din("lam_q1", [L, 64])
        lam_k1 = din("lam_k1", [L, 64])
        lam_q2 = din("lam_q2", [L, 64])
        lam_k2 = din("lam_k2", [L, 64])
        diff_norm_g = din("diff_norm_g", [L, 128])
        sgu_ln_g = din("sgu_ln_g", [L, 512])
        sgu_ln_b = din("sgu_ln_b", [L, 512])
        sgu_w = din("sgu_w", [L, 4, 128, 128])
        sgu_b = din("sgu_b", [L, 4, 128])
        w_pa = din("w_pa", [L, 512, D])
        w_pb = din("w_pb", [L, 512, D])
        w_pc = din("w_pc", [L, 512, D])
        w_o = din("w_o", [L, D, D])
        final_g = din("final_norm_g", [D])
        y_out = nc.dram_tensor("y", [T, D], F32, kind="ExternalOutput").ap()

        xs = nc.dram_tensor("xs", [T, D], F32).ap()
        wbi = nc.dram_tensor("wbi", [L, 128, 8, NCOL], BF16).ap()
        wbp = nc.dram_tensor("wbp", [L, 128, 3, 4, D], BF16).ap()
        wbo = nc.dram_tensor("wbo", [L, 128, 8, D], BF16).ap()
        hT_d = nc.dram_tensor("hT", [128, 8, S], BF16).ap()
        cT_d = nc.dram_tensor("cT", [3, 128, 4, S], BF16).ap()
        if dbg:
            dbg_h = nc.dram_tensor("dbg_h", [128, 8, S], BF16, kind="ExternalOutput").ap()
            dbg_c = nc.dram_tensor("dbg_c", [3, 128, 4, S], BF16, kind="ExternalOutput").ap()

        S_ = Sched(nc, stack)
        B_wl = [Buf("wdram%d" % i) for i in range(L)]

        uid = [0]
        ARENA_COLS = 34304
        arena_off = [0]
        arena_box = []

        class _Phase:
            def __enter__(self):
                arena_off[0] = 0
                return self

            def __exit__(self, *a):
                return False

        def phase():
            return _Phase()

        def sb(name, shape, dt=F32, st=None):
            uid[0] += 1
            if st is None:
                return stack.enter_context(nc.sbuf_tensor("%s_%d" % (name, uid[0]), list(shape), dt))
            n = 1
            for v in shape[1:]:
                n *= v
            ncol = n if dt == F32 else (n + 1) // 2
            ncol = (ncol + 7) // 8 * 8
            off = arena_off[0]
            arena_off[0] += ncol
            assert arena_off[0] <= ARENA_COLS, (name, arena_off[0])
            ARENA_MAX[0] = max(ARENA_MAX[0], arena_off[0])
            v = arena_box[0][0:shape[0], off:off + ncol]
            if dt != F32:
                v = v.bitcast(dt)
            v = v[:, 0:n]
            if len(shape) == 3:
                v = v.rearrange("p (a b) -> p a b", a=shape[1])
            elif len(shape) == 4:
                v = v.rearrange("p (a b c) -> p a b c", a=shape[1], b=shape[2])
            return v

        banks = [stack.enter_context(nc.psum_tensor("bank%d" % i, [128, 512], F32)) for i in range(8)]
        bbuf = [Buf("bank%d" % i) for i in range(8)]
        arena_box.append(stack.enter_context(nc.sbuf_tensor("arena", [128, ARENA_COLS], F32)))

        ident_f = sb("ident_f", [128, 128])
        ident_b = sb("ident_b", [128, 128], BF16)
        tri_f = sb("tri_f", [128, 128])
        tri_b = sb("tri_b", [128, 128], BF16)
        ones512 = sb("ones512", [128, 128])
        ones128 = sb("ones128", [128, 128])
        ones_b = sb("ones_b", [128, 128], BF16)
        ones_row = sb("ones_row", [1, 128])
        B_const = Buf("const")

        def mk_consts(e):
            e.memset(ident_f[:], 0.0)
            e.affine_select(out=ident_f[:], in_=ident_f[:], pattern=[[-1, 128]], compare_op=ALU.not_equal,
                            fill=1.0, base=0, channel_multiplier=1)
            e.memset(tri_f[:], 1.0)
            e.affine_select(out=tri_f[:], in_=tri_f[:], pattern=[[1, 128]], compare_op=ALU.is_ge,
                            fill=0.0, base=0, channel_multiplier=-1)
            e.memset(ones512[:], 1.0 / 512)
            e.memset(ones128[:], 1.0 / 128)
            e.memset(ones_b[:], 1.0)
            return e.memset(ones_row[:], 1.0)

        S_.op("pool", mk_consts, w=[B_const])
        S_.op("dve", lambda e: (e.tensor_copy(out=ident_b[:], in_=ident_f[:]),
                                e.tensor_copy(out=tri_b[:], in_=tri_f[:]))[-1], r=[B_const], w=[B_const])

        cpar = sb("cpar", [128, L, 4, 34])
        gcol = sb("gcol", [128, L * 8])
        subg = sb("subg", [128, L])
        nlam = sb("nlam", [128, L])
        wsT = sb("wsT", [128, L, 4, 128], BF16)
        fgb = sb("fgb", [128, D])
        B_par = Buf("par")
        lam_init = [0.8 - 0.6 * math.exp(-0.3 * (l + l0)) for l in range(L)]

        with phase() as pst:
            prow = sb("prow", [34, 512], st=pst)
            B_prow = Buf("prow")
            grow = sb("grow", [L * 8, 128], st=pst)
            B_grow = Buf("grow")
            drow = sb("drow", [L, 128], st=pst)
            B_drow = Buf("drow")
            lq = sb("lq", [128, 4, L * 64], st=pst)
            B_lq = Buf("lq")
            lt = sb("lt", [128, 2, L * 64], st=pst)
            ls = sb("ls", [128, 2, L], st=pst)
            wst = sb("wst", [128, 128], st=pst)
            B_wst = Buf("wst")

            S_.dma(grow[:], norm_g.rearrange("l (k p) -> (l k) p", p=128), B_grow, w=[B_grow])
            S_.op("pe", lambda e: e.transpose(out=banks[0][:, 0:L * 8], in_=grow[:], identity=ident_f[0:L * 8, 0:L * 8]),
                  r=[B_grow, B_const], w=[bbuf[0]])
            S_.op("dve", lambda e: e.tensor_copy(out=gcol[:], in_=banks[0][:, 0:L * 8]), r=[bbuf[0]], w=[B_par])
            S_.dma(drow[:], diff_norm_g, B_drow, w=[B_drow])
            S_.op("pe", lambda e: e.transpose(out=banks[1][:, 0:L], in_=drow[:], identity=ident_f[0:L, 0:L]),
                  r=[B_drow, B_const], w=[bbuf[1]])

            def f_subg(e):
                ins = None
                for l in range(L):
                    ins = e.tensor_scalar(out=subg[:, l:l + 1], in0=banks[1][:, l:l + 1], scalar1=1.0 - lam_init[l],
                                          scalar2=None, op0=ALU.mult)
                return ins

            S_.op("dve", f_subg, r=[bbuf[1]], w=[B_par])
            for i, v in enumerate([lam_q1, lam_k1, lam_q2, lam_k2]):
                S_.dma(lq[:, i, :], v.rearrange("l k -> (l k)").partition_broadcast(128), B_lq, w=[B_lq])

            ls2 = sb("ls2", [128, 2, L], st=pst)
            ls3 = sb("ls3", [128, 2, L], st=pst)
            ljunk = sb("ljunk", [128, 64], st=pst)
            B_lt, B_ls, B_ls2, B_ls3, B_lj = Buf("lt"), Buf("ls"), Buf("ls2"), Buf("ls3"), Buf("lj")
            S_.op("pool", lambda e: (e.tensor_tensor(out=lt[:, 0, :], in0=lq[:, 0, :], in1=lq[:, 1, :], op=ALU.mult),
                                     e.tensor_tensor(out=lt[:, 1, :], in0=lq[:, 2, :], in1=lq[:, 3, :], op=ALU.mult)),
                  r=[B_lq], w=[B_lt])

            def f_lsum(e):
                for i in range(2):
                    for l in range(L):
                        e.activation(out=ljunk[:], in_=lt[:, i, l * 64:(l + 1) * 64], func=AF.Identity,
                                     accum_out=ls[:, i, l:l + 1])

            S_.op("act", f_lsum, r=[B_lt], w=[B_ls, B_lj])
            S_.op("pool", lambda e: e.tensor_copy(out=ls2[:], in_=ls[:]), r=[B_ls], w=[B_ls2])
            S_.op("act", lambda e: e.activation(out=ls3[:], in_=ls2[:], func=AF.Exp), r=[B_ls2], w=[B_ls3])

            def f_lam2(e):
                for l in range(L):
                    e.tensor_tensor(out=nlam[:, l:l + 1], in0=ls3[:, 1, l:l + 1], in1=ls3[:, 0, l:l + 1], op=ALU.subtract)
                for l in range(L):
                    e.tensor_scalar(out=nlam[:, l:l + 1], in0=nlam[:, l:l + 1], scalar1=-lam_init[l],
                                    scalar2=None, op0=ALU.add)

            S_.op("pool", f_lam2, r=[B_ls3], w=[B_par])
            S_.dma(fgb[:], final_g.partition_broadcast(128), B_par, w=[B_par])

            for l in range(L):
                S_.dma(prow[0:31, :], conv_w[l], B_prow, w=[B_prow])
                S_.dma(prow[31:32, :], conv_b[l:l + 1, :], B_prow, w=[B_prow])
                S_.dma(prow[32:33, :], conv_ln_g[l:l + 1, :], B_prow, w=[B_prow])
                S_.dma(prow[33:34, :], conv_ln_b[l:l + 1, :], B_prow, w=[B_prow])
                for j in range(4):
                    bk = 2 + (j % 2)
                    S_.op("pe", lambda e, j=j, bk=bk: e.transpose(out=banks[bk][:, 0:34], in_=prow[0:34, j * 128:(j + 1) * 128],
                                                                 identity=ident_f[0:34, 0:34]),
                          r=[B_prow, B_const], w=[bbuf[bk]])
                    S_.op("dve", lambda e, j=j, bk=bk, l=l: e.tensor_copy(out=cpar[:, l, j, :], in_=banks[bk][:, 0:34]),
                          r=[bbuf[bk]], w=[B_par])
                for g in range(4):
                    bk = 4 + (g % 2)
                    S_.dma(wst[:], sgu_w[l, g], B_wst, w=[B_wst])
                    S_.op("pe", lambda e, bk=bk: e.transpose(out=banks[bk][:, 0:128], in_=wst[:], identity=ident_f[:]),
                          r=[B_wst, B_const], w=[bbuf[bk]])
                    S_.op("dve", lambda e, bk=bk, l=l, g=g: e.tensor_tensor(out=wsT[:, l, g, :], in0=banks[bk][:, 0:128],
                                                                          in1=tri_f[:], op=ALU.mult),
                          r=[bbuf[bk], B_const], w=[B_par])

            NSLOT = 8
            stg = [sb("stg%d" % i, [128, 2048], st=pst) for i in range(NSLOT)]
            B_stg = [Buf("stg%d" % i) for i in range(NSLOT)]
            cvo = [sb("cvo%d" % i, [128, 2048], BF16, st=pst) for i in range(NSLOT)]
            B_cvo = [Buf("cvo%d" % i) for i in range(NSLOT)]
            cv_i = [0]
            cv_eng = ["dve", "act"]

            def convert(src, dst, n, scale_ap, B_w):
                i = cv_i[0] % NSLOT
                eng = cv_eng[cv_i[0] % 2]
                cv_i[0] += 1
                S_.dma(stg[i][:, 0:n], src, B_stg[i], w=[B_stg[i]])
                if eng == "act":
                    if scale_ap is None:
                        fn = lambda e: e.copy(out=cvo[i][:, 0:n], in_=stg[i][:, 0:n])
                    else:
                        fn = lambda e: e.activation(out=cvo[i][:, 0:n], in_=stg[i][:, 0:n], func=AF.Copy, scale=scale_ap)
                else:
                    if scale_ap is None:
                        fn = lambda e: e.tensor_copy(out=cvo[i][:, 0:n], in_=stg[i][:, 0:n])
                    else:
                        fn = lambda e: e.tensor_scalar(out=cvo[i][:, 0:n], in0=stg[i][:, 0:n], scalar1=scale_ap,
                                                       scalar2=None, op0=ALU.mult)
                S_.op(eng, fn, r=[B_stg[i], B_par], w=[B_cvo[i]])
                S_.dma(dst, cvo[i][:, 0:n], B_cvo[i], r=[B_cvo[i]], w=[B_w])

            for l in range(1):
                for k in range(8):
                    for pc in range(4):
                        convert(w_in[l, k * 128:(k + 1) * 128, pc * 2048:(pc + 1) * 2048],
                                wbi[l, :, k, pc * 2048:(pc + 1) * 2048], 2048, gcol[:, l * 8 + k:l * 8 + k + 1], B_wl[l])
                for bi, wp in enumerate([w_pa, w_pb, w_pc]):
                    for j in range(4):
                        convert(wp[l, j * 128:(j + 1) * 128, :], wbp[l, :, bi, j, :], 1024, None, B_wl[l])
                for k in range(8):
                    convert(w_o[l, k * 128:(k + 1) * 128, :], wbo[l, :, k, :], 1024, None, B_wl[l])
            S_.barrier()

        R1 = sb("R1", [128, 8 * 1536], BF16)
        R2 = sb("R2", [128, 8 * 1536], BF16)
        B_R1, B_R2 = Buf("R1"), Buf("R2")
        lng = sb("lng", [128, 512])
        lnb = sb("lnb", [128, 512])
        bsr = sb("bsr", [1, 512])
        B_lp = Buf("layerpar")

        WA = R1[:].rearrange("p (k n) -> p k n", k=8)
        WC = R2[:].rearrange("p (k n) -> p k n", k=8)
        WP = R1[:].rearrange("p (b j n) -> p b j n", b=3, j=4)
        WO = R2[:, 0:8 * D].rearrange("p (k n) -> p k n", k=8)

        def WB(h):
            reg = R1 if h % 2 == 0 else R2
            off = (h // 2) * 4096
            return reg[:, off:off + 4096].rearrange("p (k t n) -> p k t n", k=8, t=4), (B_R1 if h % 2 == 0 else B_R2)

        def bg_pieces(ln):
            ps = []
            for k in range(8):
                for pc in range(8):
                    ps.append((w_in[ln, k * 128:(k + 1) * 128, pc * 1024:(pc + 1) * 1024],
                               wbi[ln, :, k, pc * 1024:(pc + 1) * 1024], gcol[:, ln * 8 + k:ln * 8 + k + 1]))
            for bi, wp in enumerate([w_pa, w_pb, w_pc]):
                for j in range(4):
                    ps.append((wp[ln, j * 128:(j + 1) * 128, :], wbp[ln, :, bi, j, :], None))
            for k in range(8):
                ps.append((w_o[ln, k * 128:(k + 1) * 128, :], wbo[ln, :, k, :], None))
            return ps

        bg = {"pieces": [], "next": 0, "ln": None, "stg": None, "cvo": None, "B_stg": None, "B_cvo": None,
              "loaded": [], "done": []}

        def bg_start_layer(ln):
            bg["pieces"] = bg_pieces(ln)
            bg["next"] = 0
            bg["ln"] = ln

        def bg_attach(st):
            bg["stg"] = [sb("bgs%d" % i, [128, 1024], st=st) for i in range(3)]
            bg["cvo"] = [sb("bgo%d" % i, [128, 1024], BF16, st=st) for i in range(3)]
            bg["B_stg"] = [Buf("bgs%d" % i) for i in range(3)]
            bg["B_cvo"] = [Buf("bgo%d" % i) for i in range(3)]
            bg["loaded"] = []
            bg["done"] = []

        def bg_tick(load=True):
            if bg["done"]:
                n = bg["done"].pop(0)
                i = n % 3
                S_.dma(bg["pieces"][n][1], bg["cvo"][i][:], bg["B_cvo"][i], r=[bg["B_cvo"][i]], w=[B_wl[bg["ln"]]])
            if bg["loaded"]:
                n = bg["loaded"].pop(0)
                i = n % 3
                sc = bg["pieces"][n][2]
                if sc is None:
                    S_.op("dve", lambda e: e.tensor_copy(out=bg["cvo"][i][:], in_=bg["stg"][i][:]),
                          r=[bg["B_stg"][i]], w=[bg["B_cvo"][i]])
                else:
                    S_.op("dve", lambda e: e.tensor_scalar(out=bg["cvo"][i][:], in0=bg["stg"][i][:], scalar1=sc, scalar2=None,
                                                            op0=ALU.mult),
                          r=[bg["B_stg"][i], B_par], w=[bg["B_cvo"][i]])
                bg["done"].append(n)
            if load and bg["next"] < len(bg["pieces"]):
                n = bg["next"]
                bg["next"] += 1
                i = n % 3
                S_.dma(bg["stg"][i][:], bg["pieces"][n][0], bg["B_stg"][i], w=[bg["B_stg"][i]])
                bg["loaded"].append(n)

        def bg_flush(finish):
            while bg["loaded"] or bg["done"] or (finish and bg["next"] < len(bg["pieces"])):
                bg_tick(load=finish)

        def load_WB(l, h):
            ap, bf = WB(h)
            for t, off in enumerate([OFF_Q, OFF_K, OFF_V, OFF_BG]):
                S_.dma(ap[:, :, t, :], wbi[l, :, :, off + h * 128:off + (h + 1) * 128], bf, r=[B_wl[l]], w=[bf])

        for l in range(L):
            last = (l == L - 1)
            do_final = last and apply_final
            S_.dma(lng[:], sgu_ln_g[l].partition_broadcast(128), B_lp, w=[B_lp])
            S_.dma(lnb[:], sgu_ln_b[l].partition_broadcast(128), B_lp, w=[B_lp])
            S_.dma(bsr[:], sgu_b[l:l + 1].rearrange("o g t -> o (g t)"), B_lp, w=[B_lp])
            for s in range(NS):
                xsrc = x_in if l == 0 else xs
                t0 = s * S
                B_hT = [Buf("hTd%d" % i) for i in range(NB)]
                B_cT = [[Buf("cTd%d_%d" % (b, i)) for i in range(NB)] for b in range(3)]
                S_.dma(WA, wbi[l, :, :, 0:1536], B_R1, r=[B_wl[l]], w=[B_R1])
                S_.dma(WC, wbi[l, :, :, OFF_U:OFF_U + 1536], B_R2, r=[B_wl[l]], w=[B_R2])

                with phase() as st:
                    xt = [sb("xt%d" % i, [128, D], st=st) for i in range(4)]
                    B_xt = [Buf("xt%d" % i) for i in range(4)]
                    junk = sb("junk", [128, D], BF16, st=st)
                    B_junk = Buf("junk")
                    ssq = [sb("ssq%d" % i, [128, 4], st=st) for i in range(4)]
                    B_ssq = [Buf("ssq%d" % i) for i in range(4)]
                    hb = [sb("hb%d" % i, [128, D], BF16, st=st) for i in range(2)]
                    B_hb = [Buf("hb%d" % i) for i in range(2)]
                    hTo = [sb("hTo%d" % i, [128, 8, 512], BF16, st=st) for i in range(2)]
                    B_hTo = [Buf("hTo%d" % i) for i in range(2)]

                    def stage_a(i):
                        xi = i % 4
                        S_.dma(xt[xi][:], xsrc[t0 + i * 128:t0 + (i + 1) * 128, :], B_xt[xi], w=[B_xt[xi]])
                        S_.op("act", lambda e: e.activation(out=junk[:], in_=xt[xi][:], func=AF.Square, accum_out=ssq[xi][:, 0:1]),
                              r=[B_xt[xi]], w=[B_junk, B_ssq[xi]])

                    def stage_b(i):
                        xi = i % 4
                        S_.op("dve", lambda e: e.tensor_scalar(out=ssq[xi][:, 1:2], in0=ssq[xi][:, 0:1], scalar1=1.0 / D,
                                                               scalar2=EPS, op0=ALU.mult, op1=ALU.add),
                              r=[B_ssq[xi]], w=[B_ssq[xi]])
                        S_.op("act", lambda e: e.activation(out=ssq[xi][:, 2:3], in_=ssq[xi][:, 1:2], func=AF.Sqrt),
                              r=[B_ssq[xi]], w=[B_ssq[xi]])
                        S_.op("dve", lambda e: e.reciprocal(out=ssq[xi][:, 3:4], in_=ssq[xi][:, 2:3]),
                              r=[B_ssq[xi]], w=[B_ssq[xi]])

                    def stage_c(i):
                        xi, si, bi, bk, q4 = i % 4, i % 2, (i // 4) % 2, i % 2, i % 4
                        S_.op("act", lambda e: e.activation(out=hb[si][:], in_=xt[xi][:], func=AF.Copy, scale=ssq[xi][:, 3:4]),
                              r=[B_xt[xi], B_ssq[xi]], w=[B_hb[si]])
                        pbank = banks[bk][:].bitcast(BF16)
                        S_.op("pe", lambda e: [e.transpose(out=pbank[:, k * 128:(k + 1) * 128], in_=hb[si][:, k * 128:(k + 1) * 128],
                                                           identity=ident_b[:]) for k in range(8)],
                              r=[B_hb[si], B_const], w=[bbuf[bk]])
                        S_.op("dve", lambda e: e.tensor_copy(out=hTo[bi][:, :, q4 * 128:(q4 + 1) * 128],
                                                             in_=pbank.rearrange("p (k t) -> p k t", k=8)),
                              r=[bbuf[bk]], w=[B_hTo[bi]])
                        if q4 == 3:
                            tb = i // 4
                            S_.dma(hT_d[:, :, tb * 512:(tb + 1) * 512], hTo[bi][:], B_hTo[bi], r=[B_hTo[bi]], w=[B_hT[tb]])

                    for i in range(NT + 2):
                        if i < NT:
                            stage_a(i)
                        if 0 <= i - 1 < NT:
                            stage_b(i - 1)
                        if 0 <= i - 2 < NT:
                            stage_c(i - 2)
                    S_.barrier()
                if dbg and l == L - 1 and s == NS - 1:
                    S_.dma(dbg_h, hT_d, B_R1, r=B_hT, w=[])
                    S_.barrier()

                with phase() as st:
                    hTb = [sb("hTb%d" % i, [128, 8, 512], BF16, st=st) for i in range(2)]
                    B_hTb = [Buf("hTb%d" % i) for i in range(2)]
                    sg = [sb("sg%d" % i, [128, 512], st=st) for i in range(2)]
                    B_sg = [Buf("sg%d" % i) for i in range(2)]
                    zbb = sb("zbb", [128, 4, 544], BF16, st=st)
                    B_zb = [Buf("zb%d" % j) for j in range(4)]
                    dg = sb("dg", [128, 124, 128], BF16, st=st)
                    B_dg = [Buf("dg%d" % j) for j in range(4)]
                    zc = sb("zc", [128, 4, 512], st=st)
                    B_zc = [Buf("zc%d" % j) for j in range(4)]
                    zc2 = [sb("zc2_%d" % i, [128, 512], st=st) for i in range(2)]
                    B_zc2 = [Buf("zc2_%d" % i) for i in range(2)]
                    sga = sb("sga", [128, 4, 512], st=st)
                    B_sga = [Buf("sga%d" % j) for j in range(4)]
                    mean_sb = sb("mean_sb", [128, 512], st=st)
                    m2 = sb("m2", [128, 512], st=st)
                    rstd = sb("rstd", [128, 512], st=st)
                    B_stat = Buf("stat")
                    B_m2 = Buf("m2")
                    tt = [sb("tt%d" % i, [128, 512], st=st) for i in range(2)]
                    B_tt = [Buf("tt%d" % i) for i in range(2)]
                    cao = [sb("cao%d" % i, [128, 4, 512], BF16, st=st) for i in range(2)]
                    B_cao = [Buf("cao%d" % i) for i in range(2)]
                    S_.dma(hTb[0][:], hT_d[:, :, 0:512], B_hTb[0], r=[B_hT[0]], w=[B_hTb[0]])
                    for tb in range(NB):
                        hi = tb % 2
                        ci = tb % 2
                        if tb + 1 < NB:
                            S_.dma(hTb[1 - hi][:], hT_d[:, :, (tb + 1) * 512:(tb + 2) * 512], B_hTb[1 - hi],
                                   r=[B_hT[tb + 1]], w=[B_hTb[1 - hi]])

                        def emit_proj(j, tb=tb, hi=hi):
                            for bk, off in ((0, OFF_A), (1, OFF_G), (2, OFF_AG)):
                                def f_mm(e, bk=bk, off=off):
                                    for k in range(8):
                                        e.matmul(banks[bk][:], lhsT=WA[:, k, off + j * 128:off + (j + 1) * 128],
                                                 rhs=hTb[hi][:, k, :], start=(k == 0), stop=(k == 7))

                                S_.op("pe", f_mm, r=[B_R1, B_hTb[hi]], w=[bbuf[bk]])
                            si = j % 2
                            S_.op("act", lambda e: e.activation(out=sg[si][:], in_=banks[1][:], func=AF.Sigmoid),
                                  r=[bbuf[1]], w=[B_sg[si]])
                            if tb == 0:
                                S_.op("dve", lambda e, l=l: [
                                    e.tensor_scalar(out=dg[:, j * 31 + tp, :], in0=ident_f[:], scalar1=cpar[:, l, j, tp:tp + 1],
                                                    scalar2=None, op0=ALU.mult) for tp in range(31)],
                                    r=[B_par, B_const], w=[B_dg[j]])
                                S_.op("pool", lambda e: e.memset(zbb[:, j, 0:30], 0.0), w=[B_zb[j]])
                            else:
                                S_.op("pool", lambda e: e.tensor_copy(out=zbb[:, j, 0:30], in_=zbb[:, j, 512:542]),
                                      r=[B_zb[j]], w=[B_zb[j]])
                            S_.op("dve", lambda e: e.tensor_tensor(out=zbb[:, j, 30:542], in0=banks[0][:], in1=sg[si][:], op=ALU.mult),
                                  r=[bbuf[0], B_sg[si]], w=[B_zb[j]])
                            S_.op("act", lambda e: e.activation(out=sga[:, j, :], in_=banks[2][:], func=AF.Silu),
                                  r=[bbuf[2]], w=[B_sga[j]])

                        def emit_conv(j, l=l):
                            cb_ = 3 + (j % 2)
                            si = j % 2

                            def f_cv(e):
                                for tp in range(31):
                                    e.matmul(banks[cb_][:], lhsT=dg[:, j * 31 + tp, :], rhs=zbb[:, j, tp:tp + 512],
                                             start=(tp == 0), stop=(tp == 30))

                            S_.op("pe", f_cv, r=[B_zb[j], B_dg[j]], w=[bbuf[cb_]])
                            S_.op("act", lambda e: e.activation(out=zc[:, j, :], in_=banks[cb_][:], func=AF.Identity,
                                                                bias=cpar[:, l, j, 31:32]),
                                  r=[bbuf[cb_], B_par], w=[B_zc[j]])
                            S_.op("act", lambda e: e.activation(out=zc2[si][:], in_=banks[cb_][:], func=AF.Square,
                                                                bias=cpar[:, l, j, 31:32]),
                                  r=[bbuf[cb_], B_par], w=[B_zc2[si]])
                            S_.op("pe", lambda e: e.matmul(banks[6][:], lhsT=ones512[:], rhs=zc[:, j, :], start=(j == 0), stop=(j == 3)),
                                  r=[B_zc[j], B_const], w=[bbuf[6]])
                            S_.op("pe", lambda e: e.matmul(banks[7][:], lhsT=ones512[:], rhs=zc2[si][:], start=(j == 0), stop=(j == 3)),
                                  r=[B_zc2[si], B_const], w=[bbuf[7]])

                        emit_proj(0)
                        emit_proj(1)
                        emit_conv(0)
                        emit_proj(2)
                        emit_conv(1)
                        emit_proj(3)
                        emit_conv(2)
                        emit_conv(3)
                        S_.op("act", lambda e: e.copy(out=mean_sb[:], in_=banks[6][:]), r=[bbuf[6]], w=[B_stat])
                        S_.op("act", lambda e: e.activation(out=m2[:], in_=banks[6][:], func=AF.Square), r=[bbuf[6]], w=[B_m2])

                        def f_var(e):
                            e.tensor_tensor(out=rstd[:], in0=banks[7][:], in1=m2[:], op=ALU.subtract)
                            e.tensor_scalar(out=rstd[:], in0=rstd[:], scalar1=0.0, scalar2=EPS, op0=ALU.max, op1=ALU.add)

                        S_.op("dve", f_var, r=[bbuf[7], B_m2], w=[B_stat])
                        S_.op("act", lambda e: e.activation(out=m2[:], in_=rstd[:], func=AF.Sqrt), r=[B_stat], w=[B_m2])
                        S_.op("dve", lambda e: e.reciprocal(out=rstd[:], in_=m2[:]), r=[B_m2], w=[B_stat])
                        for j in range(4):
                            ti = j % 2
                            en = "pool" if j % 2 == 0 else "dve"

                            def f_norm(e, j=j, ti=ti):
                                e.tensor_tensor(out=tt[ti][:], in0=zc[:, j, :], in1=mean_sb[:], op=ALU.subtract)
                                e.tensor_tensor(out=tt[ti][:], in0=tt[ti][:], in1=rstd[:], op=ALU.mult)

                            S_.op(en, f_norm, r=[B_zc[j], B_stat], w=[B_tt[ti]])
                            S_.op("act", lambda e, j=j, ti=ti, l=l: e.activation(out=tt[ti][:], in_=tt[ti][:], func=AF.Silu,
                                                                               scale=cpar[:, l, j, 32:33], bias=cpar[:, l, j, 33:34]),
                                  r=[B_tt[ti], B_par], w=[B_tt[ti]])
                            S_.op(en, lambda e, j=j, ti=ti, ci=ci: e.tensor_tensor(out=cao[ci][:, j, :], in0=tt[ti][:],
                                                                                 in1=sga[:, j, :], op=ALU.mult),
                                  r=[B_tt[ti], B_sga[j]], w=[B_cao[ci]])
                        S_.dma(cT_d[0, :, :, tb * 512:(tb + 1) * 512], cao[ci][:], B_cao[ci], r=[B_cao[ci]], w=[B_cT[0][tb]])
                    S_.barrier()
                load_WB(l, 0)

                with phase() as st:
                    hTb = [sb("hTb%d" % i, [128, 8, 512], BF16, st=st) for i in range(3)]
                    B_hTb = [Buf("hTb%d" % i) for i in range(3)]
                    usb = [sb("usb%d" % i, [128, 512], st=st) for i in range(2)]
                    B_usb = [Buf("usb%d" % i) for i in range(2)]
                    sgc = [sb("sgc%d" % i, [128, 512], st=st) for i in range(2)]
                    B_sgc = [Buf("sgc%d" % i) for i in range(2)]
                    stt = sb("stt", [128, 24], st=st)
                    B_stt = Buf("stt")
                    vn = [sb("vn%d" % i, [128, 512], st=st) for i in range(2)]
                    B_vn = [Buf("vn%d" % i) for i in range(2)]
                    vln = [sb("vln%d" % i, [128, 512], BF16, st=st) for i in range(8)]
                    B_vln = [Buf("vln%d" % i) for i in range(8)]
                    t2 = [sb("t2_%d" % i, [128, 512], st=st) for i in range(2)]
                    B_t2 = [Buf("t2_%d" % i) for i in range(2)]
                    cco = [sb("cco%d" % i, [128, 4, 512], BF16, st=st) for i in range(2)]
                    B_cco = [Buf("cco%d" % i) for i in range(2)]
                    def load_h(tb):
                        S_.dma(hTb[tb % 3][:], hT_d[:, :, tb * 512:(tb + 1) * 512], B_hTb[tb % 3], r=[B_hT[tb]], w=[B_hTb[tb % 3]])

                    def emit_ln(tb):
                        hi = tb % 3
                        vs = 4 * (tb % 2)
                        for tq in range(4):
                            def f_sv(e, tq=tq, hi=hi):
                                for k in range(8):
                                    e.matmul(banks[tq][:], lhsT=hTb[hi][:, k, tq * 128:(tq + 1) * 128],
                                             rhs=WC[:, k, 512:1024], start=(k == 0), stop=(k == 7))

                            S_.op("pe", f_sv, r=[B_R2, B_hTb[hi]], w=[bbuf[tq]])
                            mi = tq % 2
                            S_.op("act", lambda e, tq=tq, mi=mi: e.activation(out=vn[mi][:], in_=banks[tq][:], func=AF.Identity,
                                                                            accum_out=stt[:, tq:tq + 1]),
                                  r=[bbuf[tq]], w=[B_vn[mi], B_stt])
                            S_.op("act", lambda e, tq=tq, mi=mi: e.activation(out=vn[mi][:], in_=banks[tq][:], func=AF.Square,
                                                                            accum_out=stt[:, 4 + tq:5 + tq]),
                                  r=[bbuf[tq]], w=[B_vn[mi], B_stt])

                        def f_st1(e):
                            e.tensor_scalar(out=stt[:, 8:12], in0=stt[:, 0:4], scalar1=1.0 / 512, scalar2=None, op0=ALU.mult)
                            e.tensor_tensor(out=stt[:, 12:16], in0=stt[:, 8:12], in1=stt[:, 8:12], op=ALU.mult)
                            e.tensor_scalar(out=stt[:, 4:8], in0=stt[:, 4:8], scalar1=1.0 / 512, scalar2=EPS, op0=ALU.mult, op1=ALU.add)
                            e.tensor_tensor(out=stt[:, 4:8], in0=stt[:, 4:8], in1=stt[:, 12:16], op=ALU.subtract)

                        S_.op("pool", f_st1, r=[B_stt], w=[B_stt])
                        S_.op("act", lambda e: e.activation(out=stt[:, 16:20], in_=stt[:, 4:8], func=AF.Sqrt), r=[B_stt], w=[B_stt])
                        S_.op("dve", lambda e: e.reciprocal(out=stt[:, 20:24], in_=stt[:, 16:20]), r=[B_stt], w=[B_stt])

                        def f_st2(e):
                            e.tensor_tensor(out=stt[:, 12:16], in0=stt[:, 8:12], in1=stt[:, 20:24], op=ALU.mult)
                            e.tensor_scalar(out=stt[:, 12:16], in0=stt[:, 12:16], scalar1=-1.0, scalar2=None, op0=ALU.mult)

                        S_.op("pool", f_st2, r=[B_stt], w=[B_stt])
                        for tq in range(4):
                            mi = tq % 2
                            S_.op("act", lambda e, tq=tq, mi=mi: e.activation(out=vn[mi][:], in_=banks[tq][:], func=AF.Identity,
                                                                            scale=stt[:, 20 + tq:21 + tq], bias=stt[:, 12 + tq:13 + tq]),
                                  r=[bbuf[tq], B_stt], w=[B_vn[mi]])

                            def f_gb(e, mi=mi, tq=tq):
                                e.tensor_tensor(out=vn[mi][:], in0=vn[mi][:], in1=lng[:], op=ALU.mult)
                                return e.tensor_tensor(out=vln[vs + tq][:], in0=vn[mi][:], in1=lnb[:], op=ALU.add)

                            S_.op("pool", f_gb, r=[B_vn[mi], B_lp], w=[B_vn[mi], B_vln[vs + tq]])

                    def emit_gate(tb):
                        hi = tb % 3
                        ci = tb % 2
                        vs = 4 * (tb % 2)
                        for g in range(4):
                            pu, pc_, pm = [(4, 5, 6), (7, 4, 5), (6, 7, 4), (5, 6, 7)][g]
                            gi = g % 2
                            for bk, off in ((pu, 0), (pc_, 1024)):
                                def f_mm(e, bk=bk, off=off, g=g, hi=hi):
                                    ins = None
                                    for k in range(8):
                                        ins = e.matmul(banks[bk][:], lhsT=WC[:, k, off + g * 128:off + (g + 1) * 128],
                                                       rhs=hTb[hi][:, k, :], start=(k == 0), stop=(k == 7))
                                    return ins

                                S_.op("pe", f_mm, r=[B_R2, B_hTb[hi]], w=[bbuf[bk]])
                            S_.op("act", lambda e, gi=gi, pu=pu: e.copy(out=usb[gi][:], in_=banks[pu][:]),
                                  r=[bbuf[pu]], w=[B_usb[gi]])
                            S_.op("act", lambda e, gi=gi, pc_=pc_: e.activation(out=sgc[gi][:], in_=banks[pc_][:], func=AF.Silu),
                                  r=[bbuf[pc_]], w=[B_sgc[gi]])

                            def f_sp(e, pm=pm, g=g, l=l):
                                ins = None
                                for tq in range(4):
                                    e.matmul(banks[pm][:, tq * 128:(tq + 1) * 128], lhsT=vln[vs + tq][:, g * 128:(g + 1) * 128],
                                             rhs=wsT[:, l, g, :], start=True, stop=False)
                                    ins = e.matmul(banks[pm][:, tq * 128:(tq + 1) * 128], lhsT=ones_row[:],
                                                   rhs=bsr[:, g * 128:(g + 1) * 128], start=False, stop=True)
                                return ins

                            S_.op("pe", f_sp, r=B_vln[vs:vs + 4] + [B_par, B_lp, B_const], w=[bbuf[pm]])
                            S_.op("dve", lambda e, gi=gi, pm=pm: e.tensor_tensor(out=t2[gi][:], in0=banks[pm][:], in1=usb[gi][:],
                                                                               op=ALU.mult),
                                  r=[bbuf[pm], B_usb[gi]], w=[B_t2[gi]])
                            S_.op("dve", lambda e, gi=gi, g=g, ci=ci: e.tensor_tensor(out=cco[ci][:, g, :], in0=t2[gi][:],
                                                                                    in1=sgc[gi][:], op=ALU.mult),
                                  r=[B_t2[gi], B_sgc[gi]], w=[B_cco[ci]])

                    load_h(0)
                    if NB > 1:
                        load_h(1)
                    emit_ln(0)
                    for tb in range(NB):
                        ci = tb % 2
                        if tb + 2 < NB:
                            load_h(tb + 2)
                        if tb + 1 < NB:
                            emit_ln(tb + 1)
                        emit_gate(tb)
                        S_.dma(cT_d[2, :, :, tb * 512:(tb + 1) * 512], cco[ci][:], B_cco[ci], r=[B_cco[ci]], w=[B_cT[2][tb]])
                    S_.barrier()
                load_WB(l, 1)

                with phase() as st:
                    hTb = [sb("hTb%d" % i, [128, 8, 512], BF16, st=st) for i in range(2)]
                    B_hTb = [Buf("hTb%d" % i) for i in range(2)]
                    qT = sb("qT", [128, S], BF16, st=st)
                    kT = sb("kT", [128, S], BF16, st=st)
                    vh = sb("vh", [128, NT, 128], BF16, st=st)
                    sgb = sb("sgb", [128, S], st=st)
                    B_qkv = Buf("qkv")
                    pt = [sb("pt%d" % i, [128, 2, 512], BF16, st=st) for i in range(3)]
                    B_pt = [Buf("pt%d" % i) for i in range(3)]
                    osb = sb("osb", [128, 4, 512], st=st)
                    B_osb = Buf("osb")
                    rr = sb("rr", [128, 2, 512], st=st)
                    B_rr = Buf("rr")
                    o2 = sb("o2", [128, 512], st=st)
                    od = sb("od", [128, 512], st=st)
                    B_od = Buf("od")
                    sqa = sb("sqa", [128, S], st=st)
                    oga = sb("oga", [128, S], st=st)
                    B_sqa = [Buf("sqa%d" % i) for i in range(NB)]
                    B_oga = [Buf("oga%d" % i) for i in range(NB)]
                    rs2 = sb("rs2", [128, 512], st=st)
                    B_rs2 = Buf("rs2")
                    sqt = sb("sqt", [128, 512], st=st)
                    B_sqt = Buf("sqt")
                    cbo = [sb("cbo%d" % i, [128, 512], BF16, st=st) for i in range(2)]
                    B_cbo = [Buf("cbo%d" % i) for i in range(2)]
                    cb_cnt = [0]
                    if l + 1 < L:
                        if s == 0:
                            bg_start_layer(l + 1)
                        bg_attach(st)

                    def emit_norm(h, qb):
                        blk = slice(qb * 512, (qb + 1) * 512)
                        S_.op("pe", lambda e: e.matmul(banks[4][:], lhsT=ones128[:], rhs=sqa[:, blk], start=True, stop=True),
                              r=[B_sqa[qb], B_const], w=[bbuf[4]])
                        S_.op("dve", lambda e: e.tensor_scalar(out=rs2[:], in0=banks[4][:], scalar1=EPS, scalar2=None, op0=ALU.add),
                              r=[bbuf[4]], w=[B_rs2])
                        S_.op("act", lambda e: e.activation(out=sqt[:], in_=rs2[:], func=AF.Sqrt), r=[B_rs2], w=[B_sqt])
                        S_.op("dve", lambda e: e.reciprocal(out=rs2[:], in_=sqt[:]), r=[B_sqt], w=[B_rs2])
                        ci = cb_cnt[0] % 2
                        cb_cnt[0] += 1
                        S_.op("dve", lambda e: e.tensor_tensor(out=cbo[ci][:], in0=oga[:, blk], in1=rs2[:], op=ALU.mult),
                              r=[B_rs2, B_oga[qb]], w=[B_cbo[ci]])
                        S_.dma(cT_d[1, :, h, blk], cbo[ci][:], B_cbo[ci], r=[B_cbo[ci]], w=[B_cT[1][qb]])

                    for h in range(4):
                        Wh, B_Wh = WB(h)
                        if h == 0:
                            S_.dma(hTb[0][:], hT_d[:, :, 0:512], B_hTb[0], r=[B_hT[0]], w=[B_hTb[0]])
                        for tb in range(NB):
                            hi = tb % 2
                            if tb + 1 < NB and not (h > 0 and tb == 0 and NB > 1):
                                S_.dma(hTb[1 - hi][:], hT_d[:, :, (tb + 1) * 512:(tb + 2) * 512], B_hTb[1 - hi],
                                       r=[B_hT[tb + 1]], w=[B_hTb[1 - hi]])
                            for t, bk in ((0, 0), (1, 1), (3, 2)):
                                def f_mm(e, bk=bk, t=t, hi=hi, Wh=Wh):
                                    for k in range(8):
                                        e.matmul(banks[bk][:], lhsT=Wh[:, k, t, :], rhs=hTb[hi][:, k, :],
                                                 start=(k == 0), stop=(k == 7))

                                S_.op("pe", f_mm, r=[B_Wh, B_hTb[hi]], w=[bbuf[bk]])

                            def f_v(e, hi=hi, Wh=Wh):
                                for tq in range(4):
                                    for k in range(8):
                                        e.matmul(banks[3][:, tq * 128:(tq + 1) * 128],
                                                 lhsT=hTb[hi][:, k, tq * 128:(tq + 1) * 128], rhs=Wh[:, k, 2, :],
                                                 start=(k == 0), stop=(k == 7))

                            S_.op("pe", f_v, r=[B_Wh, B_hTb[hi]], w=[bbuf[3]])
                            sl = slice(tb * 512, (tb + 1) * 512)
                            S_.op("dve", lambda e, sl=sl: e.tensor_copy(out=qT[:, sl], in_=banks[0][:]), r=[bbuf[0]], w=[B_qkv])
                            S_.op("act", lambda e, sl=sl: e.copy(out=kT[:, sl], in_=banks[1][:]), r=[bbuf[1]], w=[B_qkv])
                            S_.op("act", lambda e, sl=sl: e.activation(out=sgb[:, sl], in_=banks[2][:], func=AF.Silu),
                                  r=[bbuf[2]], w=[B_qkv])
                            S_.op("dve", lambda e, tb=tb: e.tensor_copy(out=vh[:, tb * 4:(tb + 1) * 4, :],
                                                                       in_=banks[3][:].rearrange("p (t n) -> p t n", t=4)),
                                  r=[bbuf[3]], w=[B_qkv])
                            if h > 0:
                                emit_norm(h - 1, tb)
                        if h < 3:
                            S_.dma(hTb[0][:], hT_d[:, :, 0:512], B_hTb[0], r=[B_hT[0]], w=[B_hTb[0]])
                            if NB > 1:
                                S_.dma(hTb[1][:], hT_d[:, :, 512:1024], B_hTb[1], r=[B_hT[1]], w=[B_hTb[1]])
                        if h == 3:
                            S_.dma(WP, wbp[l], B_R1, r=[B_wl[l]], w=[B_R1])
                            S_.dma(WO, wbo[l], B_R2, r=[B_wl[l]], w=[B_R2])
                        steps = [(qb, kt) for qb in range(NB) for kt in range(4 * qb + 4)]

                        def emit_qk(i):
                            qb, kt = steps[i]
                            j = kt - 4 * qb
                            c0 = 128 * j if j > 0 else 0
                            q0 = qb * 512
                            pi = i % 3
                            sb0 = 4 + 2 * (i % 2)

                            def f_qk(e):
                                for u in range(2):
                                    e.matmul(banks[sb0 + u][:, c0:512], lhsT=kT[u * 64:(u + 1) * 64, kt * 128:(kt + 1) * 128],
                                             rhs=qT[u * 64:(u + 1) * 64, q0 + c0:q0 + 512], start=True, stop=True)

                            S_.op("pe", f_qk, r=[B_qkv], w=[bbuf[sb0], bbuf[sb0 + 1]])
                            for u in range(2):
                                S_.op("act", lambda e, u=u: e.activation(out=pt[pi][:, u, c0:512], in_=banks[sb0 + u][:, c0:512],
                                                                       func=AF.Exp, scale=0.125),
                                      r=[bbuf[sb0 + u]], w=[B_pt[pi]])
                            if j >= 0:
                                S_.op("pool", lambda e: (
                                    e.tensor_tensor(out=pt[pi][:, 0, c0:c0 + 128], in0=pt[pi][:, 0, c0:c0 + 128], in1=tri_b[:], op=ALU.mult),
                                    e.tensor_tensor(out=pt[pi][:, 1, c0:c0 + 128], in0=pt[pi][:, 1, c0:c0 + 128], in1=tri_b[:], op=ALU.mult)),
                                    r=[B_pt[pi], B_const], w=[B_pt[pi]])

                        def emit_pv(i, h=h):
                            qb, kt = steps[i]
                            nk = 4 * qb + 4
                            j = kt - 4 * qb
                            c0 = 128 * j if j > 0 else 0
                            pi = i % 3

                            def f_pv(e):
                                for u in range(2):
                                    e.matmul(banks[u][:, c0:512], lhsT=vh[:, kt, :], rhs=pt[pi][:, u, c0:512],
                                             start=(kt == 0), stop=(kt == nk - 1), skip_group_check=True)
                                    e.matmul(banks[2 + u][:, c0:512], lhsT=ones_b[:], rhs=pt[pi][:, u, c0:512],
                                             start=(kt == 0), stop=(kt == nk - 1), skip_group_check=True)

                            S_.op("pe", f_pv, r=[B_pt[pi], B_qkv, B_const], w=[bbuf[0], bbuf[1], bbuf[2], bbuf[3]])
                            if kt == nk - 1:
                                blk = slice(qb * 512, (qb + 1) * 512)
                                S_.op("act", lambda e: [e.copy(out=osb[:, t, :], in_=banks[t][:]) for t in range(4)],
                                      r=[bbuf[0], bbuf[1], bbuf[2], bbuf[3]], w=[B_osb])
                                S_.op("dve", lambda e: e.reciprocal(out=rr[:], in_=osb[:, 2:4, :]), r=[B_osb], w=[B_rr])

                                def f_o(e, l=l):
                                    e.tensor_tensor(out=od[:], in0=osb[:, 0, :], in1=rr[:, 0, :], op=ALU.mult)
                                    e.tensor_tensor(out=o2[:], in0=osb[:, 1, :], in1=rr[:, 1, :], op=ALU.mult)
                                    e.scalar_tensor_tensor(out=od[:], in0=o2[:], scalar=nlam[:, l:l + 1], in1=od[:],
                                                           op0=ALU.mult, op1=ALU.add)

                                S_.op("dve", f_o, r=[B_osb, B_rr, B_par], w=[B_od])
                                S_.op("act", lambda e: e.activation(out=sqa[:, blk], in_=od[:], func=AF.Square),
                                      r=[B_od], w=[B_sqa[qb]])
                                S_.op("dve", lambda e, l=l: e.scalar_tensor_tensor(out=oga[:, blk], in0=od[:], scalar=subg[:, l:l + 1],
                                                                                  in1=sgb[:, blk], op0=ALU.mult, op1=ALU.mult),
                                      r=[B_od, B_par, B_qkv], w=[B_oga[qb]])

                        emit_qk(0)
                        for i in range(len(steps)):
                            if i + 1 < len(steps):
                                emit_qk(i + 1)
                            emit_pv(i)
                            if l + 1 < L and i % 12 == 5:
                                bg_tick()
                        if h + 2 < 4:
                            load_WB(l, h + 2)
                    for qb in range(NB):
                        emit_norm(3, qb)
                    if l + 1 < L:
                        bg_flush(finish=(s == NS - 1))
                    S_.barrier()
                if dbg and l == L - 1 and s == NS - 1:
                    S_.dma(dbg_c, cT_d, B_R1, r=B_cT[0] + B_cT[1] + B_cT[2], w=[])
                    S_.barrier()

                with phase() as st:
                    hTb = [sb("hTb%d" % i, [128, 8, 512], BF16, st=st) for i in range(2)]
                    B_hTb = [Buf("hTb%d" % i) for i in range(2)]
                    cTb = [sb("cTb%d" % i, [128, 3, 4, 512], BF16, st=st) for i in range(2)]
                    B_cTb = [Buf("cTb%d" % i) for i in range(2)]
                    WG = [sb("WG%d" % i, [128, 8, 3, 128], BF16, st=st) for i in range(3)]
                    B_WG = [Buf("WG%d" % i) for i in range(3)]
                    sgm = [sb("sgm%d" % i, [128, 512], st=st) for i in range(3)]
                    B_sgm = [Buf("sgm%d" % i) for i in range(3)]
                    ma = [sb("ma%d" % i, [128, 512], st=st) for i in range(2)]
                    B_ma = [Buf("ma%d" % i) for i in range(2)]
                    mb = [sb("mb%d" % i, [128, 512], st=st) for i in range(2)]
                    B_mb = [Buf("mb%d" % i) for i in range(2)]
                    mT = [sb("mT%d" % i, [128, 8, 512], BF16, st=st) for i in range(2)]
                    B_mT = [Buf("mT%d" % i) for i in range(2)]
                    xt = [sb("xt%d" % i, [128, D], st=st) for i in range(2)]
                    B_xt = [Buf("xt%d" % i) for i in range(2)]
                    xo = [sb("xo%d" % i, [128, D], st=st) for i in range(2)]
                    B_xo = [Buf("xo%d" % i) for i in range(2)]
                    junk = sb("junk", [128, D], BF16, st=st)
                    B_junk = Buf("junk")
                    ssq = [sb("ssq%d" % i, [128, 4], st=st) for i in range(2)]
                    B_ssq = [Buf("ssq%d" % i) for i in range(2)]
                    B_xs = Buf("xs_dram")

                    def load_blk(tb):
                        hi = tb % 2
                        S_.dma(hTb[hi][:], hT_d[:, :, tb * 512:(tb + 1) * 512], B_hTb[hi], r=[B_hT[tb]], w=[B_hTb[hi]])
                        for b in range(3):
                            S_.dma(cTb[hi][:, b, :, :], cT_d[b, :, :, tb * 512:(tb + 1) * 512], B_cTb[hi],
                                   r=[B_cT[b][tb]], w=[B_cTb[hi]])

                    cnt = {"wg": 0, "xt": 0}

                    def emit_dcs(tb, dcs):
                        hi = tb % 2
                        for dc in dcs:
                            wi = cnt['wg'] % 3
                            cnt['wg'] += 1
                            S_.dma(WG[wi][:], wbi[l, :, :, OFF_GATES:NCOL].rearrange("p k (b n) -> p k b n", b=3)[:, :, :, dc * 128:(dc + 1) * 128],
                                   B_WG[wi], r=[B_wl[l]], w=[B_WG[wi]])
                            mi = dc % 2
                            for b in range(3):
                                pg, py = 2 * b, 2 * b + 1

                                def f_g(e, pg=pg, b=b, wi=wi, hi=hi):
                                    ins = None
                                    for k in range(8):
                                        ins = e.matmul(banks[pg][:], lhsT=WG[wi][:, k, b, :], rhs=hTb[hi][:, k, :],
                                                       start=(k == 0), stop=(k == 7))
                                    return ins

                                S_.op("pe", f_g, r=[B_WG[wi], B_hTb[hi]], w=[bbuf[pg]])

                                def f_y(e, py=py, b=b, dc=dc, hi=hi):
                                    ins = None
                                    for jj in range(4):
                                        ins = e.matmul(banks[py][:], lhsT=WP[:, b, jj, dc * 128:(dc + 1) * 128],
                                                       rhs=cTb[hi][:, b, jj, :], start=(jj == 0), stop=(jj == 3))
                                    return ins

                                S_.op("pe", f_y, r=[B_R1, B_cTb[hi]], w=[bbuf[py]])
                                S_.op("act", lambda e, b=b, pg=pg: e.activation(out=sgm[b][:], in_=banks[pg][:], func=AF.Sigmoid),
                                      r=[bbuf[pg]], w=[B_sgm[b]])
                                if b == 0:
                                    S_.op("dve", lambda e, mi=mi, py=py: e.tensor_tensor(out=ma[mi][:], in0=banks[py][:],
                                                                                       in1=sgm[0][:], op=ALU.mult),
                                          r=[bbuf[py], B_sgm[0]], w=[B_ma[mi]])
                                else:
                                    S_.op("dve", lambda e, mi=mi, py=py, b=b: e.tensor_tensor(out=mb[mi][:], in0=banks[py][:],
                                                                                            in1=sgm[b][:], op=ALU.mult),
                                          r=[bbuf[py], B_sgm[b]], w=[B_mb[mi]])
                                    if b == 1:
                                        S_.op("pool", lambda e, mi=mi: e.tensor_tensor(out=ma[mi][:], in0=ma[mi][:], in1=mb[mi][:],
                                                                                      op=ALU.add),
                                              r=[B_mb[mi], B_ma[mi]], w=[B_ma[mi]])
                                    else:
                                        S_.op("pool", lambda e, mi=mi, dc=dc: e.tensor_tensor(out=mT[tb % 2][:, dc, :], in0=ma[mi][:],
                                                                                             in1=mb[mi][:], op=ALU.add),
                                              r=[B_mb[mi], B_ma[mi]], w=[B_mT[tb % 2]])

                    def emit_wo(tb):
                        for tq in range(4):
                            xi = cnt['xt'] % 2
                            cnt['xt'] += 1
                            r0 = t0 + tb * 512 + tq * 128
                            S_.dma(xt[xi][:], xsrc[r0:r0 + 128, :], B_xt[xi], r=[B_xs], w=[B_xt[xi]])
                            for hf in range(2):
                                bk = 6 + hf

                                def f_o(e, bk=bk, hf=hf, tq=tq):
                                    ins = None
                                    for k in range(8):
                                        ins = e.matmul(banks[bk][:], lhsT=mT[tb % 2][:, k, tq * 128:(tq + 1) * 128],
                                                       rhs=WO[:, k, hf * 512:(hf + 1) * 512], start=(k == 0), stop=(k == 7))
                                    return ins

                                S_.op("pe", f_o, r=[B_mT[tb % 2], B_R2], w=[bbuf[bk]])
                                S_.op("dve", lambda e, bk=bk, hf=hf, xi=xi: e.tensor_tensor(
                                    out=xo[xi][:, hf * 512:(hf + 1) * 512], in0=banks[bk][:], in1=xt[xi][:, hf * 512:(hf + 1) * 512],
                                    op=ALU.add), r=[bbuf[bk], B_xt[xi]], w=[B_xo[xi]])
                            if not last:
                                S_.dma(xs[r0:r0 + 128, :], xo[xi][:], B_xo[xi], r=[B_xo[xi]], w=[B_xs])
                            elif not do_final:
                                S_.dma(y_out[r0:r0 + 128, :], xo[xi][:], B_xo[xi], r=[B_xo[xi]], w=[B_xs])
                            else:
                                si = xi
                                S_.op("act", lambda e, xi=xi, si=si: e.activation(out=junk[:], in_=xo[xi][:], func=AF.Square,
                                                                                accum_out=ssq[si][:, 0:1]),
                                      r=[B_xo[xi]], w=[B_junk, B_ssq[si]])

                                S_.op("dve", lambda e, si=si: e.tensor_scalar(out=ssq[si][:, 1:2], in0=ssq[si][:, 0:1], scalar1=1.0 / D,
                                                                              scalar2=EPS, op0=ALU.mult, op1=ALU.add),
                                      r=[B_ssq[si]], w=[B_ssq[si]])
                                S_.op("act", lambda e, si=si: e.activation(out=ssq[si][:, 2:3], in_=ssq[si][:, 1:2], func=AF.Sqrt),
                                      r=[B_ssq[si]], w=[B_ssq[si]])
                                S_.op("dve", lambda e, si=si: e.reciprocal(out=ssq[si][:, 3:4], in_=ssq[si][:, 2:3]),
                                      r=[B_ssq[si]], w=[B_ssq[si]])

                                S_.op("act", lambda e, si=si, xi=xi: e.activation(out=xo[xi][:], in_=xo[xi][:], func=AF.Copy,
                                                                                scale=ssq[si][:, 3:4]),
                                      r=[B_ssq[si], B_xo[xi]], w=[B_xo[xi]])
                                S_.op("dve", lambda e, xi=xi: e.tensor_tensor(out=xo[xi][:], in0=xo[xi][:], in1=fgb[:], op=ALU.mult),
                                      r=[B_xo[xi], B_par], w=[B_xo[xi]])
                                S_.dma(y_out[r0:r0 + 128, :], xo[xi][:], B_xo[xi], r=[B_xo[xi]], w=[B_xs])

                    load_blk(0)
                    for tb in range(NB):
                        if tb + 1 < NB:
                            load_blk(tb + 1)
                        emit_dcs(tb, range(0, 2))
                        if tb > 0:
                            emit_wo(tb - 1)
                        emit_dcs(tb, range(2, 8))
                    emit_wo(NB - 1)
                    S_.barrier()
        S_.barrier()
        S_.replay()
    return nc


_NC_CACHE = {}
FUSED_LAYERS = 4


def _get(L, NS, S, apply_final, l0):
    key = (L, NS, S, apply_final, l0)
    if key not in _NC_CACHE:
        _NC_CACHE[key] = build(L, NS, S, apply_final=apply_final, l0=l0)
    return _NC_CACHE[key]


def kernel(**inputs):
    DEPTH, NS, S = 4, 2, 4096
    n = 8
    x = np.ascontiguousarray(np.asarray(inputs["x"], dtype=np.float32)).reshape(n, NS * S, D)
    full = {k: np.ascontiguousarray(np.asarray(v, dtype=np.float32)) for k, v in inputs.items() if k != "x"}
    G = FUSED_LAYERS
    for l0 in range(0, DEPTH, G):
        fin = (l0 + G == DEPTH)
        nc = _get(G, NS, S, fin, l0)
        shared = {k: (v if k == "final_norm_g" else np.ascontiguousarray(v[l0:l0 + G])) for k, v in full.items()}
        in_maps = []
        for c in range(n):
            m = dict(shared)
            m["x"] = np.ascontiguousarray(x[c])
            in_maps.append(m)
        res = run_bass_kernel_spmd(nc, in_maps, core_ids=list(range(n)))
        x = np.stack([np.asarray(r["y"]) for r in res.results], axis=0)
    return x.reshape(16, S, D).astype(np.float32)
```

```python
import math
from contextlib import ExitStack

import numpy as np
import concourse.bass as bass
import concourse.mybir as mybir
from concourse.bass_utils import run_bass_kernel_spmd

F32 = mybir.dt.float32
BF16 = mybir.dt.bfloat16
AF = mybir.ActivationFunctionType
ALU = mybir.AluOpType
AX = mybir.AxisListType

ARENA_MAX = [0]
D = 1024
NCOL = 8192
EPS = 1e-6
SAME_ENG_SYNC = False

OFF_A, OFF_G, OFF_AG = 0, 512, 1024
OFF_Q, OFF_K, OFF_V, OFF_BG = 1536, 2048, 2560, 3072
OFF_U, OFF_SV, OFF_CG = 3584, 4096, 4608
OFF_GATES = 5120


class Buf:
    __slots__ = ("name", "w", "r", "dsem", "dval", "dkey")

    def __init__(self, name):
        self.name = name
        self.w = {}
        self.r = {}
        self.dsem = None
        self.dval = 0
        self.dkey = None


class _Rec:
    def __init__(self):
        self.calls = []

    def __getattr__(self, name):
        def f(*a, **k):
            self.calls.append((name, a, k))
            return None

        return f


class Sched:
    def __init__(self, nc, stack):
        self.nc = nc
        self.stack = stack
        self.names = ["pe", "act", "dve", "pool", "sp"]
        self.streams = {n: [] for n in self.names}
        self.esem = {n: stack.enter_context(nc.semaphore("es_" + n)) for n in ["pe", "act", "dve", "pool"]}
        self.cnt = {n: 0 for n in self.esem}
        self.seen = {n: {} for n in self.names}
        self.all = {}
        self.nd = 0
        self.dpool = {}

    def _deps(self, eng, r, w):
        deps = {}

        def add(d):
            for key, tv in d.items():
                if key not in deps or deps[key][1] < tv[1]:
                    deps[key] = tv

        for b in r:
            add(b.w)
        for b in w:
            add(b.w)
            add(b.r)
        if not SAME_ENG_SYNC and eng != "pool":
            deps.pop(eng, None)
        return deps

    def _wait(self, eng, deps):
        for key, (sem, val) in deps.items():
            if self.seen[eng].get(key, 0) < val:
                self.streams[eng].append(("wait", sem, val))
                self.seen[eng][key] = val

    def op(self, eng, fn, r=(), w=()):
        rec = _Rec()
        fn(rec)
        assert rec.calls
        if eng == "pool" and len(rec.calls) > 1:
            for c in rec.calls:
                self._op1(eng, [c], r, w)
        else:
            self._op1(eng, rec.calls, r, w)

    def _op1(self, eng, calls, r, w):
        self._wait(eng, self._deps(eng, r, w))
        self.cnt[eng] += 1
        tok = (self.esem[eng], self.cnt[eng])
        self.streams[eng].append(("op", calls, self.esem[eng]))
        self.all[eng] = tok
        for b in r:
            b.r[eng] = tok
        for b in w:
            b.w[eng] = tok

    def dma(self, out, in_, sb, r=(), w=(), q="sp"):
        self._wait(q, self._deps(q, r, w))
        if sb.dsem is None:
            if sb.name not in self.dpool:
                self.dpool[sb.name] = [self.stack.enter_context(self.nc.semaphore("ds%d" % self.nd)), "d%d" % self.nd, 0]
                self.nd += 1
            sb.dsem, sb.dkey, sb.dval = self.dpool[sb.name]
        sb.dval += 16
        self.dpool[sb.name][2] = sb.dval
        tok = (sb.dsem, sb.dval)
        self.streams[q].append(("dma", out, in_, sb.dsem))
        self.all[sb.dkey] = tok
        for b in r:
            b.r[sb.dkey] = tok
        for b in w:
            b.w[sb.dkey] = tok

    def barrier(self):
        for e in self.names:
            d = dict(self.all)
            d.pop(e, None)
            self._wait(e, d)

    def replay(self):
        nc = self.nc
        streams = self.streams

        def mk(name):
            def body(e):
                for it in streams[name]:
                    if it[0] == "wait":
                        e.wait_ge(it[1], it[2])
                    elif it[0] == "op":
                        ins = None
                        for (nm, a, k) in it[1]:
                            ins = getattr(e, nm)(*a, **k)
                        ins.then_inc(it[2], 1)
                    else:
                        e.dma_start(out=it[1], in_=it[2]).then_inc(it[3], 16)

            return body

        with nc.Block() as block:
            block.tensor(mk("pe"))
            block.scalar(mk("act"))
            block.vector(mk("dve"))
            block.gpsimd(mk("pool"))
            block.sync(mk("sp"))


def build(L=4, NS=2, S=4096, dbg=False, apply_final=True, l0=0):
    nc = bass.Bass("TRN2", target_bir_lowering=False)
    T = NS * S
    NB = S // 512
    NT = S // 128
    stack = ExitStack()
    with stack:
        stack.enter_context(nc.allow_low_precision("bf16 matmul operands, fp32 accumulation"))

        def din(name, shape):
            return nc.dram_tensor(name, list(shape), F32, kind="ExternalInput").ap()

        x_in = din("x", [T, D])
        norm_g = din("norm_g", [L, D])
        w_in = din("w_in", [L, D, NCOL])
        conv_w = din("conv_w", [L, 31, 512])
        conv_b = din("conv_b", [L, 512])
        conv_ln_g = din("conv_ln_g", [L, 512])
        conv_ln_b = din("conv_ln_b", [L, 512])
        lam_q1 = din("lam_q1", [L, 64])
        lam_k1 = din("lam_k1", [L, 64])
        lam_q2 = din("lam_q2", [L, 64])
        lam_k2 = din("lam_k2", [L, 64])
        diff_norm_g = din("diff_norm_g", [L, 128])
        sgu_ln_g = din("sgu_ln_g", [L, 512])
        sgu_ln_b = din("sgu_ln_b", [L, 512])
        sgu_w = din("sgu_w", [L, 4, 128, 128])
        sgu_b = din("sgu_b", [L, 4, 128])
        w_pa = din("w_pa", [L, 512, D])
        w_pb = din("w_pb", [L, 512, D])
        w_pc = din("w_pc", [L, 512, D])
        w_o = din("w_o", [L, D, D])
        final_g = din("final_norm_g", [D])
        y_out = nc.dram_tensor("y", [T, D], F32, kind="ExternalOutput").ap()

        xs = nc.dram_tensor("xs", [T, D], F32).ap()
        wbi = nc.dram_tensor("wbi", [L, 128, 8, NCOL], BF16).ap()
        wbp = nc.dram_tensor("wbp", [L, 128, 3, 4, D], BF16).ap()
        wbo = nc.dram_tensor("wbo", [L, 128, 8, D], BF16).ap()
        hT_d = nc.dram_tensor("hT", [128, 8, S], BF16).ap()
        cT_d = nc.dram_tensor("cT", [3, 128, 4, S], BF16).ap()
        if dbg:
            dbg_h = nc.dram_tensor("dbg_h", [128, 8, S], BF16, kind="ExternalOutput").ap()
            dbg_c = nc.dram_tensor("dbg_c", [3, 128, 4, S], BF16, kind="ExternalOutput").ap()

        S_ = Sched(nc, stack)
        B_wl = [Buf("wdram%d" % i) for i in range(L)]

        uid = [0]
        ARENA_COLS = 34304
        arena_off = [0]
        arena_box = []

        class _Phase:
            def __enter__(self):
                arena_off[0] = 0
                return self

            def __exit__(self, *a):
                return False

        def phase():
            return _Phase()

        def sb(name, shape, dt=F32, st=None):
            uid[0] += 1
            if st is None:
                return stack.enter_context(nc.sbuf_tensor("%s_%d" % (name, uid[0]), list(shape), dt))
            n = 1
            for v in shape[1:]:
                n *= v
            ncol = n if dt == F32 else (n + 1) // 2
            ncol = (ncol + 7) // 8 * 8
            off = arena_off[0]
            arena_off[0] += ncol
            assert arena_off[0] <= ARENA_COLS, (name, arena_off[0])
            ARENA_MAX[0] = max(ARENA_MAX[0], arena_off[0])
            v = arena_box[0][0:shape[0], off:off + ncol]
            if dt != F32:
                v = v.bitcast(dt)
            v = v[:, 0:n]
            if len(shape) == 3:
                v = v.rearrange("p (a b) -> p a b", a=shape[1])
            elif len(shape) == 4:
                v = v.rearrange("p (a b c) -> p a b c", a=shape[1], b=shape[2])
            return v

        banks = [stack.enter_context(nc.psum_tensor("bank%d" % i, [128, 512], F32)) for i in range(8)]
        bbuf = [Buf("bank%d" % i) for i in range(8)]
        arena_box.append(stack.enter_context(nc.sbuf_tensor("arena", [128, ARENA_COLS], F32)))

        ident_f = sb("ident_f", [128, 128])
        ident_b = sb("ident_b", [128, 128], BF16)
        tri_f = sb("tri_f", [128, 128])
        tri_b = sb("tri_b", [128, 128], BF16)
        ones512 = sb("ones512", [128, 128])
        ones128 = sb("ones128", [128, 128])
        ones_b = sb("ones_b", [128, 128], BF16)
        ones_row = sb("ones_row", [1, 128])
        B_const = Buf("const")

        def mk_consts(e):
            e.memset(ident_f[:], 0.0)
            e.affine_select(out=ident_f[:], in_=ident_f[:], pattern=[[-1, 128]], compare_op=ALU.not_equal,
                            fill=1.0, base=0, channel_multiplier=1)
            e.memset(tri_f[:], 1.0)
            e.affine_select(out=tri_f[:], in_=tri_f[:], pattern=[[1, 128]], compare_op=ALU.is_ge,
                            fill=0.0, base=0, channel_multiplier=-1)
            e.memset(ones512[:], 1.0 / 512)
            e.memset(ones128[:], 1.0 / 128)
            e.memset(ones_b[:], 1.0)
            return e.memset(ones_row[:], 1.0)

        S_.op("pool", mk_consts, w=[B_const])
        S_.op("dve", lambda e: (e.tensor_copy(out=ident_b[:], in_=ident_f[:]),
                                e.tensor_copy(out=tri_b[:], in_=tri_f[:]))[-1], r=[B_const], w=[B_const])

        cpar = sb("cpar", [128, L, 4, 34])
        gcol = sb("gcol", [128, L * 8])
        subg = sb("subg", [128, L])
        nlam = sb("nlam", [128, L])
        wsT = sb("wsT", [128, L, 4, 128], BF16)
        fgb = sb("fgb", [128, D])
        B_par = Buf("par")
        lam_init = [0.8 - 0.6 * math.exp(-0.3 * (l + l0)) for l in range(L)]

        with phase() as pst:
            prow = sb("prow", [34, 512], st=pst)
            B_prow = Buf("prow")
            grow = sb("grow", [L * 8, 128], st=pst)
            B_grow = Buf("grow")
            drow = sb("drow", [L, 128], st=pst)
            B_drow = Buf("drow")
            lq = sb("lq", [128, 4, L * 64], st=pst)
            B_lq = Buf("lq")
            lt = sb("lt", [128, 2, L * 64], st=pst)
            ls = sb("ls", [128, 2, L], st=pst)
            wst = sb("wst", [128, 128], st=pst)
            B_wst = Buf("wst")

            S_.dma(grow[:], norm_g.rearrange("l (k p) -> (l k) p", p=128), B_grow, w=[B_grow])
            S_.op("pe", lambda e: e.transpose(out=banks[0][:, 0:L * 8], in_=grow[:], identity=ident_f[0:L * 8, 0:L * 8]),
                  r=[B_grow, B_const], w=[bbuf[0]])
            S_.op("dve", lambda e: e.tensor_copy(out=gcol[:], in_=banks[0][:, 0:L * 8]), r=[bbuf[0]], w=[B_par])
            S_.dma(drow[:], diff_norm_g, B_drow, w=[B_drow])
            S_.op("pe", lambda e: e.transpose(out=banks[1][:, 0:L], in_=drow[:], identity=ident_f[0:L, 0:L]),
                  r=[B_drow, B_const], w=[bbuf[1]])

            def f_subg(e):
                ins = None
                for l in range(L):
                    ins = e.tensor_scalar(out=subg[:, l:l + 1], in0=banks[1][:, l:l + 1], scalar1=1.0 - lam_init[l],
                                          scalar2=None, op0=ALU.mult)
                return ins

            S_.op("dve", f_subg, r=[bbuf[1]], w=[B_par])
            for i, v in enumerate([lam_q1, lam_k1, lam_q2, lam_k2]):
                S_.dma(lq[:, i, :], v.rearrange("l k -> (l k)").partition_broadcast(128), B_lq, w=[B_lq])

            ls2 = sb("ls2", [128, 2, L], st=pst)
            ls3 = sb("ls3", [128, 2, L], st=pst)
            ljunk = sb("ljunk", [128, 64], st=pst)
            B_lt, B_ls, B_ls2, B_ls3, B_lj = Buf("lt"), Buf("ls"), Buf("ls2"), Buf("ls3"), Buf("lj")
            S_.op("pool", lambda e: (e.tensor_tensor(out=lt[:, 0, :], in0=lq[:, 0, :], in1=lq[:, 1, :], op=ALU.mult),
                                     e.tensor_tensor(out=lt[:, 1, :], in0=lq[:, 2, :], in1=lq[:, 3, :], op=ALU.mult)),
                  r=[B_lq], w=[B_lt])

            def f_lsum(e):
                for i in range(2):
                    for l in range(L):
                        e.activation(out=ljunk[:], in_=lt[:, i, l * 64:(l + 1) * 64], func=AF.Identity,
                                     accum_out=ls[:, i, l:l + 1])

            S_.op("act", f_lsum, r=[B_lt], w=[B_ls, B_lj])
            S_.op("pool", lambda e: e.tensor_copy(out=ls2[:], in_=ls[:]), r=[B_ls], w=[B_ls2])
            S_.op("act", lambda e: e.activation(out=ls3[:], in_=ls2[:], func=AF.Exp), r=[B_ls2], w=[B_ls3])

            def f_lam2(e):
                for l in range(L):
                    e.tensor_tensor(out=nlam[:, l:l + 1], in0=ls3[:, 1, l:l + 1], in1=ls3[:, 0, l:l + 1], op=ALU.subtract)
                for l in range(L):
                    e.tensor_scalar(out=nlam[:, l:l + 1], in0=nlam[:, l:l + 1], scalar1=-lam_init[l],
                                    scalar2=None, op0=ALU.add)

            S_.op("pool", f_lam2, r=[B_ls3], w=[B_par])
            S_.dma(fgb[:], final_g.partition_broadcast(128), B_par, w=[B_par])

            for l in range(L):
                S_.dma(prow[0:31, :], conv_w[l], B_prow, w=[B_prow])
                S_.dma(prow[31:32, :], conv_b[l:l + 1, :], B_prow, w=[B_prow])
                S_.dma(prow[32:33, :], conv_ln_g[l:l + 1, :], B_prow, w=[B_prow])
                S_.dma(prow[33:34, :], conv_ln_b[l:l + 1, :], B_prow, w=[B_prow])
                for j in range(4):
                    bk = 2 + (j % 2)
                    S_.op("pe", lambda e, j=j, bk=bk: e.transpose(out=banks[bk][:, 0:34], in_=prow[0:34, j * 128:(j + 1) * 128],
                                                                 identity=ident_f[0:34, 0:34]),
                          r=[B_prow, B_const], w=[bbuf[bk]])
                    S_.op("dve", lambda e, j=j, bk=bk, l=l: e.tensor_copy(out=cpar[:, l, j, :], in_=banks[bk][:, 0:34]),
                          r=[bbuf[bk]], w=[B_par])
                for g in range(4):
                    bk = 4 + (g % 2)
                    S_.dma(wst[:], sgu_w[l, g], B_wst, w=[B_wst])
                    S_.op("pe", lambda e, bk=bk: e.transpose(out=banks[bk][:, 0:128], in_=wst[:], identity=ident_f[:]),
                          r=[B_wst, B_const], w=[bbuf[bk]])
                    S_.op("dve", lambda e, bk=bk, l=l, g=g: e.tensor_tensor(out=wsT[:, l, g, :], in0=banks[bk][:, 0:128],
                                                                          in1=tri_f[:], op=ALU.mult),
                          r=[bbuf[bk], B_const], w=[B_par])

            NSLOT = 8
            stg = [sb("stg%d" % i, [128, 2048], st=pst) for i in range(NSLOT)]
            B_stg = [Buf("stg%d" % i) for i in range(NSLOT)]
            cvo = [sb("cvo%d" % i, [128, 2048], BF16, st=pst) for i in range(NSLOT)]
            B_cvo = [Buf("cvo%d" % i) for i in range(NSLOT)]
            cv_i = [0]
            cv_eng = ["dve", "act"]

            def convert(src, dst, n, scale_ap, B_w):
                i = cv_i[0] % NSLOT
                eng = cv_eng[cv_i[0] % 2]
                cv_i[0] += 1
                S_.dma(stg[i][:, 0:n], src, B_stg[i], w=[B_stg[i]])
                if eng == "act":
                    if scale_ap is None:
                        fn = lambda e: e.copy(out=cvo[i][:, 0:n], in_=stg[i][:, 0:n])
                    else:
                        fn = lambda e: e.activation(out=cvo[i][:, 0:n], in_=stg[i][:, 0:n], func=AF.Copy, scale=scale_ap)
                else:
                    if scale_ap is None:
                        fn = lambda e: e.tensor_copy(out=cvo[i][:, 0:n], in_=stg[i][:, 0:n])
                    else:
                        fn = lambda e: e.tensor_scalar(out=cvo[i][:, 0:n], in0=stg[i][:, 0:n], scalar1=scale_ap,
                                                       scalar2=None, op0=ALU.mult)
                S_.op(eng, fn, r=[B_stg[i], B_par], w=[B_cvo[i]])
                S_.dma(dst, cvo[i][:, 0:n], B_cvo[i], r=[B_cvo[i]], w=[B_w])

            for l in range(1):
                for k in range(8):
                    for pc in range(4):
                        convert(w_in[l, k * 128:(k + 1) * 128, pc * 2048:(pc + 1) * 2048],
                                wbi[l, :, k, pc * 2048:(pc + 1) * 2048], 2048, gcol[:, l * 8 + k:l * 8 + k + 1], B_wl[l])
                for bi, wp in enumerate([w_pa, w_pb, w_pc]):
                    for j in range(4):
                        convert(wp[l, j * 128:(j + 1) * 128, :], wbp[l, :, bi, j, :], 1024, None, B_wl[l])
                for k in range(8):
                    convert(w_o[l, k * 128:(k + 1) * 128, :], wbo[l, :, k, :], 1024, None, B_wl[l])
            S_.barrier()

        R1 = sb("R1", [128, 8 * 1536], BF16)
        R2 = sb("R2", [128, 8 * 1536], BF16)
        B_R1, B_R2 = Buf("R1"), Buf("R2")
        lng = sb("lng", [128, 512])
        lnb = sb("lnb", [128, 512])
        bsr = sb("bsr", [1, 512])
        B_lp = Buf("layerpar")

        WA = R1[:].rearrange("p (k n) -> p k n", k=8)
        WC = R2[:].rearrange("p (k n) -> p k n", k=8)
        WP = R1[:].rearrange("p (b j n) -> p b j n", b=3, j=4)
        WO = R2[:, 0:8 * D].rearrange("p (k n) -> p k n", k=8)

        def WB(h):
            reg = R1 if h % 2 == 0 else R2
            off = (h // 2) * 4096
            return reg[:, off:off + 4096].rearrange("p (k t n) -> p k t n", k=8, t=4), (B_R1 if h % 2 == 0 else B_R2)

        def bg_pieces(ln):
            ps = []
            for k in range(8):
                for pc in range(8):
                    ps.append((w_in[ln, k * 128:(k + 1) * 128, pc * 1024:(pc + 1) * 1024],
                               wbi[ln, :, k, pc * 1024:(pc + 1) * 1024], gcol[:, ln * 8 + k:ln * 8 + k + 1]))
            for bi, wp in enumerate([w_pa, w_pb, w_pc]):
                for j in range(4):
                    ps.append((wp[ln, j * 128:(j + 1) * 128, :], wbp[ln, :, bi, j, :], None))
            for k in range(8):
                ps.append((w_o[ln, k * 128:(k + 1) * 128, :], wbo[ln, :, k, :], None))
            return ps

        bg = {"pieces": [], "next": 0, "ln": None, "stg": None, "cvo": None, "B_stg": None, "B_cvo": None,
              "loaded": [], "done": []}

        def bg_start_layer(ln):
            bg["pieces"] = bg_pieces(ln)
            bg["next"] = 0
            bg["ln"] = ln

        def bg_attach(st):
            bg["stg"] = [sb("bgs%d" % i, [128, 1024], st=st) for i in range(3)]
            bg["cvo"] = [sb("bgo%d" % i, [128, 1024], BF16, st=st) for i in range(3)]
            bg["B_stg"] = [Buf("bgs%d" % i) for i in range(3)]
            bg["B_cvo"] = [Buf("bgo%d" % i) for i in range(3)]
            bg["loaded"] = []
            bg["done"] = []

        def bg_tick(load=True):
            if bg["done"]:
                n = bg["done"].pop(0)
                i = n % 3
                S_.dma(bg["pieces"][n][1], bg["cvo"][i][:], bg["B_cvo"][i], r=[bg["B_cvo"][i]], w=[B_wl[bg["ln"]]])
            if bg["loaded"]:
                n = bg["loaded"].pop(0)
                i = n % 3
                sc = bg["pieces"][n][2]
                if sc is None:
                    S_.op("dve", lambda e: e.tensor_copy(out=bg["cvo"][i][:], in_=bg["stg"][i][:]),
                          r=[bg["B_stg"][i]], w=[bg["B_cvo"][i]])
                else:
                    S_.op("dve", lambda e: e.tensor_scalar(out=bg["cvo"][i][:], in0=bg["stg"][i][:], scalar1=sc, scalar2=None,
                                                            op0=ALU.mult),
                          r=[bg["B_stg"][i], B_par], w=[bg["B_cvo"][i]])
                bg["done"].append(n)
            if load and bg["next"] < len(bg["pieces"]):
                n = bg["next"]
                bg["next"] += 1
                i = n % 3
                S_.dma(bg["stg"][i][:], bg["pieces"][n][0], bg["B_stg"][i], w=[bg["B_stg"][i]])
                bg["loaded"].append(n)

        def bg_flush(finish):
            while bg["loaded"] or bg["done"] or (finish and bg["next"] < len(bg["pieces"])):
                bg_tick(load=finish)

        def load_WB(l, h):
            ap, bf = WB(h)
            for t, off in enumerate([OFF_Q, OFF_K, OFF_V, OFF_BG]):
                S_.dma(ap[:, :, t, :], wbi[l, :, :, off + h * 128:off + (h + 1) * 128], bf, r=[B_wl[l]], w=[bf])

        for l in range(L):
            last = (l == L - 1)
            do_final = last and apply_final
            S_.dma(lng[:], sgu_ln_g[l].partition_broadcast(128), B_lp, w=[B_lp])
            S_.dma(lnb[:], sgu_ln_b[l].partition_broadcast(128), B_lp, w=[B_lp])
            S_.dma(bsr[:], sgu_b[l:l + 1].rearrange("o g t -> o (g t)"), B_lp, w=[B_lp])
            for s in range(NS):
                xsrc = x_in if l == 0 else xs
                t0 = s * S
                B_hT = [Buf("hTd%d" % i) for i in range(NB)]
                B_cT = [[Buf("cTd%d_%d" % (b, i)) for i in range(NB)] for b in range(3)]
                S_.dma(WA, wbi[l, :, :, 0:1536], B_R1, r=[B_wl[l]], w=[B_R1])
                S_.dma(WC, wbi[l, :, :, OFF_U:OFF_U + 1536], B_R2, r=[B_wl[l]], w=[B_R2])

                with phase() as st:
                    xt = [sb("xt%d" % i, [128, D], st=st) for i in range(4)]
                    B_xt = [Buf("xt%d" % i) for i in range(4)]
                    junk = sb("junk", [128, D], BF16, st=st)
                    B_junk = Buf("junk")
                    ssq = [sb("ssq%d" % i, [128, 4], st=st) for i in range(4)]
                    B_ssq = [Buf("ssq%d" % i) for i in range(4)]
                    hb = [sb("hb%d" % i, [128, D], BF16, st=st) for i in range(2)]
                    B_hb = [Buf("hb%d" % i) for i in range(2)]
                    hTo = [sb("hTo%d" % i, [128, 8, 512], BF16, st=st) for i in range(2)]
                    B_hTo = [Buf("hTo%d" % i) for i in range(2)]

                    def stage_a(i):
                        xi = i % 4
                        S_.dma(xt[xi][:], xsrc[t0 + i * 128:t0 + (i + 1) * 128, :], B_xt[xi], w=[B_xt[xi]])
                        S_.op("act", lambda e: e.activation(out=junk[:], in_=xt[xi][:], func=AF.Square, accum_out=ssq[xi][:, 0:1]),
                              r=[B_xt[xi]], w=[B_junk, B_ssq[xi]])

                    def stage_b(i):
                        xi = i % 4
                        S_.op("dve", lambda e: e.tensor_scalar(out=ssq[xi][:, 1:2], in0=ssq[xi][:, 0:1], scalar1=1.0 / D,
                                                               scalar2=EPS, op0=ALU.mult, op1=ALU.add),
                              r=[B_ssq[xi]], w=[B_ssq[xi]])
                        S_.op("act", lambda e: e.activation(out=ssq[xi][:, 2:3], in_=ssq[xi][:, 1:2], func=AF.Sqrt),
                              r=[B_ssq[xi]], w=[B_ssq[xi]])
                        S_.op("dve", lambda e: e.reciprocal(out=ssq[xi][:, 3:4], in_=ssq[xi][:, 2:3]),
                              r=[B_ssq[xi]], w=[B_ssq[xi]])

                    def stage_c(i):
                        xi, si, bi, bk, q4 = i % 4, i % 2, (i // 4) % 2, i % 2, i % 4
                        S_.op("act", lambda e: e.activation(out=hb[si][:], in_=xt[xi][:], func=AF.Copy, scale=ssq[xi][:, 3:4]),
                              r=[B_xt[xi], B_ssq[xi]], w=[B_hb[si]])
                        pbank = banks[bk][:].bitcast(BF16)
                        S_.op("pe", lambda e: [e.transpose(out=pbank[:, k * 128:(k + 1) * 128], in_=hb[si][:, k * 128:(k + 1) * 128],
                                                           identity=ident_b[:]) for k in range(8)],
                              r=[B_hb[si], B_const], w=[bbuf[bk]])
                        S_.op("dve", lambda e: e.tensor_copy(out=hTo[bi][:, :, q4 * 128:(q4 + 1) * 128],
                                                             in_=pbank.rearrange("p (k t) -> p k t", k=8)),
                              r=[bbuf[bk]], w=[B_hTo[bi]])
                        if q4 == 3:
                            tb = i // 4
                            S_.dma(hT_d[:, :, tb * 512:(tb + 1) * 512], hTo[bi][:], B_hTo[bi], r=[B_hTo[bi]], w=[B_hT[tb]])

                    for i in range(NT + 2):
                        if i < NT:
                            stage_a(i)
                        if 0 <= i - 1 < NT:
                            stage_b(i - 1)
                        if 0 <= i - 2 < NT:
                            stage_c(i - 2)
                    S_.barrier()
                if dbg and l == L - 1 and s == NS - 1:
                    S_.dma(dbg_h, hT_d, B_R1, r=B_hT, w=[])
                    S_.barrier()

                with phase() as st:
                    hTb = [sb("hTb%d" % i, [128, 8, 512], BF16, st=st) for i in range(2)]
                    B_hTb = [Buf("hTb%d" % i) for i in range(2)]
                    sg = [sb("sg%d" % i, [128, 512], st=st) for i in range(2)]
                    B_sg = [Buf("sg%d" % i) for i in range(2)]
                    zbb = sb("zbb", [128, 4, 544], BF16, st=st)
                    B_zb = [Buf("zb%d" % j) for j in range(4)]
                    dg = sb("dg", [128, 124, 128], BF16, st=st)
                    B_dg = [Buf("dg%d" % j) for j in range(4)]
                    zc = sb("zc", [128, 4, 512], st=st)
                    B_zc = [Buf("zc%d" % j) for j in range(4)]
                    zc2 = [sb("zc2_%d" % i, [128, 512], st=st) for i in range(2)]
                    B_zc2 = [Buf("zc2_%d" % i) for i in range(2)]
                    sga = sb("sga", [128, 4, 512], st=st)
                    B_sga = [Buf("sga%d" % j) for j in range(4)]
                    mean_sb = sb("mean_sb", [128, 512], st=st)
                    m2 = sb("m2", [128, 512], st=st)
                    rstd = sb("rstd", [128, 512], st=st)
                    B_stat = Buf("stat")
                    B_m2 = Buf("m2")
                    tt = [sb("tt%d" % i, [128, 512], st=st) for i in range(2)]
                    B_tt = [Buf("tt%d" % i) for i in range(2)]
                    cao = [sb("cao%d" % i, [128, 4, 512], BF16, st=st) for i in range(2)]
                    B_cao = [Buf("cao%d" % i) for i in range(2)]
                    S_.dma(hTb[0][:], hT_d[:, :, 0:512], B_hTb[0], r=[B_hT[0]], w=[B_hTb[0]])
                    for tb in range(NB):
                        hi = tb % 2
                        ci = tb % 2
                        if tb + 1 < NB:
                            S_.dma(hTb[1 - hi][:], hT_d[:, :, (tb + 1) * 512:(tb + 2) * 512], B_hTb[1 - hi],
                                   r=[B_hT[tb + 1]], w=[B_hTb[1 - hi]])

                        def emit_proj(j, tb=tb, hi=hi):
                            for bk, off in ((0, OFF_A), (1, OFF_G), (2, OFF_AG)):
                                def f_mm(e, bk=bk, off=off):
                                    for k in range(8):
                                        e.matmul(banks[bk][:], lhsT=WA[:, k, off + j * 128:off + (j + 1) * 128],
                                                 rhs=hTb[hi][:, k, :], start=(k == 0), stop=(k == 7))

                                S_.op("pe", f_mm, r=[B_R1, B_hTb[hi]], w=[bbuf[bk]])
                            si = j % 2
                            S_.op("act", lambda e: e.activation(out=sg[si][:], in_=banks[1][:], func=AF.Sigmoid),
                                  r=[bbuf[1]], w=[B_sg[si]])
                            if tb == 0:
                                S_.op("dve", lambda e, l=l: [
                                    e.tensor_scalar(out=dg[:, j * 31 + tp, :], in0=ident_f[:], scalar1=cpar[:, l, j, tp:tp + 1],
                                                    scalar2=None, op0=ALU.mult) for tp in range(31)],
                                    r=[B_par, B_const], w=[B_dg[j]])
                                S_.op("pool", lambda e: e.memset(zbb[:, j, 0:30], 0.0), w=[B_zb[j]])
                            else:
                                S_.op("pool", lambda e: e.tensor_copy(out=zbb[:, j, 0:30], in_=zbb[:, j, 512:542]),
                                      r=[B_zb[j]], w=[B_zb[j]])
                            S_.op("dve", lambda e: e.tensor_tensor(out=zbb[:, j, 30:542], in0=banks[0][:], in1=sg[si][:], op=ALU.mult),
                                  r=[bbuf[0], B_sg[si]], w=[B_zb[j]])
                            S_.op("act", lambda e: e.activation(out=sga[:, j, :], in_=banks[2][:], func=AF.Silu),
                                  r=[bbuf[2]], w=[B_sga[j]])

                        def emit_conv(j, l=l):
                            cb_ = 3 + (j % 2)
                            si = j % 2

                            def f_cv(e):
                                for tp in range(31):
                                    e.matmul(banks[cb_][:], lhsT=dg[:, j * 31 + tp, :], rhs=zbb[:, j, tp:tp + 512],
                                             start=(tp == 0), stop=(tp == 30))

                            S_.op("pe", f_cv, r=[B_zb[j], B_dg[j]], w=[bbuf[cb_]])
                            S_.op("act", lambda e: e.activation(out=zc[:, j, :], in_=banks[cb_][:], func=AF.Identity,
                                                                bias=cpar[:, l, j, 31:32]),
                                  r=[bbuf[cb_], B_par], w=[B_zc[j]])
                            S_.op("act", lambda e: e.activation(out=zc2[si][:], in_=banks[cb_][:], func=AF.Square,
                                                                bias=cpar[:, l, j, 31:32]),
                                  r=[bbuf[cb_], B_par], w=[B_zc2[si]])
                            S_.op("pe", lambda e: e.matmul(banks[6][:], lhsT=ones512[:], rhs=zc[:, j, :], start=(j == 0), stop=(j == 3)),
                                  r=[B_zc[j], B_const], w=[bbuf[6]])
                            S_.op("pe", lambda e: e.matmul(banks[7][:], lhsT=ones512[:], rhs=zc2[si][:], start=(j == 0), stop=(j == 3)),
                                  r=[B_zc2[si], B_const], w=[bbuf[7]])

                        emit_proj(0)
                        emit_proj(1)
                        emit_conv(0)
                        emit_proj(2)
                        emit_conv(1)
                        emit_proj(3)
                        emit_conv(2)
                        emit_conv(3)
                        S_.op("act", lambda e: e.copy(out=mean_sb[:], in_=banks[6][:]), r=[bbuf[6]], w=[B_stat])
                        S_.op("act", lambda e: e.activation(out=m2[:], in_=banks[6][:], func=AF.Square), r=[bbuf[6]], w=[B_m2])

                        def f_var(e):
                            e.tensor_tensor(out=rstd[:], in0=banks[7][:], in1=m2[:], op=ALU.subtract)
                            e.tensor_scalar(out=rstd[:], in0=rstd[:], scalar1=0.0, scalar2=EPS, op0=ALU.max, op1=ALU.add)

                        S_.op("dve", f_var, r=[bbuf[7], B_m2], w=[B_stat])
                        S_.op("act", lambda e: e.activation(out=m2[:], in_=rstd[:], func=AF.Sqrt), r=[B_stat], w=[B_m2])
                        S_.op("dve", lambda e: e.reciprocal(out=rstd[:], in_=m2[:]), r=[B_m2], w=[B_stat])
                        for j in range(4):
                            ti = j % 2
                            en = "pool" if j % 2 == 0 else "dve"

                            def f_norm(e, j=j, ti=ti):
                                e.tensor_tensor(out=tt[ti][:], in0=zc[:, j, :], in1=mean_sb[:], op=ALU.subtract)
                                e.tensor_tensor(out=tt[ti][:], in0=tt[ti][:], in1=rstd[:], op=ALU.mult)

                            S_.op(en, f_norm, r=[B_zc[j], B_stat], w=[B_tt[ti]])
                            S_.op("act", lambda e, j=j, ti=ti, l=l: e.activation(out=tt[ti][:], in_=tt[ti][:], func=AF.Silu,
                                                                               scale=cpar[:, l, j, 32:33], bias=cpar[:, l, j, 33:34]),
                                  r=[B_tt[ti], B_par], w=[B_tt[ti]])
                            S_.op(en, lambda e, j=j, ti=ti, ci=ci: e.tensor_tensor(out=cao[ci][:, j, :], in0=tt[ti][:],
                                                                                 in1=sga[:, j, :], op=ALU.mult),
                                  r=[B_tt[ti], B_sga[j]], w=[B_cao[ci]])
                        S_.dma(cT_d[0, :, :, tb * 512:(tb + 1) * 512], cao[ci][:], B_cao[ci], r=[B_cao[ci]], w=[B_cT[0][tb]])
                    S_.barrier()
                load_WB(l, 0)

                with phase() as st:
                    hTb = [sb("hTb%d" % i, [128, 8, 512], BF16, st=st) for i in range(3)]
                    B_hTb = [Buf("hTb%d" % i) for i in range(3)]
                    usb = [sb("usb%d" % i, [128, 512], st=st) for i in range(2)]
                    B_usb = [Buf("usb%d" % i) for i in range(2)]
                    sgc = [sb("sgc%d" % i, [128, 512], st=st) for i in range(2)]
                    B_sgc = [Buf("sgc%d" % i) for i in range(2)]
                    stt = sb("stt", [128, 24], st=st)
                    B_stt = Buf("stt")
                    vn = [sb("vn%d" % i, [128, 512], st=st) for i in range(2)]
                    B_vn = [Buf("vn%d" % i) for i in range(2)]
                    vln = [sb("vln%d" % i, [128, 512], BF16, st=st) for i in range(8)]
                    B_vln = [Buf("vln%d" % i) for i in range(8)]
                    t2 = [sb("t2_%d" % i, [128, 512], st=st) for i in range(2)]
                    B_t2 = [Buf("t2_%d" % i) for i in range(2)]
                    cco = [sb("cco%d" % i, [128, 4, 512], BF16, st=st) for i in range(2)]
                    B_cco = [Buf("cco%d" % i) for i in range(2)]
                    def load_h(tb):
                        S_.dma(hTb[tb % 3][:], hT_d[:, :, tb * 512:(tb + 1) * 512], B_hTb[tb % 3], r=[B_hT[tb]], w=[B_hTb[tb % 3]])

                    def emit_ln(tb):
                        hi = tb % 3
                        vs = 4 * (tb % 2)
                        for tq in range(4):
                            def f_sv(e, tq=tq, hi=hi):
                                for k in range(8):
                                    e.matmul(banks[tq][:], lhsT=hTb[hi][:, k, tq * 128:(tq + 1) * 128],
                                             rhs=WC[:, k, 512:1024], start=(k == 0), stop=(k == 7))

                            S_.op("pe", f_sv, r=[B_R2, B_hTb[hi]], w=[bbuf[tq]])
                            mi = tq % 2
                            S_.op("act", lambda e, tq=tq, mi=mi: e.activation(out=vn[mi][:], in_=banks[tq][:], func=AF.Identity,
                                                                            accum_out=stt[:, tq:tq + 1]),
                                  r=[bbuf[tq]], w=[B_vn[mi], B_stt])
                            S_.op("act", lambda e, tq=tq, mi=mi: e.activation(out=vn[mi][:], in_=banks[tq][:], func=AF.Square,
                                                                            accum_out=stt[:, 4 + tq:5 + tq]),
                                  r=[bbuf[tq]], w=[B_vn[mi], B_stt])

                        def f_st1(e):
                            e.tensor_scalar(out=stt[:, 8:12], in0=stt[:, 0:4], scalar1=1.0 / 512, scalar2=None, op0=ALU.mult)
                            e.tensor_tensor(out=stt[:, 12:16], in0=stt[:, 8:12], in1=stt[:, 8:12], op=ALU.mult)
                            e.tensor_scalar(out=stt[:, 4:8], in0=stt[:, 4:8], scalar1=1.0 / 512, scalar2=EPS, op0=ALU.mult, op1=ALU.add)
                            e.tensor_tensor(out=stt[:, 4:8], in0=stt[:, 4:8], in1=stt[:, 12:16], op=ALU.subtract)

                        S_.op("pool", f_st1, r=[B_stt], w=[B_stt])
                        S_.op("act", lambda e: e.activation(out=stt[:, 16:20], in_=stt[:, 4:8], func=AF.Sqrt), r=[B_stt], w=[B_stt])
                        S_.op("dve", lambda e: e.reciprocal(out=stt[:, 20:24], in_=stt[:, 16:20]), r=[B_stt], w=[B_stt])

                        def f_st2(e):
                            e.tensor_tensor(out=stt[:, 12:16], in0=stt[:, 8:12], in1=stt[:, 20:24], op=ALU.mult)
                            e.tensor_scalar(out=stt[:, 12:16], in0=stt[:, 12:16], scalar1=-1.0, scalar2=None, op0=ALU.mult)

                        S_.op("pool", f_st2, r=[B_stt], w=[B_stt])
                        for tq in range(4):
                            mi = tq % 2
                            S_.op("act", lambda e, tq=tq, mi=mi: e.activation(out=vn[mi][:], in_=banks[tq][:], func=AF.Identity,
                                                                            scale=stt[:, 20 + tq:21 + tq], bias=stt[:, 12 + tq:13 + tq]),
                                  r=[bbuf[tq], B_stt], w=[B_vn[mi]])

                            def f_gb(e, mi=mi, tq=tq):
                                e.tensor_tensor(out=vn[mi][:], in0=vn[mi][:], in1=lng[:], op=ALU.mult)
                                return e.tensor_tensor(out=vln[vs + tq][:], in0=vn[mi][:], in1=lnb[:], op=ALU.add)

                            S_.op("pool", f_gb, r=[B_vn[mi], B_lp], w=[B_vn[mi], B_vln[vs + tq]])

                    def emit_gate(tb):
                        hi = tb % 3
                        ci = tb % 2
                        vs = 4 * (tb % 2)
                        for g in range(4):
                            pu, pc_, pm = [(4, 5, 6), (7, 4, 5), (6, 7, 4), (5, 6, 7)][g]
                            gi = g % 2
                            for bk, off in ((pu, 0), (pc_, 1024)):
                                def f_mm(e, bk=bk, off=off, g=g, hi=hi):
                                    ins = None
                                    for k in range(8):
                                        ins = e.matmul(banks[bk][:], lhsT=WC[:, k, off + g * 128:off + (g + 1) * 128],
                                                       rhs=hTb[hi][:, k, :], start=(k == 0), stop=(k == 7))
                                    return ins

                                S_.op("pe", f_mm, r=[B_R2, B_hTb[hi]], w=[bbuf[bk]])
                            S_.op("act", lambda e, gi=gi, pu=pu: e.copy(out=usb[gi][:], in_=banks[pu][:]),
                                  r=[bbuf[pu]], w=[B_usb[gi]])
                            S_.op("act", lambda e, gi=gi, pc_=pc_: e.activation(out=sgc[gi][:], in_=banks[pc_][:], func=AF.Silu),
                                  r=[bbuf[pc_]], w=[B_sgc[gi]])

                            def f_sp(e, pm=pm, g=g, l=l):
                                ins = None
                                for tq in range(4):
                                    e.matmul(banks[pm][:, tq * 128:(tq + 1) * 128], lhsT=vln[vs + tq][:, g * 128:(g + 1) * 128],
                                             rhs=wsT[:, l, g, :], start=True, stop=False)
                                    ins = e.matmul(banks[pm][:, tq * 128:(tq + 1) * 128], lhsT=ones_row[:],
                                                   rhs=bsr[:, g * 128:(g + 1) * 128], start=False, stop=True)
                                return ins

                            S_.op("pe", f_sp, r=B_vln[vs:vs + 4] + [B_par, B_lp, B_const], w=[bbuf[pm]])
                            S_.op("dve", lambda e, gi=gi, pm=pm: e.tensor_tensor(out=t2[gi][:], in0=banks[pm][:], in1=usb[gi][:],
                                                                               op=ALU.mult),
                                  r=[bbuf[pm], B_usb[gi]], w=[B_t2[gi]])
                            S_.op("dve", lambda e, gi=gi, g=g, ci=ci: e.tensor_tensor(out=cco[ci][:, g, :], in0=t2[gi][:],
                                                                                    in1=sgc[gi][:], op=ALU.mult),
                                  r=[B_t2[gi], B_sgc[gi]], w=[B_cco[ci]])

                    load_h(0)
                    if NB > 1:
                        load_h(1)
                    emit_ln(0)
                    for tb in range(NB):
                        ci = tb % 2
                        if tb + 2 < NB:
                            load_h(tb + 2)
                        if tb + 1 < NB:
                            emit_ln(tb + 1)
                        emit_gate(tb)
                        S_.dma(cT_d[2, :, :, tb * 512:(tb + 1) * 512], cco[ci][:], B_cco[ci], r=[B_cco[ci]], w=[B_cT[2][tb]])
                    S_.barrier()
                load_WB(l, 1)

                with phase() as st:
                    hTb = [sb("hTb%d" % i, [128, 8, 512], BF16, st=st) for i in range(2)]
                    B_hTb = [Buf("hTb%d" % i) for i in range(2)]
                    qT = sb("qT", [128, S], BF16, st=st)
                    kT = sb("kT", [128, S], BF16, st=st)
                    vh = sb("vh", [128, NT, 128], BF16, st=st)
                    sgb = sb("sgb", [128, S], st=st)
                    B_qkv = Buf("qkv")
                    pt = [sb("pt%d" % i, [128, 2, 512], BF16, st=st) for i in range(3)]
                    B_pt = [Buf("pt%d" % i) for i in range(3)]
                    osb = sb("osb", [128, 4, 512], st=st)
                    B_osb = Buf("osb")
                    B_oss = Buf("oss")
                    rr = sb("rr", [128, 2, 512], st=st)
                    B_rr = Buf("rr")
                    o2 = sb("o2", [128, 512], st=st)
                    od = sb("od", [128, 512], st=st)
                    B_od = Buf("od")
                    sqa = sb("sqa", [128, S], st=st)
                    oga = sb("oga", [128, S], st=st)
                    B_sqa = [Buf("sqa%d" % i) for i in range(NB)]
                    B_oga = [Buf("oga%d" % i) for i in range(NB)]
                    rs2 = sb("rs2", [128, 512], st=st)
                    B_rs2 = Buf("rs2")
                    sqt = sb("sqt", [128, 512], st=st)
                    B_sqt = Buf("sqt")
                    cbo = [sb("cbo%d" % i, [128, 512], BF16, st=st) for i in range(2)]
                    B_cbo = [Buf("cbo%d" % i) for i in range(2)]
                    cb_cnt = [0]
                    if l + 1 < L:
                        if s == 0:
                            bg_start_layer(l + 1)
                        bg_attach(st)

                    def emit_norm(h, qb):
                        blk = slice(qb * 512, (qb + 1) * 512)
                        S_.op("pe", lambda e: e.matmul(banks[4][:], lhsT=ones128[:], rhs=sqa[:, blk], start=True, stop=True),
                              r=[B_sqa[qb], B_const], w=[bbuf[4]])
                        S_.op("dve", lambda e: e.tensor_scalar(out=rs2[:], in0=banks[4][:], scalar1=EPS, scalar2=None, op0=ALU.add),
                              r=[bbuf[4]], w=[B_rs2])
                        S_.op("act", lambda e: e.activation(out=sqt[:], in_=rs2[:], func=AF.Sqrt), r=[B_rs2], w=[B_sqt])
                        S_.op("dve", lambda e: e.reciprocal(out=rs2[:], in_=sqt[:]), r=[B_sqt], w=[B_rs2])
                        ci = cb_cnt[0] % 2
                        cb_cnt[0] += 1
                        S_.op("dve", lambda e: e.tensor_tensor(out=cbo[ci][:], in0=oga[:, blk], in1=rs2[:], op=ALU.mult),
                              r=[B_rs2, B_oga[qb]], w=[B_cbo[ci]])
                        S_.dma(cT_d[1, :, h, blk], cbo[ci][:], B_cbo[ci], r=[B_cbo[ci]], w=[B_cT[1][qb]])

                    for h in range(4):
                        Wh, B_Wh = WB(h)
                        if h == 0:
                            S_.dma(hTb[0][:], hT_d[:, :, 0:512], B_hTb[0], r=[B_hT[0]], w=[B_hTb[0]])
                        for tb in range(NB):
                            hi = tb % 2
                            if tb + 1 < NB and not (h > 0 and tb == 0 and NB > 1):
                                S_.dma(hTb[1 - hi][:], hT_d[:, :, (tb + 1) * 512:(tb + 2) * 512], B_hTb[1 - hi],
                                       r=[B_hT[tb + 1]], w=[B_hTb[1 - hi]])
                            for t, bk in ((0, 0), (1, 1), (3, 2)):
                                def f_mm(e, bk=bk, t=t, hi=hi, Wh=Wh):
                                    for k in range(8):
                                        e.matmul(banks[bk][:], lhsT=Wh[:, k, t, :], rhs=hTb[hi][:, k, :],
                                                 start=(k == 0), stop=(k == 7))

                                S_.op("pe", f_mm, r=[B_Wh, B_hTb[hi]], w=[bbuf[bk]])

                            def f_v(e, hi=hi, Wh=Wh):
                                for tq in range(4):
                                    for k in range(8):
                                        e.matmul(banks[3][:, tq * 128:(tq + 1) * 128],
                                                 lhsT=hTb[hi][:, k, tq * 128:(tq + 1) * 128], rhs=Wh[:, k, 2, :],
                                                 start=(k == 0), stop=(k == 7))

                            S_.op("pe", f_v, r=[B_Wh, B_hTb[hi]], w=[bbuf[3]])
                            sl = slice(tb * 512, (tb + 1) * 512)
                            S_.op("dve", lambda e, sl=sl: e.tensor_copy(out=qT[:, sl], in_=banks[0][:]), r=[bbuf[0]], w=[B_qkv])
                            S_.op("act", lambda e, sl=sl: e.copy(out=kT[:, sl], in_=banks[1][:]), r=[bbuf[1]], w=[B_qkv])
                            S_.op("act", lambda e, sl=sl: e.activation(out=sgb[:, sl], in_=banks[2][:], func=AF.Silu),
                                  r=[bbuf[2]], w=[B_qkv])
                            S_.op("dve", lambda e, tb=tb: e.tensor_copy(out=vh[:, tb * 4:(tb + 1) * 4, :],
                                                                       in_=banks[3][:].rearrange("p (t n) -> p t n", t=4)),
                                  r=[bbuf[3]], w=[B_qkv])
                            if h > 0:
                                emit_norm(h - 1, tb)
                        if h < 3:
                            S_.dma(hTb[0][:], hT_d[:, :, 0:512], B_hTb[0], r=[B_hT[0]], w=[B_hTb[0]])
                            if NB > 1:
                                S_.dma(hTb[1][:], hT_d[:, :, 512:1024], B_hTb[1], r=[B_hT[1]], w=[B_hTb[1]])
                        if h == 3:
                            S_.dma(WP, wbp[l], B_R1, r=[B_wl[l]], w=[B_R1])
                            S_.dma(WO, wbo[l], B_R2, r=[B_wl[l]], w=[B_R2])
                        steps = [(qb, kt) for qb in range(NB) for kt in range(4 * qb + 4)]

                        def emit_qk(i):
                            qb, kt = steps[i]
                            j = kt - 4 * qb
                            c0 = 128 * j if j > 0 else 0
                            q0 = qb * 512
                            pi = i % 3
                            sb0 = 4 + 2 * (i % 2)

                            def f_qk(e):
                                for u in range(2):
                                    e.matmul(banks[sb0 + u][:, c0:512], lhsT=kT[u * 64:(u + 1) * 64, kt * 128:(kt + 1) * 128],
                                             rhs=qT[u * 64:(u + 1) * 64, q0 + c0:q0 + 512], start=True, stop=True)

                            S_.op("pe", f_qk, r=[B_qkv], w=[bbuf[sb0], bbuf[sb0 + 1]])
                            for u in range(2):
                                S_.op("act", lambda e, u=u: e.activation(out=pt[pi][:, u, c0:512], in_=banks[sb0 + u][:, c0:512],
                                                                       func=AF.Exp, scale=0.125),
                                      r=[bbuf[sb0 + u]], w=[B_pt[pi]])
                            if j >= 0:
                                S_.op("pool", lambda e: (
                                    e.tensor_tensor(out=pt[pi][:, 0, c0:c0 + 128], in0=pt[pi][:, 0, c0:c0 + 128], in1=tri_b[:], op=ALU.mult),
                                    e.tensor_tensor(out=pt[pi][:, 1, c0:c0 + 128], in0=pt[pi][:, 1, c0:c0 + 128], in1=tri_b[:], op=ALU.mult)),
                                    r=[B_pt[pi], B_const], w=[B_pt[pi]])

                        def emit_pv(i, h=h):
                            qb, kt = steps[i]
                            nk = 4 * qb + 4
                            j = kt - 4 * qb
                            c0 = 128 * j if j > 0 else 0
                            pi = i % 3

                            def f_pv(e):
                                for u in range(2):
                                    e.matmul(banks[u][:, c0:512], lhsT=vh[:, kt, :], rhs=pt[pi][:, u, c0:512],
                                             start=(kt == 0), stop=(kt == nk - 1), skip_group_check=True)
                                    e.matmul(banks[2 + u][:, c0:512], lhsT=ones_b[:], rhs=pt[pi][:, u, c0:512],
                                             start=(kt == 0), stop=(kt == nk - 1), skip_group_check=True)

                            S_.op("pe", f_pv, r=[B_pt[pi], B_qkv, B_const], w=[bbuf[0], bbuf[1], bbuf[2], bbuf[3]])
                            if kt == nk - 1:
                                blk = slice(qb * 512, (qb + 1) * 512)
                                S_.op("act", lambda e: [e.copy(out=osb[:, t, :], in_=banks[t][:]) for t in range(2)],
                                      r=[bbuf[0], bbuf[1]], w=[B_osb])
                                S_.op("dve", lambda e: [e.tensor_copy(out=osb[:, t, :], in_=banks[t][:]) for t in (2, 3)],
                                      r=[bbuf[2], bbuf[3]], w=[B_oss])
                                S_.op("dve", lambda e: e.reciprocal(out=rr[:], in_=osb[:, 2:4, :]), r=[B_oss], w=[B_rr])

                                def f_o(e, l=l):
                                    e.tensor_tensor(out=od[:], in0=osb[:, 0, :], in1=rr[:, 0, :], op=ALU.mult)
                                    e.tensor_tensor(out=o2[:], in0=osb[:, 1, :], in1=rr[:, 1, :], op=ALU.mult)
                                    e.scalar_tensor_tensor(out=od[:], in0=o2[:], scalar=nlam[:, l:l + 1], in1=od[:],
                                                           op0=ALU.mult, op1=ALU.add)

                                S_.op("dve", f_o, r=[B_osb, B_rr, B_par], w=[B_od])
                                S_.op("act", lambda e: e.activation(out=sqa[:, blk], in_=od[:], func=AF.Square),
                                      r=[B_od], w=[B_sqa[qb]])
                                S_.op("dve", lambda e, l=l: e.scalar_tensor_tensor(out=oga[:, blk], in0=od[:], scalar=subg[:, l:l + 1],
                                                                                  in1=sgb[:, blk], op0=ALU.mult, op1=ALU.mult),
                                      r=[B_od, B_par, B_qkv], w=[B_oga[qb]])

                        emit_qk(0)
                        for i in range(len(steps)):
                            if i + 1 < len(steps):
                                emit_qk(i + 1)
                            emit_pv(i)
                            if l + 1 < L and i % 12 == 5:
                                bg_tick()
                        if h + 2 < 4:
                            load_WB(l, h + 2)
                    for qb in range(NB):
                        emit_norm(3, qb)
                    if l + 1 < L:
                        bg_flush(finish=(s == NS - 1))
                    S_.barrier()
                if dbg and l == L - 1 and s == NS - 1:
                    S_.dma(dbg_c, cT_d, B_R1, r=B_cT[0] + B_cT[1] + B_cT[2], w=[])
                    S_.barrier()

                with phase() as st:
                    hTb = [sb("hTb%d" % i, [128, 8, 512], BF16, st=st) for i in range(2)]
                    B_hTb = [Buf("hTb%d" % i) for i in range(2)]
                    cTb = [sb("cTb%d" % i, [128, 3, 4, 512], BF16, st=st) for i in range(2)]
                    B_cTb = [Buf("cTb%d" % i) for i in range(2)]
                    WG = [sb("WG%d" % i, [128, 8, 3, 128], BF16, st=st) for i in range(3)]
                    B_WG = [Buf("WG%d" % i) for i in range(3)]
                    sgm = [sb("sgm%d" % i, [128, 512], st=st) for i in range(3)]
                    B_sgm = [Buf("sgm%d" % i) for i in range(3)]
                    ma = [sb("ma%d" % i, [128, 512], st=st) for i in range(2)]
                    B_ma = [Buf("ma%d" % i) for i in range(2)]
                    mb = [sb("mb%d" % i, [128, 512], st=st) for i in range(2)]
                    B_mb = [Buf("mb%d" % i) for i in range(2)]
                    mT = [sb("mT%d" % i, [128, 8, 512], BF16, st=st) for i in range(2)]
                    B_mT = [Buf("mT%d" % i) for i in range(2)]
                    xt = [sb("xt%d" % i, [128, D], st=st) for i in range(2)]
                    B_xt = [Buf("xt%d" % i) for i in range(2)]
                    xo = [sb("xo%d" % i, [128, D], st=st) for i in range(2)]
                    B_xo = [Buf("xo%d" % i) for i in range(2)]
                    junk = sb("junk", [128, D], BF16, st=st)
                    B_junk = Buf("junk")
                    ssq = [sb("ssq%d" % i, [128, 4], st=st) for i in range(2)]
                    B_ssq = [Buf("ssq%d" % i) for i in range(2)]
                    B_xs = Buf("xs_dram")

                    def load_blk(tb):
                        hi = tb % 2
                        S_.dma(hTb[hi][:], hT_d[:, :, tb * 512:(tb + 1) * 512], B_hTb[hi], r=[B_hT[tb]], w=[B_hTb[hi]])
                        for b in range(3):
                            S_.dma(cTb[hi][:, b, :, :], cT_d[b, :, :, tb * 512:(tb + 1) * 512], B_cTb[hi],
                                   r=[B_cT[b][tb]], w=[B_cTb[hi]])

                    cnt = {"wg": 0, "xt": 0}

                    def emit_dcs(tb, dcs):
                        hi = tb % 2
                        for dc in dcs:
                            wi = cnt['wg'] % 3
                            cnt['wg'] += 1
                            S_.dma(WG[wi][:], wbi[l, :, :, OFF_GATES:NCOL].rearrange("p k (b n) -> p k b n", b=3)[:, :, :, dc * 128:(dc + 1) * 128],
                                   B_WG[wi], r=[B_wl[l]], w=[B_WG[wi]])
                            mi = dc % 2
                            for b in range(3):
                                pg, py = 2 * b, 2 * b + 1

                                def f_g(e, pg=pg, b=b, wi=wi, hi=hi):
                                    ins = None
                                    for k in range(8):
                                        ins = e.matmul(banks[pg][:], lhsT=WG[wi][:, k, b, :], rhs=hTb[hi][:, k, :],
                                                       start=(k == 0), stop=(k == 7))
                                    return ins

                                S_.op("pe", f_g, r=[B_WG[wi], B_hTb[hi]], w=[bbuf[pg]])

                                def f_y(e, py=py, b=b, dc=dc, hi=hi):
                                    ins = None
                                    for jj in range(4):
                                        ins = e.matmul(banks[py][:], lhsT=WP[:, b, jj, dc * 128:(dc + 1) * 128],
                                                       rhs=cTb[hi][:, b, jj, :], start=(jj == 0), stop=(jj == 3))
                                    return ins

                                S_.op("pe", f_y, r=[B_R1, B_cTb[hi]], w=[bbuf[py]])
                                S_.op("act", lambda e, b=b, pg=pg: e.activation(out=sgm[b][:], in_=banks[pg][:], func=AF.Sigmoid),
                                      r=[bbuf[pg]], w=[B_sgm[b]])
                                if b == 0:
                                    S_.op("dve", lambda e, mi=mi, py=py: e.tensor_tensor(out=ma[mi][:], in0=banks[py][:],
                                                                                       in1=sgm[0][:], op=ALU.mult),
                                          r=[bbuf[py], B_sgm[0]], w=[B_ma[mi]])
                                else:
                                    S_.op("dve", lambda e, mi=mi, py=py, b=b: e.tensor_tensor(out=mb[mi][:], in0=banks[py][:],
                                                                                            in1=sgm[b][:], op=ALU.mult),
                                          r=[bbuf[py], B_sgm[b]], w=[B_mb[mi]])
                                    if b == 1:
                                        S_.op("pool", lambda e, mi=mi: e.tensor_tensor(out=ma[mi][:], in0=ma[mi][:], in1=mb[mi][:],
                                                                                      op=ALU.add),
                                              r=[B_mb[mi], B_ma[mi]], w=[B_ma[mi]])
                                    else:
                                        S_.op("pool", lambda e, mi=mi, dc=dc: e.tensor_tensor(out=mT[tb % 2][:, dc, :], in0=ma[mi][:],
                                                                                             in1=mb[mi][:], op=ALU.add),
                                              r=[B_mb[mi], B_ma[mi]], w=[B_mT[tb % 2]])

                    def emit_wo(tb):
                        for tq in range(4):
                            xi = cnt['xt'] % 2
                            cnt['xt'] += 1
                            r0 = t0 + tb * 512 + tq * 128
                            S_.dma(xt[xi][:], xsrc[r0:r0 + 128, :], B_xt[xi], r=[B_xs], w=[B_xt[xi]])
                            for hf in range(2):
                                bk = 6 + hf

                                def f_o(e, bk=bk, hf=hf, tq=tq):
                                    ins = None
                                    for k in range(8):
                                        ins = e.matmul(banks[bk][:], lhsT=mT[tb % 2][:, k, tq * 128:(tq + 1) * 128],
                                                       rhs=WO[:, k, hf * 512:(hf + 1) * 512], start=(k == 0), stop=(k == 7))
                                    return ins

                                S_.op("pe", f_o, r=[B_mT[tb % 2], B_R2], w=[bbuf[bk]])
                                S_.op("dve", lambda e, bk=bk, hf=hf, xi=xi: e.tensor_tensor(
                                    out=xo[xi][:, hf * 512:(hf + 1) * 512], in0=banks[bk][:], in1=xt[xi][:, hf * 512:(hf + 1) * 512],
                                    op=ALU.add), r=[bbuf[bk], B_xt[xi]], w=[B_xo[xi]])
                            if not last:
                                S_.dma(xs[r0:r0 + 128, :], xo[xi][:], B_xo[xi], r=[B_xo[xi]], w=[B_xs])
                            elif not do_final:
                                S_.dma(y_out[r0:r0 + 128, :], xo[xi][:], B_xo[xi], r=[B_xo[xi]], w=[B_xs])
                            else:
                                si = xi
                                S_.op("act", lambda e, xi=xi, si=si: e.activation(out=junk[:], in_=xo[xi][:], func=AF.Square,
                                                                                accum_out=ssq[si][:, 0:1]),
                                      r=[B_xo[xi]], w=[B_junk, B_ssq[si]])

                                S_.op("dve", lambda e, si=si: e.tensor_scalar(out=ssq[si][:, 1:2], in0=ssq[si][:, 0:1], scalar1=1.0 / D,
                                                                              scalar2=EPS, op0=ALU.mult, op1=ALU.add),
                                      r=[B_ssq[si]], w=[B_ssq[si]])
                                S_.op("act", lambda e, si=si: e.activation(out=ssq[si][:, 2:3], in_=ssq[si][:, 1:2], func=AF.Sqrt),
                                      r=[B_ssq[si]], w=[B_ssq[si]])
                                S_.op("dve", lambda e, si=si: e.reciprocal(out=ssq[si][:, 3:4], in_=ssq[si][:, 2:3]),
                                      r=[B_ssq[si]], w=[B_ssq[si]])

                                S_.op("act", lambda e, si=si, xi=xi: e.activation(out=xo[xi][:], in_=xo[xi][:], func=AF.Copy,
                                                                                scale=ssq[si][:, 3:4]),
                                      r=[B_ssq[si], B_xo[xi]], w=[B_xo[xi]])
                                S_.op("dve", lambda e, xi=xi: e.tensor_tensor(out=xo[xi][:], in0=xo[xi][:], in1=fgb[:], op=ALU.mult),
                                      r=[B_xo[xi], B_par], w=[B_xo[xi]])
                                S_.dma(y_out[r0:r0 + 128, :], xo[xi][:], B_xo[xi], r=[B_xo[xi]], w=[B_xs])

                    load_blk(0)
                    for tb in range(NB):
                        if tb + 1 < NB:
                            load_blk(tb + 1)
                        emit_dcs(tb, range(0, 2))
                        if tb > 0:
                            emit_wo(tb - 1)
                        emit_dcs(tb, range(2, 8))
                    emit_wo(NB - 1)
                    S_.barrier()
        S_.barrier()
        S_.replay()
    return nc


_NC_CACHE = {}
FUSED_LAYERS = 4


def _get(L, NS, S, apply_final, l0):
    key = (L, NS, S, apply_final, l0)
    if key not in _NC_CACHE:
        _NC_CACHE[key] = build(L, NS, S, apply_final=apply_final, l0=l0)
    return _NC_CACHE[key]


def kernel(**inputs):
    DEPTH, NS, S = 4, 2, 4096
    n = 8
    x = np.ascontiguousarray(np.asarray(inputs["x"], dtype=np.float32)).reshape(n, NS * S, D)
    full = {k: np.ascontiguousarray(np.asarray(v, dtype=np.float32)) for k, v in inputs.items() if k != "x"}
    G = FUSED_LAYERS
    for l0 in range(0, DEPTH, G):
        fin = (l0 + G == DEPTH)
        nc = _get(G, NS, S, fin, l0)
        shared = {k: (v if k == "final_norm_g" else np.ascontiguousarray(v[l0:l0 + G])) for k, v in full.items()}
        in_maps = []
        for c in range(n):
            m = dict(shared)
            m["x"] = np.ascontiguousarray(x[c])
            in_maps.append(m)
        res = run_bass_kernel_spmd(nc, in_maps, core_ids=list(range(n)))
        x = np.stack([np.asarray(r["y"]) for r in res.results], axis=0)
    return x.reshape(16, S, D).astype(np.float32)
```

```python
import math
from contextlib import ExitStack

import numpy as np
import concourse.bass as bass
import concourse.mybir as mybir
from concourse.bass_utils import run_bass_kernel_spmd

F32 = mybir.dt.float32
BF16 = mybir.dt.bfloat16
AF = mybir.ActivationFunctionType
ALU = mybir.AluOpType
AX = mybir.AxisListType

ARENA_MAX = [0]
D = 1024
NCOL = 8192
EPS = 1e-6
SAME_ENG_SYNC = False

OFF_A, OFF_G, OFF_AG = 0, 512, 1024
OFF_Q, OFF_K, OFF_V, OFF_BG = 1536, 2048, 2560, 3072
OFF_U, OFF_SV, OFF_CG = 3584, 4096, 4608
OFF_GATES = 5120


class Buf:
    __slots__ = ("name", "w", "r", "dsem", "dval", "dkey")

    def __init__(self, name):
        self.name = name
        self.w = {}
        self.r = {}
        self.dsem = None
        self.dval = 0
        self.dkey = None


class _Rec:
    def __init__(self):
        self.calls = []

    def __getattr__(self, name):
        def f(*a, **k):
            self.calls.append((name, a, k))
            return None

        return f


class Sched:
    def __init__(self, nc, stack):
        self.nc = nc
        self.stack = stack
        self.names = ["pe", "act", "dve", "pool", "sp"]
        self.streams = {n: [] for n in self.names}
        self.esem = {n: stack.enter_context(nc.semaphore("es_" + n)) for n in ["pe", "act", "dve", "pool"]}
        self.cnt = {n: 0 for n in self.esem}
        self.seen = {n: {} for n in self.names}
        self.all = {}
        self.nd = 0
        self.dpool = {}

    def _deps(self, eng, r, w):
        deps = {}

        def add(d):
            for key, tv in d.items():
                if key not in deps or deps[key][1] < tv[1]:
                    deps[key] = tv

        for b in r:
            add(b.w)
        for b in w:
            add(b.w)
            add(b.r)
        if not SAME_ENG_SYNC and eng != "pool":
            deps.pop(eng, None)
        return deps

    def _wait(self, eng, deps):
        for key, (sem, val) in deps.items():
            if self.seen[eng].get(key, 0) < val:
                self.streams[eng].append(("wait", sem, val))
                self.seen[eng][key] = val

    def op(self, eng, fn, r=(), w=()):
        rec = _Rec()
        fn(rec)
        assert rec.calls
        if eng == "pool" and len(rec.calls) > 1:
            for c in rec.calls:
                self._op1(eng, [c], r, w)
        else:
            self._op1(eng, rec.calls, r, w)

    def _op1(self, eng, calls, r, w):
        self._wait(eng, self._deps(eng, r, w))
        self.cnt[eng] += 1
        tok = (self.esem[eng], self.cnt[eng])
        self.streams[eng].append(("op", calls, self.esem[eng]))
        self.all[eng] = tok
        for b in r:
            b.r[eng] = tok
        for b in w:
            b.w[eng] = tok

    def dma(self, out, in_, sb, r=(), w=(), q="sp"):
        self._wait(q, self._deps(q, r, w))
        if sb.dsem is None:
            if sb.name not in self.dpool:
                self.dpool[sb.name] = [self.stack.enter_context(self.nc.semaphore("ds%d" % self.nd)), "d%d" % self.nd, 0]
                self.nd += 1
            sb.dsem, sb.dkey, sb.dval = self.dpool[sb.name]
        sb.dval += 16
        self.dpool[sb.name][2] = sb.dval
        tok = (sb.dsem, sb.dval)
        self.streams[q].append(("dma", out, in_, sb.dsem))
        self.all[sb.dkey] = tok
        for b in r:
            b.r[sb.dkey] = tok
        for b in w:
            b.w[sb.dkey] = tok

    def barrier(self):
        for e in self.names:
            d = dict(self.all)
            d.pop(e, None)
            self._wait(e, d)

    def replay(self):
        nc = self.nc
        streams = self.streams

        def mk(name):
            def body(e):
                for it in streams[name]:
                    if it[0] == "wait":
                        e.wait_ge(it[1], it[2])
                    elif it[0] == "op":
                        ins = None
                        for (nm, a, k) in it[1]:
                            ins = getattr(e, nm)(*a, **k)
                        ins.then_inc(it[2], 1)
                    else:
                        e.dma_start(out=it[1], in_=it[2]).then_inc(it[3], 16)

            return body

        with nc.Block() as block:
            block.tensor(mk("pe"))
            block.scalar(mk("act"))
            block.vector(mk("dve"))
            block.gpsimd(mk("pool"))
            block.sync(mk("sp"))


def build(L=4, NS=2, S=4096, dbg=False, apply_final=True, l0=0):
    nc = bass.Bass("TRN2", target_bir_lowering=False)
    T = NS * S
    NB = S // 512
    NT = S // 128
    stack = ExitStack()
    with stack:
        stack.enter_context(nc.allow_low_precision("bf16 matmul operands, fp32 accumulation"))

        def din(name, shape):
            return nc.dram_tensor(name, list(shape), F32, kind="ExternalInput").ap()

        x_in = din("x", [T, D])
        norm_g = din("norm_g", [L, D])
        w_in = din("w_in", [L, D, NCOL])
        conv_w = din("conv_w", [L, 31, 512])
        conv_b = din("conv_b", [L, 512])
        conv_ln_g = din("conv_ln_g", [L, 512])
        conv_ln_b = din("conv_ln_b", [L, 512])
        lam_q1 = din("lam_q1", [L, 64])
        lam_k1 = din("lam_k1", [L, 64])
        lam_q2 = din("lam_q2", [L, 64])
        lam_k2 = din("lam_k2", [L, 64])
        diff_norm_g = din("diff_norm_g", [L, 128])
        sgu_ln_g = din("sgu_ln_g", [L, 512])
        sgu_ln_b = din("sgu_ln_b", [L, 512])
        sgu_w = din("sgu_w", [L, 4, 128, 128])
        sgu_b = din("sgu_b", [L, 4, 128])
        w_pa = din("w_pa", [L, 512, D])
        w_pb = din("w_pb", [L, 512, D])
        w_pc = din("w_pc", [L, 512, D])
        w_o = din("w_o", [L, D, D])
        final_g = din("final_norm_g", [D])
        y_out = nc.dram_tensor("y", [T, D], F32, kind="ExternalOutput").ap()

        xs = nc.dram_tensor("xs", [T, D], F32).ap()
        wbi = nc.dram_tensor("wbi", [L, 128, 8, NCOL], BF16).ap()
        wbp = nc.dram_tensor("wbp", [L, 128, 3, 4, D], BF16).ap()
        wbo = nc.dram_tensor("wbo", [L, 128, 8, D], BF16).ap()
        hT_d = nc.dram_tensor("hT", [128, 8, S], BF16).ap()
        cT_d = nc.dram_tensor("cT", [3, 128, 4, S], BF16).ap()
        if dbg:
            dbg_h = nc.dram_tensor("dbg_h", [128, 8, S], BF16, kind="ExternalOutput").ap()
            dbg_c = nc.dram_tensor("dbg_c", [3, 128, 4, S], BF16, kind="ExternalOutput").ap()

        S_ = Sched(nc, stack)
        B_wl = [Buf("wdram%d" % i) for i in range(L)]

        uid = [0]
        ARENA_COLS = 34304
        arena_off = [0]
        arena_box = []

        class _Phase:
            def __enter__(self):
                arena_off[0] = 0
                return self

            def __exit__(self, *a):
                return False

        def phase():
            return _Phase()

        def sb(name, shape, dt=F32, st=None):
            uid[0] += 1
            if st is None:
                return stack.enter_context(nc.sbuf_tensor("%s_%d" % (name, uid[0]), list(shape), dt))
            n = 1
            for v in shape[1:]:
                n *= v
            ncol = n if dt == F32 else (n + 1) // 2
            ncol = (ncol + 7) // 8 * 8
            off = arena_off[0]
            arena_off[0] += ncol
            assert arena_off[0] <= ARENA_COLS, (name, arena_off[0])
            ARENA_MAX[0] = max(ARENA_MAX[0], arena_off[0])
            v = arena_box[0][0:shape[0], off:off + ncol]
            if dt != F32:
                v = v.bitcast(dt)
            v = v[:, 0:n]
            if len(shape) == 3:
                v = v.rearrange("p (a b) -> p a b", a=shape[1])
            elif len(shape) == 4:
                v = v.rearrange("p (a b c) -> p a b c", a=shape[1], b=shape[2])
            return v

        banks = [stack.enter_context(nc.psum_tensor("bank%d" % i, [128, 512], F32)) for i in range(8)]
        bbuf = [Buf("bank%d" % i) for i in range(8)]
        arena_box.append(stack.enter_context(nc.sbuf_tensor("arena", [128, ARENA_COLS], F32)))

        ident_f = sb("ident_f", [128, 128])
        ident_b = sb("ident_b", [128, 128], BF16)
        tri_f = sb("tri_f", [128, 128])
        tri_b = sb("tri_b", [128, 128], BF16)
        ones512 = sb("ones512", [128, 128])
        ones128 = sb("ones128", [128, 128])
        ones_b = sb("ones_b", [128, 128], BF16)
        ones_row = sb("ones_row", [1, 128])
        B_const = Buf("const")

        def mk_consts(e):
            e.memset(ident_f[:], 0.0)
            e.affine_select(out=ident_f[:], in_=ident_f[:], pattern=[[-1, 128]], compare_op=ALU.not_equal,
                            fill=1.0, base=0, channel_multiplier=1)
            e.memset(tri_f[:], 1.0)
            e.affine_select(out=tri_f[:], in_=tri_f[:], pattern=[[1, 128]], compare_op=ALU.is_ge,
                            fill=0.0, base=0, channel_multiplier=-1)
            e.memset(ones512[:], 1.0 / 512)
            e.memset(ones128[:], 1.0 / 128)
            e.memset(ones_b[:], 1.0)
            return e.memset(ones_row[:], 1.0)

        S_.op("pool", mk_consts, w=[B_const])
        S_.op("dve", lambda e: (e.tensor_copy(out=ident_b[:], in_=ident_f[:]),
                                e.tensor_copy(out=tri_b[:], in_=tri_f[:]))[-1], r=[B_const], w=[B_const])

        cpar = sb("cpar", [128, L, 4, 34])
        gcol = sb("gcol", [128, L * 8])
        subg = sb("subg", [128, L])
        nlam = sb("nlam", [128, L])
        wsT = sb("wsT", [128, L, 4, 128], BF16)
        fgb = sb("fgb", [128, D])
        B_par = Buf("par")
        lam_init = [0.8 - 0.6 * math.exp(-0.3 * (l + l0)) for l in range(L)]

        with phase() as pst:
            prow = sb("prow", [34, 512], st=pst)
            B_prow = Buf("prow")
            grow = sb("grow", [L * 8, 128], st=pst)
            B_grow = Buf("grow")
            drow = sb("drow", [L, 128], st=pst)
            B_drow = Buf("drow")
            lq = sb("lq", [128, 4, L * 64], st=pst)
            B_lq = Buf("lq")
            lt = sb("lt", [128, 2, L * 64], st=pst)
            ls = sb("ls", [128, 2, L], st=pst)
            wst = sb("wst", [128, 128], st=pst)
            B_wst = Buf("wst")

            S_.dma(grow[:], norm_g.rearrange("l (k p) -> (l k) p", p=128), B_grow, w=[B_grow])
            S_.op("pe", lambda e: e.transpose(out=banks[0][:, 0:L * 8], in_=grow[:], identity=ident_f[0:L * 8, 0:L * 8]),
                  r=[B_grow, B_const], w=[bbuf[0]])
            S_.op("dve", lambda e: e.tensor_copy(out=gcol[:], in_=banks[0][:, 0:L * 8]), r=[bbuf[0]], w=[B_par])
            S_.dma(drow[:], diff_norm_g, B_drow, w=[B_drow])
            S_.op("pe", lambda e: e.transpose(out=banks[1][:, 0:L], in_=drow[:], identity=ident_f[0:L, 0:L]),
                  r=[B_drow, B_const], w=[bbuf[1]])

            def f_subg(e):
                ins = None
                for l in range(L):
                    ins = e.tensor_scalar(out=subg[:, l:l + 1], in0=banks[1][:, l:l + 1], scalar1=1.0 - lam_init[l],
                                          scalar2=None, op0=ALU.mult)
                return ins

            S_.op("dve", f_subg, r=[bbuf[1]], w=[B_par])
            for i, v in enumerate([lam_q1, lam_k1, lam_q2, lam_k2]):
                S_.dma(lq[:, i, :], v.rearrange("l k -> (l k)").partition_broadcast(128), B_lq, w=[B_lq])

            ls2 = sb("ls2", [128, 2, L], st=pst)
            ls3 = sb("ls3", [128, 2, L], st=pst)
            ljunk = sb("ljunk", [128, 64], st=pst)
            B_lt, B_ls, B_ls2, B_ls3, B_lj = Buf("lt"), Buf("ls"), Buf("ls2"), Buf("ls3"), Buf("lj")
            S_.op("pool", lambda e: (e.tensor_tensor(out=lt[:, 0, :], in0=lq[:, 0, :], in1=lq[:, 1, :], op=ALU.mult),
                                     e.tensor_tensor(out=lt[:, 1, :], in0=lq[:, 2, :], in1=lq[:, 3, :], op=ALU.mult)),
                  r=[B_lq], w=[B_lt])

            def f_lsum(e):
                for i in range(2):
                    for l in range(L):
                        e.activation(out=ljunk[:], in_=lt[:, i, l * 64:(l + 1) * 64], func=AF.Identity,
                                     accum_out=ls[:, i, l:l + 1])

            S_.op("act", f_lsum, r=[B_lt], w=[B_ls, B_lj])
            S_.op("pool", lambda e: e.tensor_copy(out=ls2[:], in_=ls[:]), r=[B_ls], w=[B_ls2])
            S_.op("act", lambda e: e.activation(out=ls3[:], in_=ls2[:], func=AF.Exp), r=[B_ls2], w=[B_ls3])

            def f_lam2(e):
                for l in range(L):
                    e.tensor_tensor(out=nlam[:, l:l + 1], in0=ls3[:, 1, l:l + 1], in1=ls3[:, 0, l:l + 1], op=ALU.subtract)
                for l in range(L):
                    e.tensor_scalar(out=nlam[:, l:l + 1], in0=nlam[:, l:l + 1], scalar1=-lam_init[l],
                                    scalar2=None, op0=ALU.add)

            S_.op("pool", f_lam2, r=[B_ls3], w=[B_par])
            S_.dma(fgb[:], final_g.partition_broadcast(128), B_par, w=[B_par])

            for l in range(L):
                S_.dma(prow[0:31, :], conv_w[l], B_prow, w=[B_prow])
                S_.dma(prow[31:32, :], conv_b[l:l + 1, :], B_prow, w=[B_prow])
                S_.dma(prow[32:33, :], conv_ln_g[l:l + 1, :], B_prow, w=[B_prow])
                S_.dma(prow[33:34, :], conv_ln_b[l:l + 1, :], B_prow, w=[B_prow])
                for j in range(4):
                    bk = 2 + (j % 2)
                    S_.op("pe", lambda e, j=j, bk=bk: e.transpose(out=banks[bk][:, 0:34], in_=prow[0:34, j * 128:(j + 1) * 128],
                                                                 identity=ident_f[0:34, 0:34]),
                          r=[B_prow, B_const], w=[bbuf[bk]])
                    S_.op("dve", lambda e, j=j, bk=bk, l=l: e.tensor_copy(out=cpar[:, l, j, :], in_=banks[bk][:, 0:34]),
                          r=[bbuf[bk]], w=[B_par])
                for g in range(4):
                    bk = 4 + (g % 2)
                    S_.dma(wst[:], sgu_w[l, g], B_wst, w=[B_wst])
                    S_.op("pe", lambda e, bk=bk: e.transpose(out=banks[bk][:, 0:128], in_=wst[:], identity=ident_f[:]),
                          r=[B_wst, B_const], w=[bbuf[bk]])
                    S_.op("dve", lambda e, bk=bk, l=l, g=g: e.tensor_tensor(out=wsT[:, l, g, :], in0=banks[bk][:, 0:128],
                                                                          in1=tri_f[:], op=ALU.mult),
                          r=[bbuf[bk], B_const], w=[B_par])

            NSLOT = 8
            stg = [sb("stg%d" % i, [128, 2048], st=pst) for i in range(NSLOT)]
            B_stg = [Buf("stg%d" % i) for i in range(NSLOT)]
            cvo = [sb("cvo%d" % i, [128, 2048], BF16, st=pst) for i in range(NSLOT)]
            B_cvo = [Buf("cvo%d" % i) for i in range(NSLOT)]
            cv_i = [0]
            cv_eng = ["dve", "act"]

            def convert(src, dst, n, scale_ap, B_w):
                i = cv_i[0] % NSLOT
                eng = cv_eng[cv_i[0] % 2]
                cv_i[0] += 1
                S_.dma(stg[i][:, 0:n], src, B_stg[i], w=[B_stg[i]])
                if eng == "act":
                    if scale_ap is None:
                        fn = lambda e: e.copy(out=cvo[i][:, 0:n], in_=stg[i][:, 0:n])
                    else:
                        fn = lambda e: e.activation(out=cvo[i][:, 0:n], in_=stg[i][:, 0:n], func=AF.Copy, scale=scale_ap)
                else:
                    if scale_ap is None:
                        fn = lambda e: e.tensor_copy(out=cvo[i][:, 0:n], in_=stg[i][:, 0:n])
                    else:
                        fn = lambda e: e.tensor_scalar(out=cvo[i][:, 0:n], in0=stg[i][:, 0:n], scalar1=scale_ap,
                                                       scalar2=None, op0=ALU.mult)
                S_.op(eng, fn, r=[B_stg[i], B_par], w=[B_cvo[i]])
                S_.dma(dst, cvo[i][:, 0:n], B_cvo[i], r=[B_cvo[i]], w=[B_w])

            for l in range(1):
                for k in range(8):
                    for pc in range(4):
                        convert(w_in[l, k * 128:(k + 1) * 128, pc * 2048:(pc + 1) * 2048],
                                wbi[l, :, k, pc * 2048:(pc + 1) * 2048], 2048, gcol[:, l * 8 + k:l * 8 + k + 1], B_wl[l])
                for bi, wp in enumerate([w_pa, w_pb, w_pc]):
                    for j in range(4):
                        convert(wp[l, j * 128:(j + 1) * 128, :], wbp[l, :, bi, j, :], 1024, None, B_wl[l])
                for k in range(8):
                    convert(w_o[l, k * 128:(k + 1) * 128, :], wbo[l, :, k, :], 1024, None, B_wl[l])
            S_.barrier()

        R1 = sb("R1", [128, 8 * 1536], BF16)
        R2 = sb("R2", [128, 8 * 1536], BF16)
        B_R1, B_R2 = Buf("R1"), Buf("R2")
        lng = sb("lng", [128, 512])
        lnb = sb("lnb", [128, 512])
        bsr = sb("bsr", [1, 512])
        B_lp = Buf("layerpar")

        WA = R1[:].rearrange("p (k n) -> p k n", k=8)
        WC = R2[:].rearrange("p (k n) -> p k n", k=8)
        WP = R1[:].rearrange("p (b j n) -> p b j n", b=3, j=4)
        WO = R2[:, 0:8 * D].rearrange("p (k n) -> p k n", k=8)

        def WB(h):
            reg = R1 if h % 2 == 0 else R2
            off = (h // 2) * 4096
            return reg[:, off:off + 4096].rearrange("p (k t n) -> p k t n", k=8, t=4), (B_R1 if h % 2 == 0 else B_R2)

        def bg_pieces(ln):
            ps = []
            for k in range(8):
                for pc in range(8):
                    ps.append((w_in[ln, k * 128:(k + 1) * 128, pc * 1024:(pc + 1) * 1024],
                               wbi[ln, :, k, pc * 1024:(pc + 1) * 1024], gcol[:, ln * 8 + k:ln * 8 + k + 1]))
            for bi, wp in enumerate([w_pa, w_pb, w_pc]):
                for j in range(4):
                    ps.append((wp[ln, j * 128:(j + 1) * 128, :], wbp[ln, :, bi, j, :], None))
            for k in range(8):
                ps.append((w_o[ln, k * 128:(k + 1) * 128, :], wbo[ln, :, k, :], None))
            return ps

        bg = {"pieces": [], "next": 0, "ln": None, "stg": None, "cvo": None, "B_stg": None, "B_cvo": None,
              "loaded": [], "done": []}

        def bg_start_layer(ln):
            bg["pieces"] = bg_pieces(ln)
            bg["next"] = 0
            bg["ln"] = ln

        def bg_attach(st):
            bg["stg"] = [sb("bgs%d" % i, [128, 1024], st=st) for i in range(3)]
            bg["cvo"] = [sb("bgo%d" % i, [128, 1024], BF16, st=st) for i in range(3)]
            bg["B_stg"] = [Buf("bgs%d" % i) for i in range(3)]
            bg["B_cvo"] = [Buf("bgo%d" % i) for i in range(3)]
            bg["loaded"] = []
            bg["done"] = []

        def bg_tick(load=True):
            if bg["done"]:
                n = bg["done"].pop(0)
                i = n % 3
                S_.dma(bg["pieces"][n][1], bg["cvo"][i][:], bg["B_cvo"][i], r=[bg["B_cvo"][i]], w=[B_wl[bg["ln"]]])
            if bg["loaded"]:
                n = bg["loaded"].pop(0)
                i = n % 3
                sc = bg["pieces"][n][2]
                if sc is None:
                    S_.op("dve", lambda e: e.tensor_copy(out=bg["cvo"][i][:], in_=bg["stg"][i][:]),
                          r=[bg["B_stg"][i]], w=[bg["B_cvo"][i]])
                else:
                    S_.op("dve", lambda e: e.tensor_scalar(out=bg["cvo"][i][:], in0=bg["stg"][i][:], scalar1=sc, scalar2=None,
                                                            op0=ALU.mult),
                          r=[bg["B_stg"][i], B_par], w=[bg["B_cvo"][i]])
                bg["done"].append(n)
            if load and bg["next"] < len(bg["pieces"]):
                n = bg["next"]
                bg["next"] += 1
                i = n % 3
                S_.dma(bg["stg"][i][:], bg["pieces"][n][0], bg["B_stg"][i], w=[bg["B_stg"][i]])
                bg["loaded"].append(n)

        def bg_flush(finish):
            while bg["loaded"] or bg["done"] or (finish and bg["next"] < len(bg["pieces"])):
                bg_tick(load=finish)

        def load_WB(l, h):
            ap, bf = WB(h)
            for t, off in enumerate([OFF_Q, OFF_K, OFF_V, OFF_BG]):
                S_.dma(ap[:, :, t, :], wbi[l, :, :, off + h * 128:off + (h + 1) * 128], bf, r=[B_wl[l]], w=[bf])

        for l in range(L):
            last = (l == L - 1)
            do_final = last and apply_final
            S_.dma(lng[:], sgu_ln_g[l].partition_broadcast(128), B_lp, w=[B_lp])
            S_.dma(lnb[:], sgu_ln_b[l].partition_broadcast(128), B_lp, w=[B_lp])
            S_.dma(bsr[:], sgu_b[l:l + 1].rearrange("o g t -> o (g t)"), B_lp, w=[B_lp])
            for s in range(NS):
                xsrc = x_in if l == 0 else xs
                t0 = s * S
                B_hT = [Buf("hTd%d" % i) for i in range(NB)]
                B_cT = [[Buf("cTd%d_%d" % (b, i)) for i in range(NB)] for b in range(3)]
                S_.dma(WA, wbi[l, :, :, 0:1536], B_R1, r=[B_wl[l]], w=[B_R1])
                S_.dma(WC, wbi[l, :, :, OFF_U:OFF_U + 1536], B_R2, r=[B_wl[l]], w=[B_R2])

                with phase() as st:
                    xt = [sb("xt%d" % i, [128, D], st=st) for i in range(4)]
                    B_xt = [Buf("xt%d" % i) for i in range(4)]
                    junk = sb("junk", [128, D], BF16, st=st)
                    B_junk = Buf("junk")
                    ssq = [sb("ssq%d" % i, [128, 4], st=st) for i in range(4)]
                    B_ssq = [Buf("ssq%d" % i) for i in range(4)]
                    hb = [sb("hb%d" % i, [128, D], BF16, st=st) for i in range(2)]
                    B_hb = [Buf("hb%d" % i) for i in range(2)]
                    hTo = [sb("hTo%d" % i, [128, 8, 512], BF16, st=st) for i in range(2)]
                    B_hTo = [Buf("hTo%d" % i) for i in range(2)]

                    def stage_a(i):
                        xi = i % 4
                        S_.dma(xt[xi][:], xsrc[t0 + i * 128:t0 + (i + 1) * 128, :], B_xt[xi], w=[B_xt[xi]])
                        S_.op("act", lambda e: e.activation(out=junk[:], in_=xt[xi][:], func=AF.Square, accum_out=ssq[xi][:, 0:1]),
                              r=[B_xt[xi]], w=[B_junk, B_ssq[xi]])

                    def stage_b(i):
                        xi = i % 4
                        S_.op("dve", lambda e: e.tensor_scalar(out=ssq[xi][:, 1:2], in0=ssq[xi][:, 0:1], scalar1=1.0 / D,
                                                               scalar2=EPS, op0=ALU.mult, op1=ALU.add),
                              r=[B_ssq[xi]], w=[B_ssq[xi]])
                        S_.op("act", lambda e: e.activation(out=ssq[xi][:, 2:3], in_=ssq[xi][:, 1:2], func=AF.Sqrt),
                              r=[B_ssq[xi]], w=[B_ssq[xi]])
                        S_.op("dve", lambda e: e.reciprocal(out=ssq[xi][:, 3:4], in_=ssq[xi][:, 2:3]),
                              r=[B_ssq[xi]], w=[B_ssq[xi]])

                    def stage_c(i):
                        xi, si, bi, bk, q4 = i % 4, i % 2, (i // 4) % 2, i % 2, i % 4
                        S_.op("act", lambda e: e.activation(out=hb[si][:], in_=xt[xi][:], func=AF.Copy, scale=ssq[xi][:, 3:4]),
                              r=[B_xt[xi], B_ssq[xi]], w=[B_hb[si]])
                        pbank = banks[bk][:].bitcast(BF16)
                        S_.op("pe", lambda e: [e.transpose(out=pbank[:, k * 128:(k + 1) * 128], in_=hb[si][:, k * 128:(k + 1) * 128],
                                                           identity=ident_b[:]) for k in range(8)],
                              r=[B_hb[si], B_const], w=[bbuf[bk]])
                        S_.op("dve", lambda e: e.tensor_copy(out=hTo[bi][:, :, q4 * 128:(q4 + 1) * 128],
                                                             in_=pbank.rearrange("p (k t) -> p k t", k=8)),
                              r=[bbuf[bk]], w=[B_hTo[bi]])
                        if q4 == 3:
                            tb = i // 4
                            S_.dma(hT_d[:, :, tb * 512:(tb + 1) * 512], hTo[bi][:], B_hTo[bi], r=[B_hTo[bi]], w=[B_hT[tb]])

                    for i in range(NT + 2):
                        if i < NT:
                            stage_a(i)
                        if 0 <= i - 1 < NT:
                            stage_b(i - 1)
                        if 0 <= i - 2 < NT:
                            stage_c(i - 2)
                    S_.barrier()
                if dbg and l == L - 1 and s == NS - 1:
                    S_.dma(dbg_h, hT_d, B_R1, r=B_hT, w=[])
                    S_.barrier()

                with phase() as st:
                    hTb = [sb("hTb%d" % i, [128, 8, 512], BF16, st=st) for i in range(2)]
                    B_hTb = [Buf("hTb%d" % i) for i in range(2)]
                    sg = [sb("sg%d" % i, [128, 512], st=st) for i in range(2)]
                    B_sg = [Buf("sg%d" % i) for i in range(2)]
                    zbb = sb("zbb", [128, 4, 544], BF16, st=st)
                    B_zb = [Buf("zb%d" % j) for j in range(4)]
                    dg = sb("dg", [128, 124, 128], BF16, st=st)
                    B_dg = [Buf("dg%d" % j) for j in range(4)]
                    zc = sb("zc", [128, 4, 512], st=st)
                    B_zc = [Buf("zc%d" % j) for j in range(4)]
                    zc2 = [sb("zc2_%d" % i, [128, 512], st=st) for i in range(2)]
                    B_zc2 = [Buf("zc2_%d" % i) for i in range(2)]
                    sga = sb("sga", [128, 4, 512], st=st)
                    B_sga = [Buf("sga%d" % j) for j in range(4)]
                    mean_sb = sb("mean_sb", [128, 512], st=st)
                    m2 = sb("m2", [128, 512], st=st)
                    rstd = sb("rstd", [128, 512], st=st)
                    B_stat = Buf("stat")
                    B_m2 = Buf("m2")
                    tt = [sb("tt%d" % i, [128, 512], st=st) for i in range(2)]
                    B_tt = [Buf("tt%d" % i) for i in range(2)]
                    cao = [sb("cao%d" % i, [128, 4, 512], BF16, st=st) for i in range(2)]
                    B_cao = [Buf("cao%d" % i) for i in range(2)]
                    S_.dma(hTb[0][:], hT_d[:, :, 0:512], B_hTb[0], r=[B_hT[0]], w=[B_hTb[0]])
                    for tb in range(NB):
                        hi = tb % 2
                        ci = tb % 2
                        if tb + 1 < NB:
                            S_.dma(hTb[1 - hi][:], hT_d[:, :, (tb + 1) * 512:(tb + 2) * 512], B_hTb[1 - hi],
                                   r=[B_hT[tb + 1]], w=[B_hTb[1 - hi]])

                        def emit_proj(j, tb=tb, hi=hi):
                            for bk, off in ((0, OFF_A), (1, OFF_G), (2, OFF_AG)):
                                def f_mm(e, bk=bk, off=off):
                                    for k in range(8):
                                        e.matmul(banks[bk][:], lhsT=WA[:, k, off + j * 128:off + (j + 1) * 128],
                                                 rhs=hTb[hi][:, k, :], start=(k == 0), stop=(k == 7))

                                S_.op("pe", f_mm, r=[B_R1, B_hTb[hi]], w=[bbuf[bk]])
                            si = j % 2
                            S_.op("act", lambda e: e.activation(out=sg[si][:], in_=banks[1][:], func=AF.Sigmoid),
                                  r=[bbuf[1]], w=[B_sg[si]])
                            if tb == 0:
                                S_.op("dve", lambda e, l=l: [
                                    e.tensor_scalar(out=dg[:, j * 31 + tp, :], in0=ident_f[:], scalar1=cpar[:, l, j, tp:tp + 1],
                                                    scalar2=None, op0=ALU.mult) for tp in range(31)],
                                    r=[B_par, B_const], w=[B_dg[j]])
                                S_.op("pool", lambda e: e.memset(zbb[:, j, 0:30], 0.0), w=[B_zb[j]])
                            else:
                                S_.op("pool", lambda e: e.tensor_copy(out=zbb[:, j, 0:30], in_=zbb[:, j, 512:542]),
                                      r=[B_zb[j]], w=[B_zb[j]])
                            S_.op("dve", lambda e: e.tensor_tensor(out=zbb[:, j, 30:542], in0=banks[0][:], in1=sg[si][:], op=ALU.mult),
                                  r=[bbuf[0], B_sg[si]], w=[B_zb[j]])
                            S_.op("act", lambda e: e.activation(out=sga[:, j, :], in_=banks[2][:], func=AF.Silu),
                                  r=[bbuf[2]], w=[B_sga[j]])

                        def emit_conv(j, l=l):
                            cb_ = 3 + (j % 2)
                            si = j % 2

                            def f_cv(e):
                                for tp in range(31):
                                    e.matmul(banks[cb_][:], lhsT=dg[:, j * 31 + tp, :], rhs=zbb[:, j, tp:tp + 512],
                                             start=(tp == 0), stop=(tp == 30))

                            S_.op("pe", f_cv, r=[B_zb[j], B_dg[j]], w=[bbuf[cb_]])
                            S_.op("act", lambda e: e.activation(out=zc[:, j, :], in_=banks[cb_][:], func=AF.Identity,
                                                                bias=cpar[:, l, j, 31:32]),
                                  r=[bbuf[cb_], B_par], w=[B_zc[j]])
                            S_.op("act", lambda e: e.activation(out=zc2[si][:], in_=banks[cb_][:], func=AF.Square,
                                                                bias=cpar[:, l, j, 31:32]),
                                  r=[bbuf[cb_], B_par], w=[B_zc2[si]])

                        def emit_stats(j):
                            si = j % 2
                            S_.op("pe", lambda e: e.matmul(banks[6][:], lhsT=ones512[:], rhs=zc[:, j, :], start=(j == 0), stop=(j == 3)),
                                  r=[B_zc[j], B_const], w=[bbuf[6]])
                            S_.op("pe", lambda e: e.matmul(banks[7][:], lhsT=ones512[:], rhs=zc2[si][:], start=(j == 0), stop=(j == 3)),
                                  r=[B_zc2[si], B_const], w=[bbuf[7]])

                        emit_proj(0)
                        emit_proj(1)
                        emit_conv(0)
                        emit_proj(2)
                        emit_stats(0)
                        emit_conv(1)
                        emit_proj(3)
                        emit_stats(1)
                        emit_conv(2)
                        emit_conv(3)
                        emit_stats(2)
                        emit_stats(3)
                        S_.op("act", lambda e: e.copy(out=mean_sb[:], in_=banks[6][:]), r=[bbuf[6]], w=[B_stat])
                        S_.op("act", lambda e: e.activation(out=m2[:], in_=banks[6][:], func=AF.Square), r=[bbuf[6]], w=[B_m2])

                        def f_var(e):
                            e.tensor_tensor(out=rstd[:], in0=banks[7][:], in1=m2[:], op=ALU.subtract)
                            e.tensor_scalar(out=rstd[:], in0=rstd[:], scalar1=0.0, scalar2=EPS, op0=ALU.max, op1=ALU.add)

                        S_.op("dve", f_var, r=[bbuf[7], B_m2], w=[B_stat])
                        S_.op("act", lambda e: e.activation(out=m2[:], in_=rstd[:], func=AF.Sqrt), r=[B_stat], w=[B_m2])
                        S_.op("dve", lambda e: e.reciprocal(out=rstd[:], in_=m2[:]), r=[B_m2], w=[B_stat])
                        for j in range(4):
                            ti = j % 2
                            en = "pool" if j % 2 == 0 else "dve"

                            def f_norm(e, j=j, ti=ti):
                                e.tensor_tensor(out=tt[ti][:], in0=zc[:, j, :], in1=mean_sb[:], op=ALU.subtract)
                                e.tensor_tensor(out=tt[ti][:], in0=tt[ti][:], in1=rstd[:], op=ALU.mult)

                            S_.op(en, f_norm, r=[B_zc[j], B_stat], w=[B_tt[ti]])
                            S_.op("act", lambda e, j=j, ti=ti, l=l: e.activation(out=tt[ti][:], in_=tt[ti][:], func=AF.Silu,
                                                                               scale=cpar[:, l, j, 32:33], bias=cpar[:, l, j, 33:34]),
                                  r=[B_tt[ti], B_par], w=[B_tt[ti]])
                            S_.op(en, lambda e, j=j, ti=ti, ci=ci: e.tensor_tensor(out=cao[ci][:, j, :], in0=tt[ti][:],
                                                                                 in1=sga[:, j, :], op=ALU.mult),
                                  r=[B_tt[ti], B_sga[j]], w=[B_cao[ci]])
                        S_.dma(cT_d[0, :, :, tb * 512:(tb + 1) * 512], cao[ci][:], B_cao[ci], r=[B_cao[ci]], w=[B_cT[0][tb]])
                    S_.barrier()
                load_WB(l, 0)

                with phase() as st:
                    hTb = [sb("hTb%d" % i, [128, 8, 512], BF16, st=st) for i in range(3)]
                    B_hTb = [Buf("hTb%d" % i) for i in range(3)]
                    usb = [sb("usb%d" % i, [128, 512], st=st) for i in range(2)]
                    B_usb = [Buf("usb%d" % i) for i in range(2)]
                    sgc = [sb("sgc%d" % i, [128, 512], st=st) for i in range(2)]
                    B_sgc = [Buf("sgc%d" % i) for i in range(2)]
                    stt = sb("stt", [128, 24], st=st)
                    B_stt = Buf("stt")
                    vn = [sb("vn%d" % i, [128, 512], st=st) for i in range(2)]
                    B_vn = [Buf("vn%d" % i) for i in range(2)]
                    vln = [sb("vln%d" % i, [128, 512], BF16, st=st) for i in range(8)]
                    B_vln = [Buf("vln%d" % i) for i in range(8)]
                    t2 = [sb("t2_%d" % i, [128, 512], st=st) for i in range(2)]
                    B_t2 = [Buf("t2_%d" % i) for i in range(2)]
                    cco = [sb("cco%d" % i, [128, 4, 512], BF16, st=st) for i in range(2)]
                    B_cco = [Buf("cco%d" % i) for i in range(2)]
                    def load_h(tb):
                        S_.dma(hTb[tb % 3][:], hT_d[:, :, tb * 512:(tb + 1) * 512], B_hTb[tb % 3], r=[B_hT[tb]], w=[B_hTb[tb % 3]])

                    def emit_ln(tb):
                        hi = tb % 3
                        vs = 4 * (tb % 2)
                        for tq in range(4):
                            def f_sv(e, tq=tq, hi=hi):
                                for k in range(8):
                                    e.matmul(banks[tq][:], lhsT=hTb[hi][:, k, tq * 128:(tq + 1) * 128],
                                             rhs=WC[:, k, 512:1024], start=(k == 0), stop=(k == 7))

                            S_.op("pe", f_sv, r=[B_R2, B_hTb[hi]], w=[bbuf[tq]])
                            mi = tq % 2
                            S_.op("act", lambda e, tq=tq, mi=mi: e.activation(out=vn[mi][:], in_=banks[tq][:], func=AF.Identity,
                                                                            accum_out=stt[:, tq:tq + 1]),
                                  r=[bbuf[tq]], w=[B_vn[mi], B_stt])
                            S_.op("act", lambda e, tq=tq, mi=mi: e.activation(out=vn[mi][:], in_=banks[tq][:], func=AF.Square,
                                                                            accum_out=stt[:, 4 + tq:5 + tq]),
                                  r=[bbuf[tq]], w=[B_vn[mi], B_stt])

                        def f_st1(e):
                            e.tensor_scalar(out=stt[:, 8:12], in0=stt[:, 0:4], scalar1=1.0 / 512, scalar2=None, op0=ALU.mult)
                            e.tensor_tensor(out=stt[:, 12:16], in0=stt[:, 8:12], in1=stt[:, 8:12], op=ALU.mult)
                            e.tensor_scalar(out=stt[:, 4:8], in0=stt[:, 4:8], scalar1=1.0 / 512, scalar2=EPS, op0=ALU.mult, op1=ALU.add)
                            e.tensor_tensor(out=stt[:, 4:8], in0=stt[:, 4:8], in1=stt[:, 12:16], op=ALU.subtract)

                        S_.op("pool", f_st1, r=[B_stt], w=[B_stt])
                        S_.op("act", lambda e: e.activation(out=stt[:, 16:20], in_=stt[:, 4:8], func=AF.Sqrt), r=[B_stt], w=[B_stt])
                        S_.op("dve", lambda e: e.reciprocal(out=stt[:, 20:24], in_=stt[:, 16:20]), r=[B_stt], w=[B_stt])

                        def f_st2(e):
                            e.tensor_tensor(out=stt[:, 12:16], in0=stt[:, 8:12], in1=stt[:, 20:24], op=ALU.mult)
                            e.tensor_scalar(out=stt[:, 12:16], in0=stt[:, 12:16], scalar1=-1.0, scalar2=None, op0=ALU.mult)

                        S_.op("pool", f_st2, r=[B_stt], w=[B_stt])
                        for tq in range(4):
                            mi = tq % 2
                            S_.op("act", lambda e, tq=tq, mi=mi: e.activation(out=vn[mi][:], in_=banks[tq][:], func=AF.Identity,
                                                                            scale=stt[:, 20 + tq:21 + tq], bias=stt[:, 12 + tq:13 + tq]),
                                  r=[bbuf[tq], B_stt], w=[B_vn[mi]])

                            def f_gb(e, mi=mi, tq=tq):
                                e.tensor_tensor(out=vn[mi][:], in0=vn[mi][:], in1=lng[:], op=ALU.mult)
                                return e.tensor_tensor(out=vln[vs + tq][:], in0=vn[mi][:], in1=lnb[:], op=ALU.add)

                            S_.op("pool", f_gb, r=[B_vn[mi], B_lp], w=[B_vn[mi], B_vln[vs + tq]])

                    def emit_gate(tb):
                        hi = tb % 3
                        ci = tb % 2
                        vs = 4 * (tb % 2)
                        for g in range(4):
                            pu, pc_, pm = [(4, 5, 6), (7, 4, 5), (6, 7, 4), (5, 6, 7)][g]
                            gi = g % 2
                            for bk, off in ((pu, 0), (pc_, 1024)):
                                def f_mm(e, bk=bk, off=off, g=g, hi=hi):
                                    ins = None
                                    for k in range(8):
                                        ins = e.matmul(banks[bk][:], lhsT=WC[:, k, off + g * 128:off + (g + 1) * 128],
                                                       rhs=hTb[hi][:, k, :], start=(k == 0), stop=(k == 7))
                                    return ins

                                S_.op("pe", f_mm, r=[B_R2, B_hTb[hi]], w=[bbuf[bk]])
                            S_.op("act", lambda e, gi=gi, pu=pu: e.copy(out=usb[gi][:], in_=banks[pu][:]),
                                  r=[bbuf[pu]], w=[B_usb[gi]])
                            S_.op("act", lambda e, gi=gi, pc_=pc_: e.activation(out=sgc[gi][:], in_=banks[pc_][:], func=AF.Silu),
                                  r=[bbuf[pc_]], w=[B_sgc[gi]])

                            def f_sp(e, pm=pm, g=g, l=l):
                                ins = None
                                for tq in range(4):
                                    e.matmul(banks[pm][:, tq * 128:(tq + 1) * 128], lhsT=vln[vs + tq][:, g * 128:(g + 1) * 128],
                                             rhs=wsT[:, l, g, :], start=True, stop=False)
                                    ins = e.matmul(banks[pm][:, tq * 128:(tq + 1) * 128], lhsT=ones_row[:],
                                                   rhs=bsr[:, g * 128:(g + 1) * 128], start=False, stop=True)
                                return ins

                            S_.op("pe", f_sp, r=B_vln[vs:vs + 4] + [B_par, B_lp, B_const], w=[bbuf[pm]])
                            S_.op("dve", lambda e, gi=gi, pm=pm: e.tensor_tensor(out=t2[gi][:], in0=banks[pm][:], in1=usb[gi][:],
                                                                               op=ALU.mult),
                                  r=[bbuf[pm], B_usb[gi]], w=[B_t2[gi]])
                            S_.op("dve", lambda e, gi=gi, g=g, ci=ci: e.tensor_tensor(out=cco[ci][:, g, :], in0=t2[gi][:],
                                                                                    in1=sgc[gi][:], op=ALU.mult),
                                  r=[B_t2[gi], B_sgc[gi]], w=[B_cco[ci]])

                    load_h(0)
                    if NB > 1:
                        load_h(1)
                    emit_ln(0)
                    for tb in range(NB):
                        ci = tb % 2
                        if tb + 2 < NB:
                            load_h(tb + 2)
                        if tb + 1 < NB:
                            emit_ln(tb + 1)
                        emit_gate(tb)
                        S_.dma(cT_d[2, :, :, tb * 512:(tb + 1) * 512], cco[ci][:], B_cco[ci], r=[B_cco[ci]], w=[B_cT[2][tb]])
                    S_.barrier()
                load_WB(l, 1)

                with phase() as st:
                    hTb = [sb("hTb%d" % i, [128, 8, 512], BF16, st=st) for i in range(2)]
                    B_hTb = [Buf("hTb%d" % i) for i in range(2)]
                    qT = sb("qT", [128, S], BF16, st=st)
                    kT = sb("kT", [128, S], BF16, st=st)
                    vh = sb("vh", [128, NT, 128], BF16, st=st)
                    sgb = sb("sgb", [128, S], st=st)
                    B_qkv = Buf("qkv")
                    pt = [sb("pt%d" % i, [128, 2, 512], BF16, st=st) for i in range(3)]
                    B_pt = [Buf("pt%d" % i) for i in range(3)]
                    osb = sb("osb", [128, 4, 512], st=st)
                    B_osb = Buf("osb")
                    B_oss = Buf("oss")
                    rr = sb("rr", [128, 2, 512], st=st)
                    B_rr = Buf("rr")
                    o2 = sb("o2", [128, 512], st=st)
                    od = sb("od", [128, 512], st=st)
                    B_od = Buf("od")
                    sqa = sb("sqa", [128, S], st=st)
                    oga = sb("oga", [128, S], st=st)
                    B_sqa = [Buf("sqa%d" % i) for i in range(NB)]
                    B_oga = [Buf("oga%d" % i) for i in range(NB)]
                    rs2 = sb("rs2", [128, 512], st=st)
                    B_rs2 = Buf("rs2")
                    sqt = sb("sqt", [128, 512], st=st)
                    B_sqt = Buf("sqt")
                    cbo = [sb("cbo%d" % i, [128, 512], BF16, st=st) for i in range(2)]
                    B_cbo = [Buf("cbo%d" % i) for i in range(2)]
                    cb_cnt = [0]
                    if l + 1 < L:
                        if s == 0:
                            bg_start_layer(l + 1)
                        bg_attach(st)

                    def emit_norm(h, qb):
                        blk = slice(qb * 512, (qb + 1) * 512)
                        S_.op("pe", lambda e: e.matmul(banks[4][:], lhsT=ones128[:], rhs=sqa[:, blk], start=True, stop=True),
                              r=[B_sqa[qb], B_const], w=[bbuf[4]])
                        S_.op("dve", lambda e: e.tensor_scalar(out=rs2[:], in0=banks[4][:], scalar1=EPS, scalar2=None, op0=ALU.add),
                              r=[bbuf[4]], w=[B_rs2])
                        S_.op("act", lambda e: e.activation(out=sqt[:], in_=rs2[:], func=AF.Sqrt), r=[B_rs2], w=[B_sqt])
                        S_.op("dve", lambda e: e.reciprocal(out=rs2[:], in_=sqt[:]), r=[B_sqt], w=[B_rs2])
                        ci = cb_cnt[0] % 2
                        cb_cnt[0] += 1
                        S_.op("dve", lambda e: e.tensor_tensor(out=cbo[ci][:], in0=oga[:, blk], in1=rs2[:], op=ALU.mult),
                              r=[B_rs2, B_oga[qb]], w=[B_cbo[ci]])
                        S_.dma(cT_d[1, :, h, blk], cbo[ci][:], B_cbo[ci], r=[B_cbo[ci]], w=[B_cT[1][qb]])

                    for h in range(4):
                        Wh, B_Wh = WB(h)
                        if h == 0:
                            S_.dma(hTb[0][:], hT_d[:, :, 0:512], B_hTb[0], r=[B_hT[0]], w=[B_hTb[0]])
                        for tb in range(NB):
                            hi = tb % 2
                            if tb + 1 < NB and not (h > 0 and tb == 0 and NB > 1):
                                S_.dma(hTb[1 - hi][:], hT_d[:, :, (tb + 1) * 512:(tb + 2) * 512], B_hTb[1 - hi],
                                       r=[B_hT[tb + 1]], w=[B_hTb[1 - hi]])
                            for t, bk in ((0, 0), (1, 1), (3, 2)):
                                def f_mm(e, bk=bk, t=t, hi=hi, Wh=Wh):
                                    for k in range(8):
                                        e.matmul(banks[bk][:], lhsT=Wh[:, k, t, :], rhs=hTb[hi][:, k, :],
                                                 start=(k == 0), stop=(k == 7))

                                S_.op("pe", f_mm, r=[B_Wh, B_hTb[hi]], w=[bbuf[bk]])

                            def f_v(e, hi=hi, Wh=Wh):
                                for tq in range(4):
                                    for k in range(8):
                                        e.matmul(banks[3][:, tq * 128:(tq + 1) * 128],
                                                 lhsT=hTb[hi][:, k, tq * 128:(tq + 1) * 128], rhs=Wh[:, k, 2, :],
                                                 start=(k == 0), stop=(k == 7))

                            S_.op("pe", f_v, r=[B_Wh, B_hTb[hi]], w=[bbuf[3]])
                            sl = slice(tb * 512, (tb + 1) * 512)
                            S_.op("dve", lambda e, sl=sl: e.tensor_copy(out=qT[:, sl], in_=banks[0][:]), r=[bbuf[0]], w=[B_qkv])
                            S_.op("act", lambda e, sl=sl: e.copy(out=kT[:, sl], in_=banks[1][:]), r=[bbuf[1]], w=[B_qkv])
                            S_.op("act", lambda e, sl=sl: e.activation(out=sgb[:, sl], in_=banks[2][:], func=AF.Silu),
                                  r=[bbuf[2]], w=[B_qkv])
                            S_.op("dve", lambda e, tb=tb: e.tensor_copy(out=vh[:, tb * 4:(tb + 1) * 4, :],
                                                                       in_=banks[3][:].rearrange("p (t n) -> p t n", t=4)),
                                  r=[bbuf[3]], w=[B_qkv])
                            if h > 0:
                                emit_norm(h - 1, tb)
                        if h < 3:
                            S_.dma(hTb[0][:], hT_d[:, :, 0:512], B_hTb[0], r=[B_hT[0]], w=[B_hTb[0]])
                            if NB > 1:
                                S_.dma(hTb[1][:], hT_d[:, :, 512:1024], B_hTb[1], r=[B_hT[1]], w=[B_hTb[1]])
                        if h == 3:
                            S_.dma(WP, wbp[l], B_R1, r=[B_wl[l]], w=[B_R1])
                            S_.dma(WO, wbo[l], B_R2, r=[B_wl[l]], w=[B_R2])
                        steps = [(qb, kt) for qb in range(NB) for kt in range(4 * qb + 4)]

                        def emit_qk(i):
                            qb, kt = steps[i]
                            j = kt - 4 * qb
                            c0 = 128 * j if j > 0 else 0
                            q0 = qb * 512
                            pi = i % 3
                            sb0 = 4 + 2 * (i % 2)

                            def f_qk(e):
                                for u in range(2):
                                    e.matmul(banks[sb0 + u][:, c0:512], lhsT=kT[u * 64:(u + 1) * 64, kt * 128:(kt + 1) * 128],
                                             rhs=qT[u * 64:(u + 1) * 64, q0 + c0:q0 + 512], start=True, stop=True)

                            S_.op("pe", f_qk, r=[B_qkv], w=[bbuf[sb0], bbuf[sb0 + 1]])
                            for u in range(2):
                                S_.op("act", lambda e, u=u: e.activation(out=pt[pi][:, u, c0:512], in_=banks[sb0 + u][:, c0:512],
                                                                       func=AF.Exp, scale=0.125),
                                      r=[bbuf[sb0 + u]], w=[B_pt[pi]])
                            if j >= 0:
                                S_.op("pool", lambda e: (
                                    e.tensor_tensor(out=pt[pi][:, 0, c0:c0 + 128], in0=pt[pi][:, 0, c0:c0 + 128], in1=tri_b[:], op=ALU.mult),
                                    e.tensor_tensor(out=pt[pi][:, 1, c0:c0 + 128], in0=pt[pi][:, 1, c0:c0 + 128], in1=tri_b[:], op=ALU.mult)),
                                    r=[B_pt[pi], B_const], w=[B_pt[pi]])

                        def emit_pv(i, h=h):
                            qb, kt = steps[i]
                            nk = 4 * qb + 4
                            j = kt - 4 * qb
                            c0 = 128 * j if j > 0 else 0
                            pi = i % 3

                            def f_pv(e):
                                for u in range(2):
                                    e.matmul(banks[u][:, c0:512], lhsT=vh[:, kt, :], rhs=pt[pi][:, u, c0:512],
                                             start=(kt == 0), stop=(kt == nk - 1), skip_group_check=True)
                                    e.matmul(banks[2 + u][:, c0:512], lhsT=ones_b[:], rhs=pt[pi][:, u, c0:512],
                                             start=(kt == 0), stop=(kt == nk - 1), skip_group_check=True)

                            S_.op("pe", f_pv, r=[B_pt[pi], B_qkv, B_const], w=[bbuf[0], bbuf[1], bbuf[2], bbuf[3]])
                            if kt == nk - 1:
                                blk = slice(qb * 512, (qb + 1) * 512)
                                S_.op("act", lambda e: [e.copy(out=osb[:, t, :], in_=banks[t][:]) for t in range(2)],
                                      r=[bbuf[0], bbuf[1]], w=[B_osb])
                                S_.op("dve", lambda e: [e.tensor_copy(out=osb[:, t, :], in_=banks[t][:]) for t in (2, 3)],
                                      r=[bbuf[2], bbuf[3]], w=[B_oss])
                                S_.op("dve", lambda e: e.reciprocal(out=rr[:], in_=osb[:, 2:4, :]), r=[B_oss], w=[B_rr])

                                def f_o(e, l=l):
                                    e.tensor_tensor(out=od[:], in0=osb[:, 0, :], in1=rr[:, 0, :], op=ALU.mult)
                                    e.tensor_tensor(out=o2[:], in0=osb[:, 1, :], in1=rr[:, 1, :], op=ALU.mult)
                                    e.scalar_tensor_tensor(out=od[:], in0=o2[:], scalar=nlam[:, l:l + 1], in1=od[:],
                                                           op0=ALU.mult, op1=ALU.add)

                                S_.op("dve", f_o, r=[B_osb, B_rr, B_par], w=[B_od])
                                S_.op("act", lambda e: e.activation(out=sqa[:, blk], in_=od[:], func=AF.Square),
                                      r=[B_od], w=[B_sqa[qb]])
                                S_.op("dve", lambda e, l=l: e.scalar_tensor_tensor(out=oga[:, blk], in0=od[:], scalar=subg[:, l:l + 1],
                                                                                  in1=sgb[:, blk], op0=ALU.mult, op1=ALU.mult),
                                      r=[B_od, B_par, B_qkv], w=[B_oga[qb]])

                        emit_qk(0)
                        for i in range(len(steps)):
                            if i + 1 < len(steps):
                                emit_qk(i + 1)
                            emit_pv(i)
                            if l + 1 < L and i % 12 == 5:
                                bg_tick()
                        if h + 2 < 4:
                            load_WB(l, h + 2)
                    for qb in range(NB):
                        emit_norm(3, qb)
                    if l + 1 < L:
                        bg_flush(finish=(s == NS - 1))
                    S_.barrier()
                if dbg and l == L - 1 and s == NS - 1:
                    S_.dma(dbg_c, cT_d, B_R1, r=B_cT[0] + B_cT[1] + B_cT[2], w=[])
                    S_.barrier()

                with phase() as st:
                    hTb = [sb("hTb%d" % i, [128, 8, 512], BF16, st=st) for i in range(2)]
                    B_hTb = [Buf("hTb%d" % i) for i in range(2)]
                    cTb = [sb("cTb%d" % i, [128, 3, 4, 512], BF16, st=st) for i in range(2)]
                    B_cTb = [Buf("cTb%d" % i) for i in range(2)]
                    WG = [sb("WG%d" % i, [128, 8, 3, 128], BF16, st=st) for i in range(3)]
                    B_WG = [Buf("WG%d" % i) for i in range(3)]
                    sgm = [sb("sgm%d" % i, [128, 512], st=st) for i in range(3)]
                    B_sgm = [Buf("sgm%d" % i) for i in range(3)]
                    ma = [sb("ma%d" % i, [128, 512], st=st) for i in range(2)]
                    B_ma = [Buf("ma%d" % i) for i in range(2)]
                    mb = [sb("mb%d" % i, [128, 512], st=st) for i in range(2)]
                    B_mb = [Buf("mb%d" % i) for i in range(2)]
                    mT = [sb("mT%d" % i, [128, 8, 512], BF16, st=st) for i in range(2)]
                    B_mT = [Buf("mT%d" % i) for i in range(2)]
                    xt = [sb("xt%d" % i, [128, D], st=st) for i in range(2)]
                    B_xt = [Buf("xt%d" % i) for i in range(2)]
                    xo = [sb("xo%d" % i, [128, D], st=st) for i in range(2)]
                    B_xo = [Buf("xo%d" % i) for i in range(2)]
                    junk = sb("junk", [128, D], BF16, st=st)
                    B_junk = Buf("junk")
                    ssq = [sb("ssq%d" % i, [128, 4], st=st) for i in range(2)]
                    B_ssq = [Buf("ssq%d" % i) for i in range(2)]
                    B_xs = Buf("xs_dram")

                    def load_blk(tb):
                        hi = tb % 2
                        S_.dma(hTb[hi][:], hT_d[:, :, tb * 512:(tb + 1) * 512], B_hTb[hi], r=[B_hT[tb]], w=[B_hTb[hi]])
                        for b in range(3):
                            S_.dma(cTb[hi][:, b, :, :], cT_d[b, :, :, tb * 512:(tb + 1) * 512], B_cTb[hi],
                                   r=[B_cT[b][tb]], w=[B_cTb[hi]])

                    cnt = {"wg": 0, "xt": 0}

                    def emit_dcs(tb, dcs):
                        hi = tb % 2
                        for dc in dcs:
                            wi = cnt['wg'] % 3
                            cnt['wg'] += 1
                            S_.dma(WG[wi][:], wbi[l, :, :, OFF_GATES:NCOL].rearrange("p k (b n) -> p k b n", b=3)[:, :, :, dc * 128:(dc + 1) * 128],
                                   B_WG[wi], r=[B_wl[l]], w=[B_WG[wi]])
                            mi = dc % 2
                            for b in range(3):
                                pg, py = 2 * b, 2 * b + 1

                                def f_g(e, pg=pg, b=b, wi=wi, hi=hi):
                                    ins = None
                                    for k in range(8):
                                        ins = e.matmul(banks[pg][:], lhsT=WG[wi][:, k, b, :], rhs=hTb[hi][:, k, :],
                                                       start=(k == 0), stop=(k == 7))
                                    return ins

                                S_.op("pe", f_g, r=[B_WG[wi], B_hTb[hi]], w=[bbuf[pg]])

                                def f_y(e, py=py, b=b, dc=dc, hi=hi):
                                    ins = None
                                    for jj in range(4):
                                        ins = e.matmul(banks[py][:], lhsT=WP[:, b, jj, dc * 128:(dc + 1) * 128],
                                                       rhs=cTb[hi][:, b, jj, :], start=(jj == 0), stop=(jj == 3))
                                    return ins

                                S_.op("pe", f_y, r=[B_R1, B_cTb[hi]], w=[bbuf[py]])
                                S_.op("act", lambda e, b=b, pg=pg: e.activation(out=sgm[b][:], in_=banks[pg][:], func=AF.Sigmoid),
                                      r=[bbuf[pg]], w=[B_sgm[b]])
                                if b == 0:
                                    S_.op("dve", lambda e, mi=mi, py=py: e.tensor_tensor(out=ma[mi][:], in0=banks[py][:],
                                                                                       in1=sgm[0][:], op=ALU.mult),
                                          r=[bbuf[py], B_sgm[0]], w=[B_ma[mi]])
                                else:
                                    S_.op("dve", lambda e, mi=mi, py=py, b=b: e.tensor_tensor(out=mb[mi][:], in0=banks[py][:],
                                                                                            in1=sgm[b][:], op=ALU.mult),
                                          r=[bbuf[py], B_sgm[b]], w=[B_mb[mi]])
                                    if b == 1:
                                        S_.op("pool", lambda e, mi=mi: e.tensor_tensor(out=ma[mi][:], in0=ma[mi][:], in1=mb[mi][:],
                                                                                      op=ALU.add),
                                              r=[B_mb[mi], B_ma[mi]], w=[B_ma[mi]])
                                    else:
                                        S_.op("pool", lambda e, mi=mi, dc=dc: e.tensor_tensor(out=mT[tb % 2][:, dc, :], in0=ma[mi][:],
                                                                                             in1=mb[mi][:], op=ALU.add),
                                              r=[B_mb[mi], B_ma[mi]], w=[B_mT[tb % 2]])

                    def emit_wo(tb):
                        for tq in range(4):
                            xi = cnt['xt'] % 2
                            cnt['xt'] += 1
                            r0 = t0 + tb * 512 + tq * 128
                            S_.dma(xt[xi][:], xsrc[r0:r0 + 128, :], B_xt[xi], r=[B_xs], w=[B_xt[xi]])
                            for hf in range(2):
                                bk = 6 + hf

                                def f_o(e, bk=bk, hf=hf, tq=tq):
                                    ins = None
                                    for k in range(8):
                                        ins = e.matmul(banks[bk][:], lhsT=mT[tb % 2][:, k, tq * 128:(tq + 1) * 128],
                                                       rhs=WO[:, k, hf * 512:(hf + 1) * 512], start=(k == 0), stop=(k == 7))
                                    return ins

                                S_.op("pe", f_o, r=[B_mT[tb % 2], B_R2], w=[bbuf[bk]])
                                S_.op("dve", lambda e, bk=bk, hf=hf, xi=xi: e.tensor_tensor(
                                    out=xo[xi][:, hf * 512:(hf + 1) * 512], in0=banks[bk][:], in1=xt[xi][:, hf * 512:(hf + 1) * 512],
                                    op=ALU.add), r=[bbuf[bk], B_xt[xi]], w=[B_xo[xi]])
                            if not last:
                                S_.dma(xs[r0:r0 + 128, :], xo[xi][:], B_xo[xi], r=[B_xo[xi]], w=[B_xs])
                            elif not do_final:
                                S_.dma(y_out[r0:r0 + 128, :], xo[xi][:], B_xo[xi], r=[B_xo[xi]], w=[B_xs])
                            else:
                                si = xi
                                S_.op("act", lambda e, xi=xi, si=si: e.activation(out=junk[:], in_=xo[xi][:], func=AF.Square,
                                                                                accum_out=ssq[si][:, 0:1]),
                                      r=[B_xo[xi]], w=[B_junk, B_ssq[si]])

                                S_.op("dve", lambda e, si=si: e.tensor_scalar(out=ssq[si][:, 1:2], in0=ssq[si][:, 0:1], scalar1=1.0 / D,
                                                                              scalar2=EPS, op0=ALU.mult, op1=ALU.add),
                                      r=[B_ssq[si]], w=[B_ssq[si]])
                                S_.op("act", lambda e, si=si: e.activation(out=ssq[si][:, 2:3], in_=ssq[si][:, 1:2], func=AF.Sqrt),
                                      r=[B_ssq[si]], w=[B_ssq[si]])
                                S_.op("dve", lambda e, si=si: e.reciprocal(out=ssq[si][:, 3:4], in_=ssq[si][:, 2:3]),
                                      r=[B_ssq[si]], w=[B_ssq[si]])

                                S_.op("act", lambda e, si=si, xi=xi: e.activation(out=xo[xi][:], in_=xo[xi][:], func=AF.Copy,
                                                                                scale=ssq[si][:, 3:4]),
                                      r=[B_ssq[si], B_xo[xi]], w=[B_xo[xi]])
                                S_.op("dve", lambda e, xi=xi: e.tensor_tensor(out=xo[xi][:], in0=xo[xi][:], in1=fgb[:], op=ALU.mult),
                                      r=[B_xo[xi], B_par], w=[B_xo[xi]])
                                S_.dma(y_out[r0:r0 + 128, :], xo[xi][:], B_xo[xi], r=[B_xo[xi]], w=[B_xs])

                    load_blk(0)
                    for tb in range(NB):
                        if tb + 1 < NB:
                            load_blk(tb + 1)
                        emit_dcs(tb, range(0, 2))
                        if tb > 0:
                            emit_wo(tb - 1)
                        emit_dcs(tb, range(2, 8))
                    emit_wo(NB - 1)
                    S_.barrier()
        S_.barrier()
        S_.replay()
    return nc


_NC_CACHE = {}
FUSED_LAYERS = 4


def _get(L, NS, S, apply_final, l0):
    key = (L, NS, S, apply_final, l0)
    if key not in _NC_CACHE:
        _NC_CACHE[key] = build(L, NS, S, apply_final=apply_final, l0=l0)
    return _NC_CACHE[key]


def kernel(**inputs):
    DEPTH, NS, S = 4, 2, 4096
    n = 8
    x = np.ascontiguousarray(np.asarray(inputs["x"], dtype=np.float32)).reshape(n, NS * S, D)
    full = {k: np.ascontiguousarray(np.asarray(v, dtype=np.float32)) for k, v in inputs.items() if k != "x"}
    G = FUSED_LAYERS
    for l0 in range(0, DEPTH, G):
        fin = (l0 + G == DEPTH)
        nc = _get(G, NS, S, fin, l0)
        shared = {k: (v if k == "final_norm_g" else np.ascontiguousarray(v[l0:l0 + G])) for k, v in full.items()}
        in_maps = []
        for c in range(n):
            m = dict(shared)
            m["x"] = np.ascontiguousarray(x[c])
            in_maps.append(m)
        res = run_bass_kernel_spmd(nc, in_maps, core_ids=list(range(n)))
        x = np.stack([np.asarray(r["y"]) for r in res.results], axis=0)
    return x.reshape(16, S, D).astype(np.float32)
```
